# Optimizing a Trainium2 kernel written in Bass

```python
import math
import jax
import jax.numpy as jnp
from jax import lax
import numpy as np

D_MODEL = 1024
BATCH = 4
SEQ = 8192
DEPTH = 4

CHUNK = 64
EPS = 1e-6
NEG_BIG = -1e30
HG_HEADS = 8
HG_DK = 128
HG_DV = 128
HG_QK = HG_HEADS * HG_DK
HG_W = HG_HEADS * HG_DV
ML_HEADS = 4
ML_DQK = 128
ML_DV = 256
ML_CONV = 4
ML_QK = ML_HEADS * ML_DQK
ML_W = ML_HEADS * ML_DV
MB_HEADS = 16
MB_P = 64
MB_GROUPS = 4
MB_N = 128
MB_CONV = 4
MB_W = MB_HEADS * MB_P
MB_CONV_DIM = MB_W + 2 * MB_GROUPS * MB_N
D_FF = 2816
FFN_CONV = 3
N_BRANCH = 3
IN_SPLITS = (HG_QK, HG_QK, HG_W, HG_W,
             2 * ML_QK, ML_W, 2 * ML_HEADS, ML_W,
             MB_W, MB_CONV_DIM, MB_HEADS,
             N_BRANCH * D_MODEL)
N_IN = sum(IN_SPLITS)

kernel_name = "hybrid_hgrn2_mlstm_ssd_convffn_trunk"


def rmsnorm(x, g):
    xf = x.astype(jnp.float32)
    y = xf * lax.rsqrt(jnp.mean(xf * xf, axis=-1, keepdims=True) + EPS)
    return (y * g.astype(jnp.float32)).astype(x.dtype)


def causal_dwconv(u, w, b):
    K = w.shape[0]
    S = u.shape[1]
    up = jnp.pad(u, ((0, 0), (K - 1, 0), (0, 0)))
    return sum(up[:, k:k + S] * w[k] for k in range(K)) + b


def to_chunks(t):
    B, S = t.shape[:2]
    t = t.astype(jnp.float32).reshape(B, S // CHUNK, CHUNK, *t.shape[2:])
    return jnp.moveaxis(t, 1, 0)


def from_chunks(t):
    t = jnp.moveaxis(t, 0, 1)
    return t.reshape(t.shape[0], -1, *t.shape[3:])


def causal_tri():
    return jnp.tril(jnp.ones((CHUNK, CHUNK), dtype=bool))


def hgrn2_scan(q, log_f, v):
    B, _, H, dk = q.shape
    dv = v.shape[-1]
    k = -jnp.expm1(log_f)
    tri = causal_tri()

    def step(state, inp):
        qc, lfc, kc, vc = inp
        b = jnp.cumsum(lfc, axis=1)
        diff = b[:, :, None] - b[:, None, :]
        decay = jnp.exp(jnp.where(tri[None, :, :, None, None], diff, NEG_BIG))
        attn = jnp.einsum('bthd,btshd->bhts', qc, decay * kc[:, None])
        o_intra = jnp.einsum('bhts,bshv->bthv', attn, vc)
        o_inter = jnp.einsum('bthd,bhdv->bthv', qc * jnp.exp(b), state)
        b_last = b[:, -1]
        k_dec = kc * jnp.exp(b_last[:, None] - b)
        state = jnp.exp(b_last)[..., None] * state + jnp.einsum('bshd,bshv->bhdv', k_dec, vc)
        return state, o_intra + o_inter

    s0 = jnp.zeros((B, H, dk, dv), jnp.float32)
    _, o = lax.scan(step, s0, (to_chunks(q), to_chunks(log_f), to_chunks(k), to_chunks(v)))
    return from_chunks(o)


def mlstm_scan(q, k, v, i_pre, log_f):
    B, _, H, dk = q.shape
    dv = v.shape[-1]
    tri = causal_tri()

    def step(carry, inp):
        C, n, m = carry
        qc, kc, vc, ic, fc = inp
        bh = jnp.moveaxis(jnp.cumsum(fc, axis=1), 1, 2)
        ih = jnp.moveaxis(ic, 1, 2)
        logD = jnp.where(tri, bh[..., :, None] - bh[..., None, :] + ih[..., None, :], NEG_BIG)
        log_inter = bh + m[..., None]
        m_t = jnp.maximum(jnp.max(logD, axis=-1), log_inter)
        s = jnp.einsum('bthd,bshd->bhts', qc, kc) * jnp.exp(logD - m_t[..., None])
        inter_w = jnp.exp(log_inter - m_t)
        inter_wt = jnp.swapaxes(inter_w, 1, 2)[..., None]
        num = jnp.einsum('bhts,bshv->bthv', s, vc) + jnp.einsum('bthd,bhdv->bthv', qc, C) * inter_wt
        den = jnp.sum(s, axis=-1) + jnp.einsum('bthd,bhd->bht', qc, n) * inter_w
        denom = jnp.maximum(jnp.abs(den), jnp.exp(-m_t))
        h = num / jnp.swapaxes(denom, 1, 2)[..., None]
        b_last = bh[..., -1]
        log_w = b_last[..., None] - bh + ih
        m_new = jnp.maximum(b_last + m, jnp.max(log_w, axis=-1))
        w = jnp.swapaxes(jnp.exp(log_w - m_new[..., None]), 1, 2)[..., None]
        dstate = jnp.exp(b_last + m - m_new)
        C = dstate[..., None, None] * C + jnp.einsum('bshd,bshv->bhdv', kc * w, vc)
        n = dstate[..., None] * n + jnp.sum(kc * w, axis=1)
        return (C, n, m_new), h

    c0 = (jnp.zeros((B, H, dk, dv), jnp.float32), jnp.zeros((B, H, dk), jnp.float32),
          jnp.zeros((B, H), jnp.float32))
    _, h = lax.scan(step, c0, (to_chunks(q), to_chunks(k), to_chunks(v), to_chunks(i_pre), to_chunks(log_f)))
    return from_chunks(h)


def ssd_scan(x, dt, A, Bm, Cm):
    B, _, H, P = x.shape
    N = Bm.shape[-1]
    hpg = H // Bm.shape[2]
    tri = causal_tri()
    A = A.astype(jnp.float32)

    def step(state, inp):
        xc, dtc, Bc, Cc = inp
        cum = jnp.moveaxis(jnp.cumsum(dtc * A, axis=1), 1, 2)
        L = jnp.exp(jnp.where(tri, cum[..., :, None] - cum[..., None, :], NEG_BIG))
        CB = jnp.repeat(jnp.einsum('btgn,bsgn->bgts', Cc, Bc), hpg, axis=1)
        xdt = xc * dtc[..., None]
        y_intra = jnp.einsum('bhts,bshp->bthp', CB * L, xdt)
        Ch = jnp.repeat(Cc, hpg, axis=2)
        y_inter = jnp.einsum('bthn,bhpn->bthp', Ch, state) * jnp.swapaxes(jnp.exp(cum), 1, 2)[..., None]
        dec = jnp.swapaxes(jnp.exp(cum[..., -1:] - cum), 1, 2)[..., None]
        Bh = jnp.repeat(Bc, hpg, axis=2)
        state = jnp.exp(cum[..., -1])[..., None, None] * state + jnp.einsum('bshn,bshp->bhpn', Bh * dec, xdt)
        return state, y_intra + y_inter

    s0 = jnp.zeros((B, H, P, N), jnp.float32)
    _, y = lax.scan(step, s0, (to_chunks(x), to_chunks(dt), to_chunks(Bm), to_chunks(Cm)))
    return from_chunks(y)


def setup_inputs(seed: int = 0) -> dict:
    key = jax.random.key(seed)
    ks = jax.random.split(key, 32)
    f32 = jnp.float32

    def nrm(k, shape, scale):
        return jax.random.normal(k, shape, f32) * scale

    def gain(k, shape):
        return 1.0 + 0.05 * jax.random.normal(k, shape, f32)

    dt0 = jnp.exp(jax.random.uniform(ks[12], (DEPTH, MB_HEADS), f32, math.log(1e-3), math.log(1e-1)))
    ml_f_bias = jnp.linspace(3.0, 6.0, ML_HEADS, dtype=f32)[None] + nrm(ks[7], (DEPTH, ML_HEADS), 0.1)
    ml_i_bias = nrm(ks[8], (DEPTH, ML_HEADS), 0.1)
    return {
        "x": jax.random.normal(ks[0], (BATCH, SEQ, D_MODEL), f32),
        "norm1_g": gain(ks[1], (DEPTH, D_MODEL)),
        "w_in": nrm(ks[2], (DEPTH, D_MODEL, N_IN), D_MODEL ** -0.5),
        "hg_lb_logits": nrm(ks[3], (DEPTH, HG_QK), 0.1),
        "hg_norm_g": gain(ks[4], (DEPTH, HG_DV)),
        "ml_conv_w": nrm(ks[5], (DEPTH, ML_CONV, 2 * ML_QK), ML_CONV ** -0.5),
        "ml_conv_b": nrm(ks[6], (DEPTH, 2 * ML_QK), 0.02),
        "ml_gate_b": jnp.concatenate([ml_i_bias, ml_f_bias], axis=-1),
        "ml_norm_g": gain(ks[9], (DEPTH, ML_DV)),
        "mb_conv_w": nrm(ks[10], (DEPTH, MB_CONV, MB_CONV_DIM), MB_CONV ** -0.5),
        "mb_conv_b": nrm(ks[11], (DEPTH, MB_CONV_DIM), 0.02),
        "mb_dt_bias": dt0 + jnp.log(-jnp.expm1(-dt0)),
        "mb_a_log": jnp.log(jax.random.uniform(ks[13], (DEPTH, MB_HEADS), f32, 1.0, 16.0)),
        "mb_d": gain(ks[14], (DEPTH, MB_HEADS)),
        "mb_norm_g": gain(ks[15], (DEPTH, MB_W)),
        "w_br_hg": nrm(ks[16], (DEPTH, HG_W, D_MODEL), HG_W ** -0.5),
        "w_br_ml": nrm(ks[17], (DEPTH, ML_W, D_MODEL), ML_W ** -0.5),
        "w_br_mb": nrm(ks[18], (DEPTH, MB_W, D_MODEL), MB_W ** -0.5),
        "w_out": nrm(ks[19], (DEPTH, D_MODEL, D_MODEL), D_MODEL ** -0.5),
        "norm2_g": gain(ks[20], (DEPTH, D_MODEL)),
        "w_up": nrm(ks[21], (DEPTH, D_MODEL, 2 * D_FF), D_MODEL ** -0.5),
        "ffn_conv_w": nrm(ks[22], (DEPTH, FFN_CONV, 2 * D_FF), FFN_CONV ** -0.5),
        "ffn_conv_b": nrm(ks[23], (DEPTH, 2 * D_FF), 0.02),
        "w_down": nrm(ks[24], (DEPTH, D_FF, D_MODEL), D_FF ** -0.5),
        "final_g": gain(ks[25], (D_MODEL,)),
    }


def reference(x, norm1_g, w_in, hg_lb_logits, hg_norm_g, ml_conv_w, ml_conv_b, ml_gate_b, ml_norm_g,
              mb_conv_w, mb_conv_b, mb_dt_bias, mb_a_log, mb_d, mb_norm_g, w_br_hg, w_br_ml, w_br_mb,
              w_out, norm2_g, w_up, ffn_conv_w, ffn_conv_b, w_down, final_g):
    B, S, _ = x.shape
    f32 = jnp.float32
    lb_p = jax.nn.softmax(hg_lb_logits.astype(f32), axis=0)
    lbs = jnp.cumsum(lb_p, axis=0) - lb_p[0]
    split_at = [int(c) for c in np.cumsum(IN_SPLITS)[:-1]]

    for l in range(DEPTH):
        h = rmsnorm(x, norm1_g[l])
        (hg_q, hg_f, hg_i, hg_g, ml_qk, ml_v, ml_if, ml_o,
         mb_z, mb_xbc, mb_dt, gate_pre) = jnp.split(h @ w_in[l], split_at, axis=-1)

        lb = lbs[l]
        log_f = jnp.log(lb + (1.0 - lb) * jax.nn.sigmoid(hg_f.astype(f32)))
        o_hg = hgrn2_scan(jax.nn.silu(hg_q).reshape(B, S, HG_HEADS, HG_DK),
                          log_f.reshape(B, S, HG_HEADS, HG_DK),
                          hg_i.reshape(B, S, HG_HEADS, HG_DV))
        y_hg = (rmsnorm(o_hg, hg_norm_g[l]).reshape(B, S, HG_W)
                * jax.nn.silu(hg_g.astype(f32))).astype(x.dtype)

        qk = jax.nn.silu(causal_dwconv(ml_qk, ml_conv_w[l], ml_conv_b[l]))
        q_m, k_m = jnp.split(qk, 2, axis=-1)
        gb = ml_gate_b[l]
        i_pre = ml_if[..., :ML_HEADS] + gb[:ML_HEADS]
        log_f_m = jax.nn.log_sigmoid((ml_if[..., ML_HEADS:] + gb[ML_HEADS:]).astype(f32))
        h_ml = mlstm_scan(q_m.reshape(B, S, ML_HEADS, ML_DQK),
                          k_m.reshape(B, S, ML_HEADS, ML_DQK) * (ML_DQK ** -0.5),
                          ml_v.reshape(B, S, ML_HEADS, ML_DV), i_pre, log_f_m)
        y_ml = (rmsnorm(h_ml, ml_norm_g[l]).reshape(B, S, ML_W)
                * jax.nn.sigmoid(ml_o.astype(f32))).astype(x.dtype)

        xbc = jax.nn.silu(causal_dwconv(mb_xbc, mb_conv_w[l], mb_conv_b[l]))
        xs, Bm, Cm = jnp.split(xbc, [MB_W, MB_W + MB_GROUPS * MB_N], axis=-1)
        xs = xs.reshape(B, S, MB_HEADS, MB_P)
        dt = jax.nn.softplus((mb_dt + mb_dt_bias[l]).astype(f32))
        A = -jnp.exp(mb_a_log[l].astype(f32))
        y_ssd = ssd_scan(xs, dt, A, Bm.reshape(B, S, MB_GROUPS, MB_N), Cm.reshape(B, S, MB_GROUPS, MB_N))
        y_ssd = y_ssd + mb_d[l].astype(f32)[:, None] * xs
        yz = (y_ssd.reshape(B, S, MB_W) * jax.nn.silu(mb_z.astype(f32))).reshape(B, S, MB_GROUPS, MB_W // MB_GROUPS)
        y_mb = rmsnorm(yz, mb_norm_g[l].reshape(MB_GROUPS, MB_W // MB_GROUPS)).reshape(B, S, MB_W).astype(x.dtype)

        gates = jax.nn.sigmoid(gate_pre).reshape(B, S, N_BRANCH, D_MODEL)
        mixed = (gates[:, :, 0] * (y_hg @ w_br_hg[l])
                 + gates[:, :, 1] * (y_ml @ w_br_ml[l])
                 + gates[:, :, 2] * (y_mb @ w_br_mb[l]))
        x = x + mixed @ w_out[l]

        u = causal_dwconv(rmsnorm(x, norm2_g[l]) @ w_up[l], ffn_conv_w[l], ffn_conv_b[l])
        u_g, u_v = jnp.split(u, 2, axis=-1)
        x = x + (jax.nn.silu(u_g) * u_v) @ w_down[l]

    return rmsnorm(x, final_g)
```

```python
import contextlib
import numpy as np
import concourse.bass as bass
import concourse.mybir as mybir
from concourse.bass_utils import run_bass_kernel_spmd

F32 = mybir.dt.float32
BF16 = mybir.dt.bfloat16
AF = mybir.ActivationFunctionType
ALU = mybir.AluOpType

L_ALL = 4
D = 1024
NIN = 13336
DFF = 2816
T = 512
EPS = 1e-6
O_HGQ, O_HGF, O_HGI, O_HGG = 0, 1024, 2048, 3072
O_MLQ, O_MLK, O_MLV, O_MLIF, O_MLO = 4096, 4608, 5120, 6144, 6152
O_MBZ, O_MBX, O_MBDT, O_GATE = 7176, 8200, 10248, 10264
R_N1, R_N2, R_LB, R_MLW, R_MLB, R_MBW, R_MBB, R_FFW, R_FFB, R_MBG, R_HGG, R_MLG, R_MBD, R_FIN = (
    0, 8, 16, 24, 56, 64, 128, 144, 276, 320, 328, 329, 331, 339)
NROW = 384
C_ID, C_M2, C_SL, C_GS, C_SEL, C_ONE, NCONST = 0, 128, 256, 320, 832, 1344, 1472


class Buf:
    __slots__ = ("name", "w", "r")

    def __init__(self, name=""):
        self.name = name
        self.w = None
        self.r = []


class Prog:
    ENGS = ("pe", "act", "dve", "pool", "sp")

    def __init__(self, nc, n_dma_sems=8, same_engine_sync=True):
        self.nc = nc
        self.ops = {e: [] for e in self.ENGS}
        self.cnt = {e: 0 for e in self.ENGS}
        self.waited = {e: {} for e in self.ENGS}
        self.same_engine_sync = same_engine_sync
        self.n_dma_sems = n_dma_sems
        self.dma_rr = {e: 0 for e in self.ENGS}
        self.dma_cnt = {}
        self.semh = {}
        self.sem_names = [("c", e) for e in self.ENGS]
        for e in ("sp", "act", "pool"):
            for j in range(n_dma_sems):
                self.sem_names.append(("d", e, j))
                self.dma_cnt[("d", e, j)] = 0
        self.ninst = 0

    def _need(self, eng, tok):
        if tok is None:
            return
        semkey, val, weng = tok
        if weng == eng and semkey[0] == "c":
            if eng == "pe" or not self.same_engine_sync:
                return
        if self.waited[eng].get(semkey, 0) >= val:
            return
        self.waited[eng][semkey] = val
        self.ops[eng].append(lambda h: h.wait_ge(self.semh[semkey], val))

    def _deps(self, eng, reads, writes):
        for b in reads:
            self._need(eng, b.w)
        for b in writes:
            self._need(eng, b.w)
            for t in b.r:
                self._need(eng, t)

    def _mark(self, tok, reads, writes):
        for b in reads:
            b.r.append(tok)
        for b in writes:
            b.w = tok
            b.r = []

    def op(self, eng, fn, reads=(), writes=()):
        self._deps(eng, reads, writes)
        self.cnt[eng] += 1
        self.ninst += 1
        semkey = ("c", eng)
        self.ops[eng].append(lambda h: fn(h).then_inc(self.semh[semkey], 1))
        self._mark((semkey, self.cnt[eng], eng), reads, writes)

    def dma(self, eng, out, in_, reads=(), writes=(), **kw):
        j = self.dma_rr[eng]
        self.dma_rr[eng] = (j + 1) % self.n_dma_sems
        semkey = ("d", eng, j)
        m = self.dma_cnt[semkey]
        if m > 0:
            self._need(eng, (semkey, 16 * m, "dma"))
        self._deps(eng, reads, writes)
        self.dma_cnt[semkey] = m + 1
        self.ninst += 1
        self.ops[eng].append(lambda h: h.dma_start(out=out, in_=in_, **kw).then_inc(self.semh[semkey], 16))
        tok = (semkey, 16 * (m + 1), "dma")
        self._mark(tok, reads, writes)
        return tok

    def barrier(self):
        for e in ("pe", "act", "dve", "pool"):
            for o in ("pe", "act", "dve", "pool"):
                if o != e and self.cnt[o] > 0:
                    semkey = ("c", o)
                    val = self.cnt[o]
                    if self.waited[e].get(semkey, 0) < val:
                        self.waited[e][semkey] = val
                        self.ops[e].append(lambda h, semkey=semkey, val=val: h.wait_ge(self.semh[semkey], val))

    def wait_all(self, eng):
        for o in ("pe", "act", "dve", "pool"):
            if o != eng and self.cnt[o] > 0:
                semkey = ("c", o)
                val = self.cnt[o]
                if self.waited[eng].get(semkey, 0) < val:
                    self.waited[eng][semkey] = val
                    self.ops[eng].append(lambda h, semkey=semkey, val=val: h.wait_ge(self.semh[semkey], val))

    def finish(self, eng, bufs):
        for b in bufs:
            self._need(eng, b.w)

    def build(self, st):
        nc = self.nc
        for k in self.sem_names:
            self.semh[k] = st.enter_context(nc.semaphore("s_" + "_".join(str(x) for x in k)))
        blk = st.enter_context(nc.Block())

        def mk(e):
            def f(h):
                for o in self.ops[e]:
                    o(h)
            return f
        blk.tensor(mk("pe"))
        blk.scalar(mk("act"))
        blk.vector(mk("dve"))
        blk.gpsimd(mk("pool"))
        blk.sync(mk("sp"))


def build_program(NTOK, NL):
    assert NTOK % T == 0
    NTILE = NTOK // T
    nc = bass.Bass("TRN2", target_bir_lowering=False)
    dr = {}

    def dram(name, shape, kind="ExternalInput", dt=F32):
        dr[name] = nc.dram_tensor(name, list(shape), dt, kind=kind).ap()
        return dr[name]
    x_in = dram("x", [NTOK, D])
    w_in = dram("w_in", [L_ALL, D, NIN])
    w_br = [dram("w_br_hg", [L_ALL, D, D]), dram("w_br_ml", [L_ALL, D, D]), dram("w_br_mb", [L_ALL, D, D])]
    w_out = dram("w_out", [L_ALL, D, D])
    w_up = dram("w_up", [L_ALL, D, 2 * DFF])
    w_down = dram("w_down", [L_ALL, DFF, D])
    pvec = dram("pvec", [L_ALL * 3 * 128, 128])
    bvec = dram("bvec", [1, L_ALL * 32])
    gbv = dram("gbv", [4, L_ALL * 2])
    consts = dram("consts", [128, NCONST])
    out_d = dram("out", [NTOK, D], kind="ExternalOutput")
    xscr = dram("xscr", [8, 128, NTOK], kind="Internal")

    P = Prog(nc)
    st = contextlib.ExitStack()

    def sb(name, shape, dt=F32):
        return st.enter_context(nc.sbuf_tensor(name, list(shape), dt))

    cst = sb("cst", [128, NCONST])
    identb = sb("identb", [128, 128], BF16)
    ones_bf = sb("ones_bf", [128, 128], BF16)
    ptab = sb("ptab", [128, L_ALL, NROW])
    bvt = sb("bvt", [128, L_ALL, 32])
    aneg = sb("aneg", [128, L_ALL, 16])
    gbt = sb("gbt", [128, L_ALL, 2])
    lbt = sb("lbt", [128, L_ALL, 8])
    omlt = sb("omlt", [128, L_ALL, 8])
    nomlt = sb("nomlt", [128, L_ALL, 8])
    xf = sb("xf", [128, 8, T])
    mixed = sb("mixed", [128, 8, T])
    hT = sb("hT", [128, 8, T], BF16)
    NWB = 5
    wring = sb("wring", [128, NWB, 8, 512], BF16)
    S_hg = sb("S_hg", [128, 8, 128])
    Sb_hg = [sb(f"Sb_hg{i}", [128, 8, 128], BF16) for i in range(2)]
    C_ml = sb("C_ml", [128, 4, 256])
    Cb_ml = [sb(f"Cb_ml{i}", [128, 4, 256], BF16) for i in range(2)]
    n_ml = sb("n_ml", [128, 4])
    nrep = [sb(f"nrep{i}", [128, 4, 128], BF16) for i in range(2)]
    St_mb = sb("St_mb", [128, 16, 64])
    Stb_mb = sb("Stb_mb", [128, 16, 64], BF16)
    tail_ml = sb("tail_ml", [128, 8, 3])
    tail_mb = sb("tail_mb", [128, 16, 3])
    tail_ff = sb("tail_ff", [128, 44, 2])
    AW = 17408
    arena = sb("arena", [128, AW])
    psum_t = [st.enter_context(nc.psum_tensor(f"ps{i}", [128, 1024], F32)) for i in range(4)]

    ident = cst[:, C_ID:C_ID + 128]
    mask2 = cst[:, C_M2:C_M2 + 128]
    maskc = cst[0:64, C_M2:C_M2 + 64]
    SLm = cst[0:64, C_SL:C_SL + 64]
    gseg = cst[:, C_GS:C_GS + 512]
    sel = cst[0:4, C_SEL:C_SEL + 512]
    ones_f = cst[:, C_ONE:C_ONE + 128]

    B_cst, B_par = Buf("cst"), Buf("par")
    B_xf, B_mixed, B_hT = Buf("xf"), Buf("mixed"), Buf("hT")
    WB = [Buf(f"wb{i}") for i in range(NWB)]
    PB = [Buf(f"pb{i}") for i in range(8)]
    B_S, B_Sb = Buf("S"), [Buf("Sb0"), Buf("Sb1")]
    B_C, B_Cb, B_n, B_nrep = Buf("C"), [Buf("Cb0"), Buf("Cb1")], Buf("n"), [Buf("nr0"), Buf("nr1")]
    B_St, B_Stb = Buf("St"), Buf("Stb")
    B_tml, B_tmb, B_tff = Buf("tml"), Buf("tmb"), Buf("tff")
    B_xs = [Buf(f"xs{j}") for j in range(NTILE)]
    B_out = [Buf(f"out{j}") for j in range(NTILE)]

    class Arena:
        def __init__(self):
            self.off = 0

        def reset(self):
            self.off = 0

        def alloc(self, shape, dt=F32, parts=128):
            n = int(np.prod(shape[1:]))
            esz = 4 if dt == F32 else 2
            nbytes = (n * esz + 63) // 64 * 64
            o = self.off
            self.off += nbytes
            assert self.off <= AW * 4, ("arena overflow", self.off)
            if dt == F32:
                v = arena[0:shape[0], o // 4:o // 4 + n]
            else:
                v = arena[0:shape[0], o // 4:o // 4 + (n * 2 + 3) // 4].bitcast(BF16)[:, 0:n]
            if len(shape) == 3:
                v = v.rearrange("p (a b) -> p a b", a=shape[1])
            elif len(shape) == 4:
                v = v.rearrange("p (a b c) -> p a b c", a=shape[1], b=shape[2])
            return v
    A = Arena()

    class PSA:
        def __init__(self):
            self.i = 0
            self.excl = set()

        def get(self, n=1, hold=False):
            for _ in range(32):
                if n == 2 and self.i % 2:
                    self.i += 1
                b = self.i % 8
                if all(((b + k) % 8) not in self.excl for k in range(n)) and b + n <= 8:
                    self.i += n
                    if hold:
                        for k in range(n):
                            self.excl.add(b + k)
                    return b
                self.i += 1
            raise RuntimeError("psum exhausted")

        def release(self, b, n=1):
            for k in range(n):
                self.excl.discard(b + k)
    PS = PSA()

    def pb(b):
        return psum_t[b // 2][:, (b % 2) * 512:(b % 2) * 512 + 512]

    def pbb(b):
        return psum_t[b // 2].bitcast(BF16)[:, (b % 2) * 1024:(b % 2) * 1024 + 1024]

    def pd(b):
        assert b % 2 == 0
        return psum_t[b // 2][:, 0:1024]

    def mm(out, lhsT, rhs, start, stop, R, W):
        P.op("pe", lambda h: h.matmul(out=out, lhsT=lhsT, rhs=rhs, start=start, stop=stop), R, W)

    def tr(out, in_, idn, R, W):
        P.op("pe", lambda h: h.transpose(out=out, in_=in_, identity=idn), R, W)

    def act(out, in_, func, R, W, scale=None, bias=None):
        kw = {}
        if scale is not None:
            kw["scale"] = scale
        if bias is not None:
            kw["bias"] = bias
        P.op("act", lambda h: h.activation(out=out, in_=in_, func=func, **kw), R, W)

    def acopy(out, in_, R, W):
        P.op("act", lambda h: h.copy(out=out, in_=in_), R, W)

    def tt(out, a, b, op, R, W, eng="dve"):
        P.op(eng, lambda h: h.tensor_tensor(out=out, in0=a, in1=b, op=op), R, W)

    def ts(out, a, s1, s2, op0, op1, R, W, eng="dve"):
        if s2 is None:
            P.op(eng, lambda h: h.tensor_scalar(out=out, in0=a, scalar1=s1, scalar2=None, op0=op0), R, W)
        else:
            P.op(eng, lambda h: h.tensor_scalar(out=out, in0=a, scalar1=s1, scalar2=s2, op0=op0, op1=op1), R, W)

    def stt(out, a, s, b, op0, op1, R, W, eng="dve"):
        P.op(eng, lambda h: h.scalar_tensor_tensor(out=out, in0=a, scalar=s, in1=b, op0=op0, op1=op1), R, W)

    def vcopy(out, in_, R, W, eng="dve"):
        P.op(eng, lambda h: h.tensor_copy(out=out, in_=in_), R, W)

    def recip(out, in_, R, W):
        P.op("dve", lambda h: h.reciprocal(out=out, in_=in_), R, W)

    def scan(out, d0, d1, R, W):
        P.op("dve", lambda h: h.tensor_tensor_scan(out=out, data0=d0, data1=d1, initial=0.0,
                                                    op0=ALU.mult, op1=ALU.add), R, W)

    def memset(ap, val, W, eng="dve"):
        P.op(eng, lambda h: h.memset(ap, val), (), W)

    def rstd_from_ss(rs, ss_psum, n, R, W):
        ts(rs, ss_psum, 1.0 / n, EPS, ALU.mult, ALU.add, R, W)
        act(rs, rs, AF.Sqrt, W, W)
        recip(rs, rs, W, W)

    wslot = [0]

    def loadw(src):
        s = wslot[0]
        wslot[0] = (s + 1) % NWB
        nk, ncol = src.shape[1], src.shape[2]
        dst = wring[:, s, 0:nk, 0:ncol]
        P.dma("pool", dst, src, reads=(), writes=[WB[s]])
        return dst, WB[s]

    def wsrc(wt, l, c0, n, k0=0, nk=None):
        v = wt[l].rearrange("(k p) n -> p k n", p=128)
        if nk is None:
            nk = v.shape[1] - k0
        return v[:, k0:k0 + nk, c0:c0 + n]

    P.dma("sp", cst[:], consts, writes=[B_cst])
    vcopy(identb[:], ident, [B_cst], [B_cst])
    vcopy(ones_bf[:], ones_f, [B_cst], [B_cst])
    A.reset()
    praw = A.alloc([128, L_ALL * 3, 128])
    B_praw = Buf("praw")
    P.dma("sp", praw, pvec.rearrange("(n p) c -> p n c", p=128), writes=[B_praw])
    for l in range(L_ALL):
        for i in range(3):
            b = PS.get()
            tr(pb(b)[:, 0:128], praw[:, l * 3 + i, :], ident, [B_praw, B_cst], [PB[b]])
            acopy(ptab[:, l, i * 128:(i + 1) * 128], pb(b)[:, 0:128], [PB[b]], [B_par])
    P.dma("sp", bvt[:].rearrange("p l c -> p (l c)"), bvec.partition_broadcast(128).rearrange("p o c -> p (o c)"),
          writes=[B_par])
    P.dma("sp", gbt[0:4].rearrange("p l c -> p (l c)"), gbv, writes=[B_par])
    act(aneg[:], bvt[:, :, 16:32], AF.Exp, [B_par], [B_par])
    ts(aneg[:], aneg[:], -1.0, None, ALU.mult, None, [B_par], [B_par])
    lbe = A.alloc([128, L_ALL, 8])
    lsum = A.alloc([128, 8])
    B_lb = Buf("lb")
    for l in range(L_ALL):
        act(lbe[:, l, :], ptab[:, l, R_LB:R_LB + 8], AF.Exp, [B_par], [B_lb])
    tt(lsum, lbe[:, 0, :], lbe[:, 1, :], ALU.add, [B_lb], [B_lb])
    tt(lsum, lsum, lbe[:, 2, :], ALU.add, [B_lb], [B_lb])
    tt(lsum, lsum, lbe[:, 3, :], ALU.add, [B_lb], [B_lb])
    recip(lsum, lsum, [B_lb], [B_lb])
    for l in range(L_ALL):
        tt(lbe[:, l, :], lbe[:, l, :], lsum, ALU.mult, [B_lb], [B_lb])
    memset(lbt[:, 0, :], 0.0, [B_par])
    for l in range(1, L_ALL):
        tt(lbt[:, l, :], lbt[:, l - 1, :], lbe[:, l, :], ALU.add, [B_lb, B_par], [B_par])
    ts(omlt[:], lbt[:], -1.0, 1.0, ALU.mult, ALU.add, [B_par], [B_par])
    ts(nomlt[:], lbt[:], 1.0, -1.0, ALU.mult, ALU.add, [B_par], [B_par])
    P.barrier()

    def prow(l, r):
        return ptab[:, l, r:r + 1]

    def norm_to_hT(l, row0):
        A.reset()
        sq = A.alloc([128, 8, T], BF16)
        rs = A.alloc([128, T])
        Bq, Br = Buf(), Buf()
        act(sq, xf[:], AF.Square, [B_xf], [Bq])
        b = PS.get()
        for kc in range(8):
            mm(pb(b), ones_bf[:], sq[:, kc, :], kc == 0, kc == 7, [Bq, B_cst], [PB[b]])
        rstd_from_ss(rs, pb(b), D, [PB[b]], [Br])
        for kc in range(8):
            stt(hT[:, kc, :], xf[:, kc, :], prow(l, row0 + kc), rs, ALU.mult, ALU.mult, [B_xf, Br, B_par], [B_hT])
        P.barrier()

    def branch_project(l, br, ybr, B_y):
        gt = A.alloc([128, T])
        tm = A.alloc([128, T])
        Bg, Bt = Buf(), Buf()
        for half in range(2):
            wb_, Bw = loadw(wsrc(w_br[br], l, half * 512, 512))
            wg_, Bwg = loadw(wsrc(w_in, l, O_GATE + br * 1024 + half * 512, 512))
            for dd in range(4):
                dc = half * 4 + dd
                bg = PS.get()
                for kc in range(8):
                    mm(pb(bg), wg_[:, kc, dd * 128:(dd + 1) * 128], hT[:, kc, :], kc == 0, kc == 7, [Bwg, B_hT], [PB[bg]])
                act(gt, pb(bg), AF.Sigmoid, [PB[bg]], [Bg])
                bp = PS.get()
                for kc in range(8):
                    mm(pb(bp), wb_[:, kc, dd * 128:(dd + 1) * 128], ybr[:, kc, :], kc == 0, kc == 7, [Bw, B_y], [PB[bp]])
                if br == 0:
                    tt(mixed[:, dc, :], pb(bp), gt, ALU.mult, [PB[bp], Bg], [B_mixed])
                else:
                    tt(tm, pb(bp), gt, ALU.mult, [PB[bp], Bg], [Bt])
                    tt(mixed[:, dc, :], mixed[:, dc, :], tm, ALU.add, [Bt, B_mixed], [B_mixed])

    def tm_proj(l, c0, dst, B_dst):
        for cb in range(2):
            wv, Bw = loadw(wsrc(w_in, l, c0 + cb * 512, 512))
            for pp in range(4):
                b = PS.get()
                for kc in range(8):
                    mm(pb(b), hT[:, kc, pp * 128:(pp + 1) * 128], wv[:, kc, :], kc == 0, kc == 7, [Bw, B_hT], [PB[b]])
                acopy(dst[:, pp, cb * 512:(cb + 1) * 512], pb(b), [PB[b]], [B_dst])

    def fm_proj_act(l, c0, nchunk, func, dst, B_dst):
        for c in range(nchunk):
            if c % 4 == 0:
                n = min(4, nchunk - c) * 128
                wq, Bw = loadw(wsrc(w_in, l, c0 + c * 128, n))
            b = PS.get()
            for kc in range(8):
                mm(pb(b), wq[:, kc, (c % 4) * 128:(c % 4 + 1) * 128], hT[:, kc, :], kc == 0, kc == 7, [Bw, B_hT], [PB[b]])
            act(dst[:, c, :], pb(b), func, [PB[b]], [B_dst])

    def conv_fm(l, c0, nchunk, K, tail, B_tail, rw, rb, wstride, outs, first_tile):
        ub = [A.alloc([128, T + K - 1]) for _ in range(2)]
        Bu = [Buf(), Buf()]
        ca = [A.alloc([128, T]) for _ in range(2)]
        Bc = [Buf(), Buf()]
        for c in range(nchunk):
            if c % 4 == 0:
                n = min(4, nchunk - c) * 128
                wq, Bw = loadw(wsrc(w_in, l, c0 + c * 128, n))
            b = PS.get()
            for kc in range(8):
                mm(pb(b), wq[:, kc, (c % 4) * 128:(c % 4 + 1) * 128], hT[:, kc, :], kc == 0, kc == 7, [Bw, B_hT], [PB[b]])
            u, bu, cc, bc = ub[c % 2], Bu[c % 2], ca[c % 2], Bc[c % 2]
            if first_tile:
                memset(u[:, 0:K - 1], 0.0, [bu])
            else:
                vcopy(u[:, 0:K - 1], tail[:, c, :], [B_tail], [bu])
            acopy(u[:, K - 1:K - 1 + T], pb(b), [PB[b]], [bu])
            vcopy(tail[:, c, :], u[:, T:T + K - 1], [bu], [B_tail])
            ts(cc, u[:, 0:T], prow(l, rw + c), prow(l, rb + c), ALU.mult, ALU.add, [bu, B_par], [bc])
            for k in range(1, K):
                stt(cc, u[:, k:k + T], prow(l, rw + k * wstride + c), cc, ALU.mult, ALU.add, [bu, bc, B_par], [bc])
            dst, bd, post = outs(c)
            if post is None:
                act(dst, cc, AF.Silu, [bc], [bd])
            else:
                act(cc, cc, AF.Silu, [bc], [bc])
                ts(dst, cc, post, None, ALU.mult, None, [bc], [bd])

    def phase_hg(l, j):
        A.reset()
        qt = A.alloc([128, 8, T], BF16)
        kt = A.alloc([128, 8, T], BF16)
        vtm = A.alloc([128, 4, 1024], BF16)
        sg = A.alloc([128, 8, T], BF16)
        ybr = A.alloc([128, 8, T], BF16)
        ebl = A.alloc([128, 8, 8])
        mark = A.off
        t_qs, t_sig, t_lf, t_b, t_eb, t_enb, t_k = [A.alloc([128, T]) for _ in range(7)]
        Bqt, Bkt, Bv, Bsg, By, Bebl = Buf(), Buf(), Buf(), Buf(), Buf(), Buf()
        Bt = [Buf() for _ in range(7)]
        if j == 0:
            memset(S_hg[:], 0.0, [B_S])
            memset(Sb_hg[0][:], 0.0, [B_Sb[0]])
        for hb in range(2):
            wq, Bwq = loadw(wsrc(w_in, l, O_HGQ + hb * 512, 512))
            wf, Bwf = loadw(wsrc(w_in, l, O_HGF + hb * 512, 512))
            for hh in range(4):
                h = hb * 4 + hh
                b = PS.get()
                for kc in range(8):
                    mm(pb(b), wq[:, kc, hh * 128:(hh + 1) * 128], hT[:, kc, :], kc == 0, kc == 7, [Bwq, B_hT], [PB[b]])
                act(t_qs, pb(b), AF.Silu, [PB[b]], [Bt[0]])
                b2 = PS.get()
                for kc in range(8):
                    mm(pb(b2), wf[:, kc, hh * 128:(hh + 1) * 128], hT[:, kc, :], kc == 0, kc == 7, [Bwf, B_hT], [PB[b2]])
                act(t_sig, pb(b2), AF.Sigmoid, [PB[b2]], [Bt[1]])
                act(t_lf, t_sig, AF.Ln, [Bt[1], B_par], [Bt[2]], scale=omlt[:, l, h:h + 1], bias=lbt[:, l, h:h + 1])
                scan(t_b, gseg, t_lf, [Bt[2], B_cst], [Bt[3]])
                act(t_eb, t_b, AF.Exp, [Bt[3]], [Bt[4]])
                act(t_enb, t_b, AF.Exp, [Bt[3]], [Bt[5]], scale=-1.0)
                ts(t_k, t_sig, nomlt[:, l, h:h + 1], omlt[:, l, h:h + 1], ALU.mult, ALU.add, [Bt[1], B_par], [Bt[6]])
                tt(qt[:, h, :], t_qs, t_eb, ALU.mult, [Bt[0], Bt[4]], [Bqt])
                tt(kt[:, h, :], t_k, t_enb, ALU.mult, [Bt[6], Bt[5]], [Bkt])
                vcopy(ebl[:, :, h], t_eb.rearrange("p (c j) -> p c j", j=64)[:, :, 63], [Bt[4]], [Bebl])
        tm_proj(l, O_HGI, vtm, Bv)
        fm_proj_act(l, O_HGG, 8, AF.Silu, sg, Bsg)
        P.barrier()
        A.off = mark
        at = A.alloc([128, 8, 128], BF16)
        ktm = A.alloc([128, 8, 128], BF16)
        tmpS = A.alloc([128, 8, 128])
        sq = A.alloc([128, 8, 128], BF16)
        rs = A.alloc([128, 8, 128])
        yn = A.alloc([128, 8, 128])
        Bat, Bktm, BtS, Bsq, Brs, Byn = Buf(), Buf(), Buf(), Buf(), Buf(), Buf()
        cur = 0
        for pp in range(4):
            c0 = pp * 128
            bo = PS.get(2, hold=True)
            bd = PS.get(2, hold=True)
            for h in range(8):
                ba = PS.get()
                mm(pb(ba)[:, 0:128], kt[:, h, c0:c0 + 128], qt[:, h, c0:c0 + 128], True, True, [Bkt, Bqt], [PB[ba]])
                tt(at[:, h, :], pb(ba)[:, 0:128], mask2, ALU.mult, [PB[ba], B_cst], [Bat])
                bt_ = PS.get()
                tr(pbb(bt_)[:, 0:128], kt[:, h, c0:c0 + 128], identb[:], [Bkt, B_cst], [PB[bt_]])
                acopy(ktm[:, h, :], pbb(bt_)[:, 0:128], [PB[bt_]], [Bktm])
            for c in range(2):
                r0 = c * 64
                ch = pp * 2 + c
                for h in range(8):
                    mm(pd(bd)[:, h * 128:(h + 1) * 128], ktm[r0:r0 + 64, h, :], vtm[r0:r0 + 64, pp, h * 128:(h + 1) * 128],
                       True, True, [Bktm, Bv], [PB[bd], PB[bd + 1]])
                for h in range(8):
                    o = pd(bo)[:, h * 128 + r0:h * 128 + r0 + 64]
                    mm(o, vtm[r0:r0 + 64, pp, h * 128:(h + 1) * 128], at[r0:r0 + 64, h, r0:r0 + 64], True, False,
                       [Bv, Bat], [PB[bo], PB[bo + 1]])
                    mm(o, Sb_hg[cur][:, h, :], qt[:, h, c0 + r0:c0 + r0 + 64], False, True,
                       [B_Sb[cur], Bqt], [PB[bo], PB[bo + 1]])
                tt(tmpS, pd(bd).rearrange("p (h v) -> p h v", h=8), S_hg[:], ALU.add, [PB[bd], PB[bd + 1], B_S], [BtS])
                tt(S_hg[:], tmpS, ebl[:, ch, :].unsqueeze(2).broadcast_to([128, 8, 128]), ALU.mult, [BtS, Bebl], [B_S])
                acopy(Sb_hg[1 - cur][:], S_hg[:], [B_S], [B_Sb[1 - cur]])
                cur = 1 - cur
            PS.release(bd, 2)
            o3 = pd(bo)
            sqf = sq.rearrange("p h t -> p (h t)")
            act(sqf, o3, AF.Square, [PB[bo], PB[bo + 1]], [Bsq])
            bs = PS.get(2)
            for hf in range(2):
                mm(pd(bs)[:, hf * 512:(hf + 1) * 512], ones_bf[:], sqf[:, hf * 512:(hf + 1) * 512], True, True,
                   [Bsq, B_cst], [PB[bs], PB[bs + 1]])
            rstd_from_ss(rs.rearrange("p h t -> p (h t)"), pd(bs), 128, [PB[bs], PB[bs + 1]], [Brs])
            tt(yn.rearrange("p h t -> p (h t)"), o3, rs.rearrange("p h t -> p (h t)"), ALU.mult, [PB[bo], PB[bo + 1], Brs], [Byn])
            stt(ybr[:, :, c0:c0 + 128], yn, prow(l, R_HGG), sg[:, :, c0:c0 + 128], ALU.mult, ALU.mult, [Byn, Bsg, B_par], [By])
            PS.release(bo, 2)
        assert cur == 0
        branch_project(l, 0, ybr, By)
        P.barrier()

    def phase_ml(l, j):
        A.reset()
        qm = A.alloc([128, 4, T], BF16)
        km = A.alloc([128, 4, T], BF16)
        vtm = A.alloc([128, 4, 1024], BF16)
        so = A.alloc([128, 8, T], BF16)
        ybr = A.alloc([128, 8, T], BF16)
        g_lf, g_bh, g_g, g_e, g_ei = [A.alloc([4, T]) for _ in range(5)]
        elbc = A.alloc([128, 4, 8])
        Bqm, Bkm, Bv, Bso, By = Buf(), Buf(), Buf(), Buf(), Buf()
        Bg = [Buf() for _ in range(5)]
        Bel = Buf()
        mark = A.off
        if j == 0:
            memset(C_ml[:], 0.0, [B_C])
            memset(Cb_ml[0][:], 0.0, [B_Cb[0]])
            memset(n_ml[:], 0.0, [B_n])
            memset(nrep[0][:], 0.0, [B_nrep[0]])
        wif, Bwif = loadw(wsrc(w_in, l, O_MLIF, 8))
        bi_, bf_ = PS.get(), PS.get()
        for kc in range(8):
            mm(pb(bi_)[0:4, :], wif[:, kc, 0:4], hT[:, kc, :], kc == 0, kc == 7, [Bwif, B_hT], [PB[bi_]])
        for kc in range(8):
            mm(pb(bf_)[0:4, :], wif[:, kc, 4:8], hT[:, kc, :], kc == 0, kc == 7, [Bwif, B_hT], [PB[bf_]])
        act(g_lf, pb(bf_)[0:4, :], AF.Sigmoid, [PB[bf_], B_par], [Bg[0]], bias=gbt[0:4, l, 1:2])
        act(g_lf, g_lf, AF.Ln, [Bg[0]], [Bg[0]])
        scan(g_bh, gseg[0:4, :], g_lf, [Bg[0], B_cst], [Bg[1]])
        stt(g_g, pb(bi_)[0:4, :], gbt[0:4, l, 0:1], g_bh, ALU.add, ALU.subtract, [PB[bi_], Bg[1], B_par], [Bg[2]])
        act(g_g, g_g, AF.Exp, [Bg[2]], [Bg[2]])
        act(g_e, g_bh, AF.Exp, [Bg[1]], [Bg[3]])
        act(g_ei, g_bh, AF.Exp, [Bg[1]], [Bg[4]], scale=-1.0)
        be = PS.get()
        gel = g_e.rearrange("p (c j) -> p c j", j=64)[:, :, 63]
        for h in range(4):
            mm(pb(be)[:, h * 8:(h + 1) * 8], sel[:, h * 128:(h + 1) * 128], gel, True, True, [Bg[3], B_cst], [PB[be]])
        acopy(elbc.rearrange("p h c -> p (h c)"), pb(be)[:, 0:32], [PB[be]], [Bel])

        def qk_out(c):
            if c < 4:
                return qm[:, c, :], Bqm, None
            return km[:, c - 4, :], Bkm, 128.0 ** -0.5
        conv_fm(l, O_MLQ, 8, 4, tail_ml, B_tml, R_MLW, R_MLB, 8, qk_out, j == 0)
        tm_proj(l, O_MLV, vtm, Bv)
        fm_proj_act(l, O_MLO, 8, AF.Sigmoid, so, Bso)
        P.barrier()
        A.off = mark
        gtm = A.alloc([128, 4])
        einvbc = A.alloc([128, 4, 128])
        at = A.alloc([128, 4, 128], BF16)
        kgtm = A.alloc([128, 4, 128], BF16)
        tmpS = A.alloc([128, 4, 256])
        tmpn = A.alloc([128, 4])
        dnm = A.alloc([128, 4, 128])
        hh = A.alloc([128, 4, 2, 128])
        sq = A.alloc([128, 8, 128], BF16)
        rs = A.alloc([128, 4, 128])
        Bgtm, Bei, Bat, Bkg, BtS, Btn, Bdn, Bhh, Bsq, Brs = [Buf() for _ in range(10)]
        cur = 0
        for pp in range(4):
            c0 = pp * 128
            bgt = PS.get()
            tr(pb(bgt)[:, 0:4], g_g[0:4, c0:c0 + 128], ident[0:4, 0:4], [Bg[2], B_cst], [PB[bgt]])
            acopy(gtm, pb(bgt)[:, 0:4], [PB[bgt]], [Bgtm])
            bei = PS.get()
            for h in range(4):
                mm(pb(bei)[:, h * 128:(h + 1) * 128], sel[:, h * 128:(h + 1) * 128], g_ei[0:4, c0:c0 + 128], True, True,
                   [Bg[4], B_cst], [PB[bei]])
            acopy(einvbc.rearrange("p h t -> p (h t)"), pb(bei), [PB[bei]], [Bei])
            bnum = PS.get(2, hold=True)
            bdC = PS.get(2, hold=True)
            bden = PS.get(1, hold=True)
            bdn = PS.get(1, hold=True)
            for h in range(4):
                ba = PS.get()
                mm(pb(ba)[:, 0:128], km[:, h, c0:c0 + 128], qm[:, h, c0:c0 + 128], True, True, [Bkm, Bqm], [PB[ba]])
                stt(at[:, h, :], pb(ba)[:, 0:128], gtm[:, h:h + 1], mask2, ALU.mult, ALU.mult, [PB[ba], Bgtm, B_cst], [Bat])
                bt_ = PS.get()
                tr(pbb(bt_)[:, 0:128], km[:, h, c0:c0 + 128], identb[:], [Bkm, B_cst], [PB[bt_]])
                ts(kgtm[:, h, :], pbb(bt_)[:, 0:128], gtm[:, h:h + 1], None, ALU.mult, None, [PB[bt_], Bgtm], [Bkg])
            for c in range(2):
                r0 = c * 64
                ch = pp * 2 + c
                tc0 = c0 + r0
                for h in range(4):
                    mm(pd(bdC)[:, h * 256:(h + 1) * 256], kgtm[r0:r0 + 64, h, :], vtm[r0:r0 + 64, pp, h * 256:(h + 1) * 256],
                       True, True, [Bkg, Bv], [PB[bdC], PB[bdC + 1]])
                    mm(pb(bdn)[:, h:h + 1], kgtm[r0:r0 + 64, h, :], ones_bf[r0:r0 + 64, 0:1],
                       True, True, [Bkg, B_cst], [PB[bdn]])
                for h in range(4):
                    for dvc in range(2):
                        o = pd(bnum)[:, (h * 2 + dvc) * 128 + r0:(h * 2 + dvc) * 128 + r0 + 64]
                        mm(o, vtm[r0:r0 + 64, pp, h * 256 + dvc * 128:h * 256 + dvc * 128 + 128], at[r0:r0 + 64, h, r0:r0 + 64],
                           True, False, [Bv, Bat], [PB[bnum], PB[bnum + 1]])
                        mm(o, Cb_ml[cur][:, h, dvc * 128:(dvc + 1) * 128], qm[:, h, tc0:tc0 + 64], False, True,
                           [B_Cb[cur], Bqm], [PB[bnum], PB[bnum + 1]])
                    od = pb(bden)[:, h * 128 + r0:h * 128 + r0 + 64]
                    mm(od, ones_bf[r0:r0 + 64, :], at[r0:r0 + 64, h, r0:r0 + 64], True, False, [Bat, B_cst], [PB[bden]])
                    mm(od, nrep[cur][:, h, :], qm[:, h, tc0:tc0 + 64], False, True, [B_nrep[cur], Bqm], [PB[bden]])
                elb = elbc[:, :, ch]
                tt(tmpS, pd(bdC).rearrange("p (h v) -> p h v", h=4), C_ml[:], ALU.add,
                   [PB[bdC], PB[bdC + 1], B_C], [BtS])
                tt(C_ml[:], tmpS, elb.unsqueeze(2).broadcast_to([128, 4, 256]), ALU.mult, [BtS, Bel], [B_C])
                acopy(Cb_ml[1 - cur][:], C_ml[:], [B_C], [B_Cb[1 - cur]])
                tt(tmpn, pb(bdn)[:, 0:4], n_ml[:], ALU.add, [PB[bdn], B_n], [Btn])
                tt(n_ml[:], tmpn, elb, ALU.mult, [Btn, Bel], [B_n])
                vcopy(nrep[1 - cur][:], n_ml[:].unsqueeze(2).broadcast_to([128, 4, 128]), [B_n], [B_nrep[1 - cur]])
                cur = 1 - cur
            PS.release(bdC, 2)
            PS.release(bdn, 1)
            dflat = dnm.rearrange("p h t -> p (h t)")
            act(dflat, pb(bden), AF.Abs, [PB[bden]], [Bdn])
            tt(dflat, dflat, einvbc.rearrange("p h t -> p (h t)"), ALU.max, [Bdn, Bei], [Bdn])
            recip(dflat, dflat, [Bdn], [Bdn])
            tt(hh, pd(bnum).rearrange("p (h d t) -> p h d t", h=4, d=2), dnm.unsqueeze(2).broadcast_to([128, 4, 2, 128]),
               ALU.mult, [PB[bnum], PB[bnum + 1], Bdn], [Bhh])
            PS.release(bnum, 2)
            PS.release(bden, 1)
            hflat = hh.rearrange("p h d t -> p (h d t)")
            act(sq.rearrange("p c t -> p (c t)"), hflat, AF.Square, [Bhh], [Bsq])
            bss = PS.get()
            for h in range(4):
                for dvc in range(2):
                    mm(pb(bss)[:, h * 128:(h + 1) * 128], ones_bf[:], sq[:, h * 2 + dvc, :], dvc == 0, dvc == 1,
                       [Bsq, B_cst], [PB[bss]])
            rstd_from_ss(rs.rearrange("p h t -> p (h t)"), pb(bss), 256, [PB[bss]], [Brs])
            tt(hh, hh, rs.unsqueeze(2).broadcast_to([128, 4, 2, 128]), ALU.mult, [Bhh, Brs], [Bhh])
            gml = ptab[:, l, R_MLG:R_MLG + 2].unsqueeze(1).unsqueeze(3).broadcast_to([128, 4, 2, 128])
            tt(hh, hh, gml, ALU.mult, [Bhh, B_par], [Bhh])
            tt(ybr[:, :, c0:c0 + 128], hh.rearrange("p h d t -> p (h d) t"), so[:, :, c0:c0 + 128], ALU.mult, [Bhh, Bso], [By])
        assert cur == 0
        branch_project(l, 1, ybr, By)
        P.barrier()

    def phase_mb(l, j):
        A.reset()
        sz = A.alloc([128, 8, T], BF16)
        xbc = A.alloc([128, 16, T], BF16)
        ybr = A.alloc([128, 8, T], BF16)
        yfm = A.alloc([128, 8, T], BF16)
        Bsz, Bx, By, Byf = Buf(), Buf(), Buf(), Buf()
        if j == 0:
            memset(St_mb[:], 0.0, [B_St])
            memset(Stb_mb[:], 0.0, [B_Stb])
        fm_proj_act(l, O_MBZ, 8, AF.Silu, sz, Bsz)
        mark = A.off
        conv_fm(l, O_MBX, 16, 4, tail_mb, B_tmb, R_MBW, R_MBB, 16, lambda c: (xbc[:, c, :], Bx, None), j == 0)
        P.barrier()
        A.off = mark
        wdt, Bwdt = loadw(wsrc(w_in, l, O_MBDT, 16))
        dtp, dte, dt_, dtA, cum, dcd, dec, ecum = [A.alloc([64, 16]) for _ in range(8)]
        elast = A.alloc([128, 16])
        R = A.alloc([64, 16, 64])
        LT = A.alloc([64, 16, 64])
        CBm = A.alloc([64, 4, 64])
        WT = A.alloc([64, 16, 64], BF16)
        xdt = A.alloc([64, 16, 64], BF16)
        xdd = A.alloc([64, 16, 64], BF16)
        Btm = A.alloc([64, 512], BF16)
        tmpy = A.alloc([64, 16, 64])
        ytm = A.alloc([64, 1024], BF16)
        tmpS = A.alloc([128, 16, 64])
        Bd = [Buf() for _ in range(8)]
        Bel, BR, BLT, BCB, BWT, Bxdt, Bxdd, BBt, Bty, Bytm, BtS = [Buf() for _ in range(11)]
        for ch in range(8):
            cc = ch * 64
            b = PS.get()
            for kc in range(8):
                mm(pb(b)[0:64, 0:16], hT[:, kc, cc:cc + 64], wdt[:, kc, 0:16], kc == 0, kc == 7, [Bwdt, B_hT], [PB[b]])
            tt(dtp, pb(b)[0:64, 0:16], bvt[0:64, l, 0:16], ALU.add, [PB[b], B_par], [Bd[0]])
            act(dte, dtp, AF.Exp, [Bd[0]], [Bd[1]])
            ts(dte, dte, 1.0, None, ALU.add, None, [Bd[1]], [Bd[1]])
            act(dt_, dte, AF.Ln, [Bd[1]], [Bd[2]])
            tt(dtA, dt_, aneg[0:64, l, :], ALU.mult, [Bd[2], B_par], [Bd[3]])
            tt(R, dtA.unsqueeze(2).broadcast_to([64, 16, 64]), maskc.unsqueeze(1).broadcast_to([64, 16, 64]), ALU.mult,
               [Bd[3], B_cst], [BR])
            Rf = R.rearrange("p h t -> p (h t)")
            bD = PS.get(2)
            for hf in range(2):
                mm(pd(bD)[0:64, hf * 512:(hf + 1) * 512], SLm, Rf[:, hf * 512:(hf + 1) * 512], True, True,
                   [BR, B_cst], [PB[bD], PB[bD + 1]])
            act(LT.rearrange("p h t -> p (h t)"), pd(bD)[0:64, :], AF.Exp, [PB[bD], PB[bD + 1]], [BLT])
            bc_ = PS.get()
            mm(pb(bc_)[0:64, 0:16], maskc, dtA, True, True, [Bd[3], B_cst], [PB[bc_]])
            mm(pb(bc_)[:, 16:32], ones_f[0:64, :], dtA, True, True, [Bd[3], B_cst], [PB[bc_]])
            acopy(cum, pb(bc_)[0:64, 0:16], [PB[bc_]], [Bd[4]])
            act(ecum, pb(bc_)[0:64, 0:16], AF.Exp, [PB[bc_]], [Bd[7]])
            act(elast, pb(bc_)[:, 16:32], AF.Exp, [PB[bc_]], [Bel])
            tt(dcd, pb(bc_)[0:64, 16:32], cum, ALU.subtract, [PB[bc_], Bd[4]], [Bd[5]])
            act(dec, dcd, AF.Exp, [Bd[5]], [Bd[6]])
            bcb = PS.get()
            for g in range(4):
                mm(pb(bcb)[0:64, g * 64:(g + 1) * 64], xbc[:, 8 + g, cc:cc + 64], xbc[:, 12 + g, cc:cc + 64], True, True,
                   [Bx], [PB[bcb]])
            tt(CBm, pb(bcb)[0:64, 0:256].rearrange("p (g t) -> p g t", g=4), maskc.unsqueeze(1).broadcast_to([64, 4, 64]),
               ALU.mult, [PB[bcb], B_cst], [BCB])
            tt(WT.rearrange("p (g k) t -> p g k t", g=4), LT.rearrange("p (g k) t -> p g k t", g=4),
               CBm.unsqueeze(2).broadcast_to([64, 4, 4, 64]), ALU.mult, [BLT, BCB], [BWT])
            bx = PS.get()
            for hc in range(8):
                tr(pbb(bx)[0:64, hc * 128:(hc + 1) * 128], xbc[:, hc, cc:cc + 64], identb[:], [Bx, B_cst], [PB[bx]])
            bB = PS.get()
            for g in range(4):
                tr(pbb(bB)[0:64, g * 128:(g + 1) * 128], xbc[:, 8 + g, cc:cc + 64], identb[:], [Bx, B_cst], [PB[bB]])
            tt(xdt, pbb(bx)[0:64, 0:1024].rearrange("p (h q) -> p h q", h=16), dt_.unsqueeze(2).broadcast_to([64, 16, 64]),
               ALU.mult, [PB[bx], Bd[2]], [Bxdt])
            tt(xdd, xdt, dec.unsqueeze(2).broadcast_to([64, 16, 64]), ALU.mult, [Bxdt, Bd[6]], [Bxdd])
            acopy(Btm, pbb(bB)[0:64, 0:512], [PB[bB]], [BBt])
            by_ = PS.get(2)
            for h in range(16):
                mm(pd(by_)[0:64, h * 64:(h + 1) * 64], WT[:, h, :], xdt[:, h, :], True, True, [BWT, Bxdt], [PB[by_], PB[by_ + 1]])
            byi = PS.get(2)
            Stbf = Stb_mb[:].rearrange("p h q -> p (h q)")
            for g in range(4):
                mm(pd(byi)[0:64, g * 256:(g + 1) * 256], xbc[:, 12 + g, cc:cc + 64], Stbf[:, g * 256:(g + 1) * 256], True, True,
                   [Bx, B_Stb], [PB[byi], PB[byi + 1]])
            tt(tmpy, pd(byi)[0:64, :].rearrange("p (h q) -> p h q", h=16), ecum.unsqueeze(2).broadcast_to([64, 16, 64]),
               ALU.mult, [PB[byi], PB[byi + 1], Bd[7]], [Bty])
            tt(ytm, pd(by_)[0:64, :], tmpy.rearrange("p h q -> p (h q)"), ALU.add, [PB[by_], PB[by_ + 1], Bty], [Bytm])
            byt = PS.get()
            for hc in range(8):
                tr(pbb(byt)[:, hc * 64:(hc + 1) * 64], ytm[0:64, hc * 128:(hc + 1) * 128], identb[0:64, 0:64],
                   [Bytm, B_cst], [PB[byt]])
            acopy(yfm[:, :, cc:cc + 64], pbb(byt)[:, 0:512].rearrange("p (c t) -> p c t", c=8), [PB[byt]], [Byf])
            bs = PS.get(2)
            xddf = xdd.rearrange("p h q -> p (h q)")
            for g in range(4):
                mm(pd(bs)[:, g * 256:(g + 1) * 256], Btm[0:64, g * 128:(g + 1) * 128], xddf[:, g * 256:(g + 1) * 256], True, True,
                   [BBt, Bxdd], [PB[bs], PB[bs + 1]])
            tt(tmpS, St_mb[:], elast.unsqueeze(2).broadcast_to([128, 16, 64]), ALU.mult, [B_St, Bel], [BtS])
            tt(St_mb[:], tmpS, pd(bs).rearrange("p (h q) -> p h q", h=16), ALU.add, [BtS, PB[bs], PB[bs + 1]], [B_St])
            acopy(Stb_mb[:], St_mb[:], [B_St], [B_Stb])
        P.barrier()
        A.off = mark
        t1 = A.alloc([128, 2, T])
        sq = A.alloc([128, 2, T], BF16)
        rs = A.alloc([128, T])
        Bt1, Bsq, Brs = Buf(), Buf(), Buf()
        for g in range(4):
            dv = ptab[:, l, R_MBD + 2 * g:R_MBD + 2 * g + 2].unsqueeze(2).broadcast_to([128, 2, T])
            tt(t1, xbc[:, 2 * g:2 * g + 2, :], dv, ALU.mult, [Bx, B_par], [Bt1])
            tt(t1, t1, yfm[:, 2 * g:2 * g + 2, :], ALU.add, [Bt1, Byf], [Bt1])
            tt(t1, t1, sz[:, 2 * g:2 * g + 2, :], ALU.mult, [Bt1, Bsz], [Bt1])
            act(sq, t1, AF.Square, [Bt1], [Bsq])
            b = PS.get()
            for k in range(2):
                mm(pb(b), ones_bf[:], sq[:, k, :], k == 0, k == 1, [Bsq, B_cst], [PB[b]])
            rstd_from_ss(rs, pb(b), 256, [PB[b]], [Brs])
            tt(t1, t1, rs.unsqueeze(1).broadcast_to([128, 2, T]), ALU.mult, [Bt1, Brs], [Bt1])
            gm = ptab[:, l, R_MBG + 2 * g:R_MBG + 2 * g + 2].unsqueeze(2).broadcast_to([128, 2, T])
            tt(ybr[:, 2 * g:2 * g + 2, :], t1, gm, ALU.mult, [Bt1, B_par], [By])
        branch_project(l, 2, ybr, By)
        P.barrier()

    def phase_out(l):
        A.reset()
        mixb = A.alloc([128, 8, T], BF16)
        Bm = Buf()
        vcopy(mixb, mixed[:], [B_mixed], [Bm])
        for half in range(2):
            wo, Bw = loadw(wsrc(w_out, l, half * 512, 512))
            for dd in range(4):
                dc = half * 4 + dd
                b = PS.get()
                for kc in range(8):
                    mm(pb(b), wo[:, kc, dd * 128:(dd + 1) * 128], mixb[:, kc, :], kc == 0, kc == 7, [Bw, Bm], [PB[b]])
                tt(xf[:, dc, :], xf[:, dc, :], pb(b), ALU.add, [PB[b], B_xf], [B_xf])
        P.barrier()

    def phase_ffn(l, j):
        A.reset()
        prod = A.alloc([128, 22, T], BF16)
        Bp = Buf()
        ub = [A.alloc([128, T + 2]) for _ in range(4)]
        Bu = [Buf() for _ in range(4)]
        cg, cv = A.alloc([128, T]), A.alloc([128, T])
        Bcg, Bcv = Buf(), Buf()
        for jc in range(22):
            if jc % 4 == 0:
                n = min(4, 22 - jc) * 128
                wg_, Bwg = loadw(wsrc(w_up, l, jc * 128, n))
                wv_, Bwv = loadw(wsrc(w_up, l, DFF + jc * 128, n))
            res = []
            for which, (wq, Bw) in enumerate(((wg_, Bwg), (wv_, Bwv))):
                c = jc + 22 * which
                b = PS.get()
                for kc in range(8):
                    mm(pb(b), wq[:, kc, (jc % 4) * 128:(jc % 4 + 1) * 128], hT[:, kc, :], kc == 0, kc == 7, [Bw, B_hT], [PB[b]])
                u, bu = ub[(jc % 2) * 2 + which], Bu[(jc % 2) * 2 + which]
                if j == 0:
                    memset(u[:, 0:2], 0.0, [bu])
                else:
                    vcopy(u[:, 0:2], tail_ff[:, c, :], [B_tff], [bu])
                acopy(u[:, 2:2 + T], pb(b), [PB[b]], [bu])
                vcopy(tail_ff[:, c, :], u[:, T:T + 2], [bu], [B_tff])
                cc, bc = (cg, Bcg) if which == 0 else (cv, Bcv)
                ts(cc, u[:, 0:T], prow(l, R_FFW + c), prow(l, R_FFB + c), ALU.mult, ALU.add, [bu, B_par], [bc])
                for k in range(1, 3):
                    stt(cc, u[:, k:k + T], prow(l, R_FFW + k * 44 + c), cc, ALU.mult, ALU.add, [bu, bc, B_par], [bc])
            act(cg, cg, AF.Silu, [Bcg], [Bcg])
            tt(prod[:, jc, :], cg, cv, ALU.mult, [Bcg, Bcv], [Bp])
        for half in range(2):
            blks = []
            for k0, nk in ((0, 8), (8, 8), (16, 6)):
                blks.append(loadw(wsrc(w_down, l, half * 512, 512, k0, nk)))
            for dd in range(4):
                dc = half * 4 + dd
                b = PS.get()
                for jc in range(22):
                    wq, Bw = blks[jc // 8]
                    mm(pb(b), wq[:, jc % 8, dd * 128:(dd + 1) * 128], prod[:, jc, :], jc == 0, jc == 21, [Bw, Bp], [PB[b]])
                tt(xf[:, dc, :], xf[:, dc, :], pb(b), ALU.add, [PB[b], B_xf], [B_xf])
        P.barrier()

    def load_x(l, j):
        t0 = j * T
        if l == 0:
            A.reset()
            xtm = A.alloc([128, 4, D])
            Bxt = Buf()
            P.wait_all("sp")
            P.dma("sp", xtm, x_in[t0:t0 + T, :].rearrange("(g p) d -> p g d", p=128), writes=[Bxt])
            for g in range(4):
                for kc in range(8):
                    b = PS.get()
                    tr(pb(b)[:, 0:128], xtm[:, g, kc * 128:(kc + 1) * 128], ident, [Bxt, B_cst], [PB[b]])
                    acopy(xf[:, kc, g * 128:(g + 1) * 128], pb(b)[:, 0:128], [PB[b]], [B_xf])
            P.barrier()
        else:
            P.dma("sp", xf[:], xscr[:, :, t0:t0 + T].rearrange("c p t -> p c t"), reads=[B_xs[j]], writes=[B_xf])

    def store_x(l, j):
        t0 = j * T
        if l < NL - 1:
            P.dma("sp", xscr[:, :, t0:t0 + T].rearrange("c p t -> p c t"), xf[:], reads=[B_xf], writes=[B_xs[j]])
        else:
            A.reset()
            sq = A.alloc([128, 8, T], BF16)
            rs = A.alloc([128, T])
            yo = A.alloc([128, 8, T])
            otm = A.alloc([128, 4, D])
            Bq, Br, Byo, Bot = Buf(), Buf(), Buf(), Buf()
            act(sq, xf[:], AF.Square, [B_xf], [Bq])
            b = PS.get()
            for kc in range(8):
                mm(pb(b), ones_bf[:], sq[:, kc, :], kc == 0, kc == 7, [Bq, B_cst], [PB[b]])
            rstd_from_ss(rs, pb(b), D, [PB[b]], [Br])
            for kc in range(8):
                stt(yo[:, kc, :], xf[:, kc, :], prow(0, R_FIN + kc), rs, ALU.mult, ALU.mult, [B_xf, Br, B_par], [Byo])
            for g in range(4):
                for kc in range(8):
                    b = PS.get()
                    tr(pb(b)[:, 0:128], yo[:, kc, g * 128:(g + 1) * 128], ident, [Byo, B_cst], [PB[b]])
                    acopy(otm[:, g, kc * 128:(kc + 1) * 128], pb(b)[:, 0:128], [PB[b]], [Bot])
            P.dma("sp", out_d[t0:t0 + T, :].rearrange("(g p) d -> p g d", p=128), otm, reads=[Bot], writes=[B_out[j]])
            P.barrier()
            P.finish("act", [B_out[j]])
            P.finish("dve", [B_out[j]])
            P.finish("pe", [B_out[j]])

    for l in range(NL):
        for j in range(NTILE):
            load_x(l, j)
            norm_to_hT(l, R_N1)
            phase_hg(l, j)
            phase_ml(l, j)
            phase_mb(l, j)
            phase_out(l)
            norm_to_hT(l, R_N2)
            phase_ffn(l, j)
            store_x(l, j)
    P.finish("sp", B_out)
    P.build(st)
    st.close()
    return nc, P


def make_consts():
    c = np.zeros((128, NCONST), np.float32)
    c[:, C_ID:C_ID + 128] = np.eye(128, dtype=np.float32)
    s = np.arange(128)[:, None]
    t = np.arange(128)[None, :]
    c[:, C_M2:C_M2 + 128] = ((s // 64 == t // 64) & (s <= t)).astype(np.float32)
    c[0:64, C_SL:C_SL + 64] = (np.arange(64)[:, None] > np.arange(64)[None, :]).astype(np.float32)
    gs = np.ones(512, np.float32)
    gs[::64] = 0.0
    c[:, C_GS:C_GS + 512] = gs[None, :]
    for h in range(4):
        c[h, C_SEL + h * 128:C_SEL + (h + 1) * 128] = 1.0
    c[:, C_ONE:C_ONE + 128] = 1.0
    return c


def pack_params(inp):
    pv = np.zeros((L_ALL, NROW, 128), np.float32)
    for l in range(L_ALL):
        pv[l, R_N1:R_N1 + 8] = inp["norm1_g"][l].reshape(8, 128)
        pv[l, R_N2:R_N2 + 8] = inp["norm2_g"][l].reshape(8, 128)
        pv[l, R_LB:R_LB + 8] = inp["hg_lb_logits"][l].reshape(8, 128)
        pv[l, R_MLW:R_MLW + 32] = inp["ml_conv_w"][l].reshape(4 * 8, 128)
        pv[l, R_MLB:R_MLB + 8] = inp["ml_conv_b"][l].reshape(8, 128)
        pv[l, R_MBW:R_MBW + 64] = inp["mb_conv_w"][l].reshape(4 * 16, 128)
        pv[l, R_MBB:R_MBB + 16] = inp["mb_conv_b"][l].reshape(16, 128)
        pv[l, R_FFW:R_FFW + 132] = inp["ffn_conv_w"][l].reshape(3 * 44, 128)
        pv[l, R_FFB:R_FFB + 44] = inp["ffn_conv_b"][l].reshape(44, 128)
        pv[l, R_MBG:R_MBG + 8] = inp["mb_norm_g"][l].reshape(8, 128)
        pv[l, R_HGG] = inp["hg_norm_g"][l]
        pv[l, R_MLG:R_MLG + 2] = inp["ml_norm_g"][l].reshape(2, 128)
        pv[l, R_MBD:R_MBD + 8] = np.repeat(inp["mb_d"][l], 64).reshape(8, 128)
        pv[l, R_FIN:R_FIN + 8] = inp["final_g"].reshape(8, 128)
    bv = np.concatenate([inp["mb_dt_bias"], inp["mb_a_log"]], axis=1).reshape(1, L_ALL * 32)
    gb = inp["ml_gate_b"].reshape(L_ALL, 2, 4).transpose(2, 0, 1).reshape(4, L_ALL * 2)
    return (np.ascontiguousarray(pv.reshape(L_ALL * 3 * 128, 128)), np.ascontiguousarray(bv.astype(np.float32)),
            np.ascontiguousarray(gb.astype(np.float32)))


_CACHE = {}


def run_cores(inp, xs, NL):
    NTOK = xs[0].shape[0]
    key = (NTOK, NL)
    if key not in _CACHE:
        _CACHE[key] = build_program(NTOK, NL)[0]
    nc = _CACHE[key]
    pv, bv, gb = pack_params(inp)
    cst = make_consts()
    shared = {"w_in": inp["w_in"], "w_br_hg": inp["w_br_hg"], "w_br_ml": inp["w_br_ml"], "w_br_mb": inp["w_br_mb"],
              "w_out": inp["w_out"], "w_up": inp["w_up"], "w_down": inp["w_down"], "pvec": pv, "bvec": bv, "gbv": gb,
              "consts": cst}
    shared = {k: np.ascontiguousarray(np.asarray(v, dtype=np.float32)) for k, v in shared.items()}
    in_maps = [dict(shared, x=np.ascontiguousarray(x)) for x in xs]
    res = run_bass_kernel_spmd(nc, in_maps, core_ids=list(range(len(xs))))
    return [r["out"] for r in res.results]


def kernel(**inputs):
    inp = {k: np.asarray(v) for k, v in inputs.items()}
    x = inp["x"].astype(np.float32)
    Bn, S, _ = x.shape
    xs = [x[b] for b in range(Bn)]
    outs = run_cores(inp, xs, L_ALL)
    return np.stack(outs[:Bn], axis=0).astype(np.float32)
```

```python
import contextlib
import numpy as np
import concourse.bass as bass
import concourse.mybir as mybir
from concourse.bass_utils import run_bass_kernel_spmd

F32 = mybir.dt.float32
BF16 = mybir.dt.bfloat16
AF = mybir.ActivationFunctionType
ALU = mybir.AluOpType

L_ALL = 4
D = 1024
NIN = 13336
DFF = 2816
T = 512
EPS = 1e-6
O_HGQ, O_HGF, O_HGI, O_HGG = 0, 1024, 2048, 3072
O_MLQ, O_MLK, O_MLV, O_MLIF, O_MLO = 4096, 4608, 5120, 6144, 6152
O_MBZ, O_MBX, O_MBDT, O_GATE = 7176, 8200, 10248, 10264
R_N1, R_N2, R_LB, R_MLW, R_MLB, R_MBW, R_MBB, R_FFW, R_FFB, R_MBG, R_HGG, R_MLG, R_MBD, R_FIN = (
    0, 8, 16, 24, 56, 64, 128, 144, 276, 320, 328, 329, 331, 339)
NROW = 384
C_ID, C_M2, C_SL, C_GS, C_SEL, C_ONE, C_EPS, NCONST = 0, 128, 256, 320, 832, 1344, 1472, 1473


class Buf:
    __slots__ = ("name", "w", "r")

    def __init__(self, name=""):
        self.name = name
        self.w = None
        self.r = []


class Prog:
    ENGS = ("pe", "act", "dve", "pool", "sp")

    def __init__(self, nc, n_dma_sems=8, same_engine_sync=True):
        self.nc = nc
        self.ops = {e: [] for e in self.ENGS}
        self.cnt = {e: 0 for e in self.ENGS}
        self.waited = {e: {} for e in self.ENGS}
        self.same_engine_sync = same_engine_sync
        self.n_dma_sems = n_dma_sems
        self.dma_rr = {e: 0 for e in self.ENGS}
        self.dma_cnt = {}
        self.semh = {}
        self.sem_names = [("c", e) for e in self.ENGS]
        for e in ("sp", "act", "pool"):
            for j in range(n_dma_sems):
                self.sem_names.append(("d", e, j))
                self.dma_cnt[("d", e, j)] = 0
        self.ninst = 0
        self.marks = []
        self.sym = {e: [] for e in self.ENGS}

    def _need(self, eng, tok):
        if tok is None:
            return
        semkey, val, weng = tok
        if weng == eng and semkey[0] == "c":
            if eng == "pe" or not self.same_engine_sync:
                return
        if self.waited[eng].get(semkey, 0) >= val:
            return
        self.waited[eng][semkey] = val
        self.sym[eng].append(("w", semkey, val))
        self.ops[eng].append(lambda h: h.wait_ge(self.semh[semkey], val))

    def _deps(self, eng, reads, writes):
        best = {}
        for b in reads:
            t = b.w
            if t is not None and best.get(t[0], (None, 0))[1] < t[1]:
                best[t[0]] = t
        for b in writes:
            for t in [b.w] + b.r:
                if t is not None and best.get(t[0], (None, 0))[1] < t[1]:
                    best[t[0]] = t
        for t in best.values():
            self._need(eng, t)

    def _mark(self, tok, reads, writes):
        for b in reads:
            for i, t in enumerate(b.r):
                if t[0] == tok[0]:
                    if t[1] < tok[1]:
                        b.r[i] = tok
                    break
            else:
                b.r.append(tok)
        for b in writes:
            b.w = tok
            b.r = []

    def op(self, eng, fn, reads=(), writes=()):
        self._deps(eng, reads, writes)
        self.cnt[eng] += 1
        self.ninst += 1
        semkey = ("c", eng)
        self.sym[eng].append(("i", semkey, 1))
        self.ops[eng].append(lambda h: fn(h).then_inc(self.semh[semkey], 1))
        self._mark((semkey, self.cnt[eng], eng), reads, writes)

    def dma(self, eng, out, in_, reads=(), writes=(), **kw):
        j = self.dma_rr[eng]
        self.dma_rr[eng] = (j + 1) % self.n_dma_sems
        semkey = ("d", eng, j)
        m = self.dma_cnt[semkey]
        if m > 0:
            self._need(eng, (semkey, 16 * m, "dma"))
        self._deps(eng, reads, writes)
        self.dma_cnt[semkey] = m + 1
        self.ninst += 1
        self.sym[eng].append(("i", semkey, 16))
        self.ops[eng].append(lambda h: h.dma_start(out=out, in_=in_, **kw).then_inc(self.semh[semkey], 16))
        tok = (semkey, 16 * (m + 1), "dma")
        self._mark(tok, reads, writes)
        return tok

    def mark(self, name):
        self.marks.append((name, dict(self.cnt)))

    def barrier(self):
        for e in ("pe", "act", "dve", "pool"):
            for o in ("pe", "act", "dve", "pool"):
                if o != e and self.cnt[o] > 0:
                    semkey = ("c", o)
                    val = self.cnt[o]
                    if self.waited[e].get(semkey, 0) < val:
                        self.waited[e][semkey] = val
                        self.sym[e].append(("w", semkey, val))
                        self.ops[e].append(lambda h, semkey=semkey, val=val: h.wait_ge(self.semh[semkey], val))

    def wait_all(self, eng):
        for o in ("pe", "act", "dve", "pool"):
            if o != eng and self.cnt[o] > 0:
                semkey = ("c", o)
                val = self.cnt[o]
                if self.waited[eng].get(semkey, 0) < val:
                    self.waited[eng][semkey] = val
                    self.sym[eng].append(("w", semkey, val))
                    self.ops[eng].append(lambda h, semkey=semkey, val=val: h.wait_ge(self.semh[semkey], val))

    def check_deadlock(self):
        sem = {}
        pc = {e: 0 for e in self.ENGS}
        prog = True
        while prog:
            prog = False
            for e in self.ENGS:
                ops = self.sym[e]
                while pc[e] < len(ops):
                    k, key, v = ops[pc[e]]
                    if k == "w":
                        if sem.get(key, 0) < v:
                            break
                    else:
                        sem[key] = sem.get(key, 0) + v
                    pc[e] += 1
                    prog = True
        stuck = {e: (pc[e], len(self.sym[e]), self.sym[e][pc[e]] if pc[e] < len(self.sym[e]) else None) for e in self.ENGS}
        return all(pc[e] == len(self.sym[e]) for e in self.ENGS), stuck, sem

    def finish(self, eng, bufs):
        for b in bufs:
            self._need(eng, b.w)

    def build(self, st):
        nc = self.nc
        for k in self.sem_names:
            self.semh[k] = st.enter_context(nc.semaphore("s_" + "_".join(str(x) for x in k)))
        blk = st.enter_context(nc.Block())

        def mk(e):
            def f(h):
                for o in self.ops[e]:
                    o(h)
            return f
        blk.tensor(mk("pe"))
        blk.scalar(mk("act"))
        blk.vector(mk("dve"))
        blk.gpsimd(mk("pool"))
        blk.sync(mk("sp"))


SES = True
CE = "dve"
WQ = "pool"
XQ = "sp"


def build_program(NTOK, NL, keys0=None):
    assert NTOK % T == 0
    NTILE = NTOK // T
    nc = bass.Bass("TRN2", target_bir_lowering=False)
    dr = {}

    def dram(name, shape, kind="ExternalInput", dt=F32):
        dr[name] = nc.dram_tensor(name, list(shape), dt, kind=kind).ap()
        return dr[name]
    x_in = dram("x", [NTOK, D])
    w_in = dram("w_in", [L_ALL, D, NIN])
    w_br = [dram("w_br_hg", [L_ALL, D, D]), dram("w_br_ml", [L_ALL, D, D]), dram("w_br_mb", [L_ALL, D, D])]
    w_out = dram("w_out", [L_ALL, D, D])
    w_up = dram("w_up", [L_ALL, D, 2 * DFF])
    w_down = dram("w_down", [L_ALL, DFF, D])
    pvec = dram("pvec", [L_ALL * 3 * 128, 128])
    bvec = dram("bvec", [1, L_ALL * 32])
    gbv = dram("gbv", [4, L_ALL * 2])
    consts = dram("consts", [128, NCONST])
    out_d = dram("out", [NTOK, D], kind="ExternalOutput")
    WSRC = {"w_in": w_in, "w_br0": w_br[0], "w_br1": w_br[1], "w_br2": w_br[2], "w_out": w_out, "w_up": w_up,
            "w_down": w_down}
    WDST = {k: dram("bf_" + k, list(v.shape), kind="Internal", dt=BF16) for k, v in WSRC.items()}
    WBUF = {}
    rec_keys = []
    xscr = dram("xscr", [8, 128, NTOK], kind="Internal")

    P = Prog(nc, same_engine_sync=SES)
    st = contextlib.ExitStack()

    def sb(name, shape, dt=F32):
        return st.enter_context(nc.sbuf_tensor(name, list(shape), dt))

    cst = sb("cst", [128, NCONST])
    identb = sb("identb", [128, 128], BF16)
    ones_bf = sb("ones_bf", [128, 128], BF16)
    ptab = sb("ptab", [128, L_ALL, NROW])
    bvt = sb("bvt", [128, L_ALL, 32])
    aneg = sb("aneg", [128, L_ALL, 16])
    gbt = sb("gbt", [128, L_ALL, 2])
    lbt = sb("lbt", [128, L_ALL, 8])
    omlt = sb("omlt", [128, L_ALL, 8])
    nomlt = sb("nomlt", [128, L_ALL, 8])
    xf = sb("xf", [128, 8, T])
    mixed = sb("mixed", [128, 8, T])
    hT = sb("hT", [128, 8, T], BF16)
    NWB = 5
    wring = sb("wring", [128, NWB, 8, 512], BF16)
    S_hg = sb("S_hg", [128, 8, 128])
    Sb_hg = [sb(f"Sb_hg{i}", [128, 8, 128], BF16) for i in range(2)]
    C_ml = sb("C_ml", [128, 4, 256])
    Cb_ml = [sb(f"Cb_ml{i}", [128, 4, 256], BF16) for i in range(2)]
    n_ml = sb("n_ml", [128, 4])
    nrep = [sb(f"nrep{i}", [128, 4, 128], BF16) for i in range(2)]
    St_mb = sb("St_mb", [128, 16, 64])
    Stb_mb = sb("Stb_mb", [128, 16, 64], BF16)
    tail_ml = sb("tail_ml", [128, 8, 3])
    tail_mb = sb("tail_mb", [128, 16, 3])
    tail_ff = sb("tail_ff", [128, 44, 2])
    AW = 17408
    arena = sb("arena", [128, AW])
    psum_t = [st.enter_context(nc.psum_tensor(f"ps{i}", [128, 1024], F32)) for i in range(4)]

    ident = cst[:, C_ID:C_ID + 128]
    mask2 = cst[:, C_M2:C_M2 + 128]
    maskc = cst[0:64, C_M2:C_M2 + 64]
    SLm = cst[0:64, C_SL:C_SL + 64]
    gseg = cst[:, C_GS:C_GS + 512]
    sel = cst[0:4, C_SEL:C_SEL + 512]
    ones_f = cst[:, C_ONE:C_ONE + 128]

    B_cst, B_par = Buf("cst"), Buf("par")
    B_xf, B_mixed, B_hT = Buf("xf"), Buf("mixed"), Buf("hT")
    WB = [Buf(f"wb{i}") for i in range(NWB)]
    PB = [Buf(f"pb{i}") for i in range(8)]
    B_S, B_Sb = Buf("S"), [Buf("Sb0"), Buf("Sb1")]
    B_C, B_Cb, B_n, B_nrep = Buf("C"), [Buf("Cb0"), Buf("Cb1")], Buf("n"), [Buf("nr0"), Buf("nr1")]
    B_St, B_Stb = Buf("St"), Buf("Stb")
    B_tml, B_tmb, B_tff = [Buf("tml0"), Buf("tml1")], [Buf("tmb0"), Buf("tmb1")], [Buf("tff0"), Buf("tff1")]
    B_xs = [Buf(f"xs{j}") for j in range(NTILE)]
    B_out = [Buf(f"out{j}") for j in range(NTILE)]

    class Arena:
        def __init__(self):
            self.off = 0

        def reset(self):
            self.off = 0

        def alloc(self, shape, dt=F32, parts=128):
            n = int(np.prod(shape[1:]))
            esz = 4 if dt == F32 else 2
            nbytes = (n * esz + 63) // 64 * 64
            o = self.off
            self.off += nbytes
            assert self.off <= AW * 4, ("arena overflow", self.off)
            if dt == F32:
                v = arena[0:shape[0], o // 4:o // 4 + n]
            else:
                v = arena[0:shape[0], o // 4:o // 4 + (n * 2 + 3) // 4].bitcast(BF16)[:, 0:n]
            if len(shape) == 3:
                v = v.rearrange("p (a b) -> p a b", a=shape[1])
            elif len(shape) == 4:
                v = v.rearrange("p (a b c) -> p a b c", a=shape[1], b=shape[2])
            return v
    A = Arena()

    class PSA:
        def __init__(self):
            self.i = 0
            self.excl = set()

        def get(self, n=1, hold=False):
            for _ in range(32):
                if n == 2 and self.i % 2:
                    self.i += 1
                b = self.i % 8
                if all(((b + k) % 8) not in self.excl for k in range(n)) and b + n <= 8:
                    self.i += n
                    if hold:
                        for k in range(n):
                            self.excl.add(b + k)
                    return b
                self.i += 1
            raise RuntimeError("psum exhausted")

        def release(self, b, n=1):
            for k in range(n):
                self.excl.discard(b + k)
    PS = PSA()

    def pb(b):
        return psum_t[b // 2][:, (b % 2) * 512:(b % 2) * 512 + 512]

    def pbb(b):
        return psum_t[b // 2].bitcast(BF16)[:, (b % 2) * 1024:(b % 2) * 1024 + 1024]

    def pd(b):
        assert b % 2 == 0
        return psum_t[b // 2][:, 0:1024]

    def mm(out, lhsT, rhs, start, stop, R, W):
        P.op("pe", lambda h: h.matmul(out=out, lhsT=lhsT, rhs=rhs, start=start, stop=stop), R, W)

    def tr(out, in_, idn, R, W):
        P.op("pe", lambda h: h.transpose(out=out, in_=in_, identity=idn), R, W)

    def act(out, in_, func, R, W, scale=None, bias=None):
        kw = {}
        if scale is not None:
            kw["scale"] = scale
        if bias is not None:
            kw["bias"] = bias
        P.op("act", lambda h: h.activation(out=out, in_=in_, func=func, **kw), R, W)

    def acopy(out, in_, R, W):
        P.op("act", lambda h: h.copy(out=out, in_=in_), R, W)

    def tt(out, a, b, op, R, W, eng="dve"):
        P.op(eng, lambda h: h.tensor_tensor(out=out, in0=a, in1=b, op=op), R, W)

    def ts(out, a, s1, s2, op0, op1, R, W, eng="dve"):
        if s2 is None:
            P.op(eng, lambda h: h.tensor_scalar(out=out, in0=a, scalar1=s1, scalar2=None, op0=op0), R, W)
        else:
            P.op(eng, lambda h: h.tensor_scalar(out=out, in0=a, scalar1=s1, scalar2=s2, op0=op0, op1=op1), R, W)

    def stt(out, a, s, b, op0, op1, R, W, eng="dve"):
        P.op(eng, lambda h: h.scalar_tensor_tensor(out=out, in0=a, scalar=s, in1=b, op0=op0, op1=op1), R, W)

    def vcopy(out, in_, R, W, eng="dve"):
        P.op(eng, lambda h: h.tensor_copy(out=out, in_=in_), R, W)

    def recip(out, in_, R, W):
        P.op("dve", lambda h: h.reciprocal(out=out, in_=in_), R, W)

    def scan(out, d0, d1, R, W):
        P.op("dve", lambda h: h.tensor_tensor_scan(out=out, data0=d0, data1=d1, initial=0.0,
                                                    op0=ALU.mult, op1=ALU.add), R, W)

    def memset(ap, val, W, eng="dve"):
        P.op(eng, lambda h: h.memset(ap, val), (), W)

    def rstd_from_ss(rs, ss_psum, n, R, W):
        act(rs, ss_psum, AF.Ln, list(R) + [B_cst], W, scale=1.0 / n, bias=cst[:, C_EPS:C_EPS + 1])
        act(rs, rs, AF.Exp, W, W, scale=-0.5)

    wslot = [0]

    def wview(wt, l, c0, n, k0, nk):
        v = wt[l].rearrange("(k p) n -> p k n", p=128)
        if nk is None:
            nk = v.shape[1] - k0
        return v[:, k0:k0 + nk, c0:c0 + n]

    def convert_w(key):
        name, l, c0, n, k0, nk = key
        b = Buf("w" + str(key))
        WBUF[key] = b
        P.dma("pool", wview(WDST[name], l, c0, n, k0, nk), wview(WSRC[name], l, c0, n, k0, nk), writes=[b])

    def loadw(key):
        name, l, c0, n, k0, nk = key
        if key not in WBUF:
            convert_w(key)
        if l == 0:
            rec_keys.append(key)
        s = wslot[0]
        wslot[0] = (s + 1) % NWB
        src = wview(WDST[name], l, c0, n, k0, nk)
        dst = wring[:, s, 0:src.shape[1], 0:n]
        P.dma(WQ, dst, src, reads=[WBUF[key]], writes=[WB[s]])
        nxt = (name, l + 1, c0, n, k0, nk)
        if l + 1 < NL and nxt not in WBUF:
            convert_w(nxt)
        return dst, WB[s]

    def wsrc(name, l, c0, n, k0=0, nk=None):
        return (name, l, c0, n, k0, nk)

    P.dma("sp", cst[:], consts, writes=[B_cst])
    vcopy(identb[:], ident, [B_cst], [B_cst])
    vcopy(ones_bf[:], ones_f, [B_cst], [B_cst])
    A.reset()
    praw = A.alloc([128, L_ALL * 3, 128])
    B_praw = Buf("praw")
    P.dma("sp", praw, pvec.rearrange("(n p) c -> p n c", p=128), writes=[B_praw])
    for l in range(L_ALL):
        for i in range(3):
            b = PS.get()
            tr(pb(b)[:, 0:128], praw[:, l * 3 + i, :], ident, [B_praw, B_cst], [PB[b]])
            acopy(ptab[:, l, i * 128:(i + 1) * 128], pb(b)[:, 0:128], [PB[b]], [B_par])
    P.dma("sp", bvt[:].rearrange("p l c -> p (l c)"), bvec.partition_broadcast(128).rearrange("p o c -> p (o c)"),
          writes=[B_par])
    P.dma("sp", gbt[0:4].rearrange("p l c -> p (l c)"), gbv, writes=[B_par])
    act(aneg[:], bvt[:, :, 16:32], AF.Exp, [B_par], [B_par])
    ts(aneg[:], aneg[:], -1.0, None, ALU.mult, None, [B_par], [B_par])
    lbe = A.alloc([128, L_ALL, 8])
    lsum = A.alloc([128, 8])
    B_lb = Buf("lb")
    for l in range(L_ALL):
        act(lbe[:, l, :], ptab[:, l, R_LB:R_LB + 8], AF.Exp, [B_par], [B_lb])
    tt(lsum, lbe[:, 0, :], lbe[:, 1, :], ALU.add, [B_lb], [B_lb])
    tt(lsum, lsum, lbe[:, 2, :], ALU.add, [B_lb], [B_lb])
    tt(lsum, lsum, lbe[:, 3, :], ALU.add, [B_lb], [B_lb])
    recip(lsum, lsum, [B_lb], [B_lb])
    for l in range(L_ALL):
        tt(lbe[:, l, :], lbe[:, l, :], lsum, ALU.mult, [B_lb], [B_lb])
    memset(lbt[:, 0, :], 0.0, [B_par])
    for l in range(1, L_ALL):
        tt(lbt[:, l, :], lbt[:, l - 1, :], lbe[:, l, :], ALU.add, [B_lb, B_par], [B_par])
    ts(omlt[:], lbt[:], -1.0, 1.0, ALU.mult, ALU.add, [B_par], [B_par])
    ts(nomlt[:], lbt[:], 1.0, -1.0, ALU.mult, ALU.add, [B_par], [B_par])
    P.barrier()

    if keys0 is not None:
        for (name, _, c0, n, k0, nk) in keys0:
            convert_w((name, 0, c0, n, k0, nk))

    def prow(l, r):
        return ptab[:, l, r:r + 1]

    def norm_to_hT(l, row0):
        A.reset()
        sq = A.alloc([128, 8, T], BF16)
        rs = A.alloc([128, T])
        Bq, Br = Buf(), Buf()
        act(sq, xf[:], AF.Square, [B_xf], [Bq])
        b = PS.get()
        for kc in range(8):
            mm(pb(b), ones_bf[:], sq[:, kc, :], kc == 0, kc == 7, [Bq, B_cst], [PB[b]])
        rstd_from_ss(rs, pb(b), D, [PB[b]], [Br])
        for kc in range(8):
            stt(hT[:, kc, :], xf[:, kc, :], prow(l, row0 + kc), rs, ALU.mult, ALU.mult, [B_xf, Br, B_par], [B_hT])
        P.barrier()

    def branch_project(l, br, ybr, B_y):
        gt = A.alloc([128, T])
        tm = A.alloc([128, T])
        Bg, Bt = Buf(), Buf()
        for half in range(2):
            wb_, Bw = loadw(wsrc("w_br%d" % br, l, half * 512, 512))
            wg_, Bwg = loadw(wsrc("w_in", l, O_GATE + br * 1024 + half * 512, 512))
            for dd in range(4):
                dc = half * 4 + dd
                bg = PS.get()
                for kc in range(8):
                    mm(pb(bg), wg_[:, kc, dd * 128:(dd + 1) * 128], hT[:, kc, :], kc == 0, kc == 7, [Bwg, B_hT], [PB[bg]])
                act(gt, pb(bg), AF.Sigmoid, [PB[bg]], [Bg])
                bp = PS.get()
                for kc in range(8):
                    mm(pb(bp), wb_[:, kc, dd * 128:(dd + 1) * 128], ybr[:, kc, :], kc == 0, kc == 7, [Bw, B_y], [PB[bp]])
                if br == 0:
                    tt(mixed[:, dc, :], pb(bp), gt, ALU.mult, [PB[bp], Bg], [B_mixed])
                else:
                    tt(tm, pb(bp), gt, ALU.mult, [PB[bp], Bg], [Bt])
                    tt(mixed[:, dc, :], mixed[:, dc, :], tm, ALU.add, [Bt, B_mixed], [B_mixed])

    def tm_proj(l, c0, dst, B_dst):
        for cb in range(2):
            wv, Bw = loadw(wsrc("w_in", l, c0 + cb * 512, 512))
            for pp in range(4):
                b = PS.get()
                for kc in range(8):
                    mm(pb(b), hT[:, kc, pp * 128:(pp + 1) * 128], wv[:, kc, :], kc == 0, kc == 7, [Bw, B_hT], [PB[b]])
                acopy(dst[:, pp, cb * 512:(cb + 1) * 512], pb(b), [PB[b]], [B_dst])

    def fm_proj_act(l, c0, nchunk, func, dst, B_dst):
        for c in range(nchunk):
            if c % 4 == 0:
                n = min(4, nchunk - c) * 128
                wq, Bw = loadw(wsrc("w_in", l, c0 + c * 128, n))
            b = PS.get()
            for kc in range(8):
                mm(pb(b), wq[:, kc, (c % 4) * 128:(c % 4 + 1) * 128], hT[:, kc, :], kc == 0, kc == 7, [Bw, B_hT], [PB[b]])
            act(dst[:, c, :], pb(b), func, [PB[b]], [B_dst])

    def conv_fm(l, c0, nchunk, K, tail, B_tail, rw, rb, wstride, outs, first_tile):
        ub = [A.alloc([128, T + K - 1]) for _ in range(2)]
        Bu = [Buf(), Buf()]
        ca = [A.alloc([128, T]) for _ in range(2)]
        Bc = [Buf(), Buf()]
        wcur = [None]

        def front(c):
            if c % 4 == 0:
                n = min(4, nchunk - c) * 128
                wcur[0] = loadw(wsrc("w_in", l, c0 + c * 128, n))
            wq, Bw = wcur[0]
            b = PS.get()
            for kc in range(8):
                mm(pb(b), wq[:, kc, (c % 4) * 128:(c % 4 + 1) * 128], hT[:, kc, :], kc == 0, kc == 7, [Bw, B_hT], [PB[b]])
            u, bu, cc, bc = ub[c % 2], Bu[c % 2], ca[c % 2], Bc[c % 2]
            bt = B_tail[c % 2]
            if first_tile:
                memset(u[:, 0:K - 1], 0.0, [bu], eng=CE)
            else:
                vcopy(u[:, 0:K - 1], tail[:, c, :], [bt], [bu], eng=CE)
            acopy(u[:, K - 1:K - 1 + T], pb(b), [PB[b]], [bu])
            vcopy(tail[:, c, :], u[:, T:T + K - 1], [bu], [bt], eng=CE)
            ts(cc, u[:, 0:T], prow(l, rw + c), prow(l, rb + c), ALU.mult, ALU.add, [bu, B_par], [bc], eng=CE)
            for k in range(1, K):
                stt(cc, u[:, k:k + T], prow(l, rw + k * wstride + c), cc, ALU.mult, ALU.add, [bu, bc, B_par], [bc])

        def back(c):
            cc, bc = ca[c % 2], Bc[c % 2]
            dst, bd, post = outs(c)
            if post is None:
                act(dst, cc, AF.Silu, [bc], [bd])
            else:
                act(cc, cc, AF.Silu, [bc], [bc])
                ts(dst, cc, post, None, ALU.mult, None, [bc], [bd])
        for c in range(nchunk):
            front(c)
            if c >= 1:
                back(c - 1)
        back(nchunk - 1)

    def phase_hg(l, j):
        A.reset()
        qt = A.alloc([128, 8, T], BF16)
        kt = A.alloc([128, 8, T], BF16)
        vtm = A.alloc([128, 4, 1024], BF16)
        sg = A.alloc([128, 8, T], BF16)
        ybr = A.alloc([128, 8, T], BF16)
        ebl = A.alloc([128, 8, 8])
        mark = A.off
        t_qs, t_sig, t_lf, t_b, t_eb, t_enb, t_k = [A.alloc([128, T]) for _ in range(7)]
        Bqt, Bkt, Bv, Bsg, By, Bebl = Buf(), Buf(), Buf(), Buf(), Buf(), Buf()
        Bt = [Buf() for _ in range(7)]
        if j == 0:
            memset(S_hg[:], 0.0, [B_S])
            memset(Sb_hg[0][:], 0.0, [B_Sb[0]])
        for hb in range(2):
            wq, Bwq = loadw(wsrc("w_in", l, O_HGQ + hb * 512, 512))
            wf, Bwf = loadw(wsrc("w_in", l, O_HGF + hb * 512, 512))
            for hh in range(4):
                h = hb * 4 + hh
                b = PS.get()
                for kc in range(8):
                    mm(pb(b), wq[:, kc, hh * 128:(hh + 1) * 128], hT[:, kc, :], kc == 0, kc == 7, [Bwq, B_hT], [PB[b]])
                act(t_qs, pb(b), AF.Silu, [PB[b]], [Bt[0]])
                b2 = PS.get()
                for kc in range(8):
                    mm(pb(b2), wf[:, kc, hh * 128:(hh + 1) * 128], hT[:, kc, :], kc == 0, kc == 7, [Bwf, B_hT], [PB[b2]])
                act(t_sig, pb(b2), AF.Sigmoid, [PB[b2]], [Bt[1]])
                act(t_lf, t_sig, AF.Ln, [Bt[1], B_par], [Bt[2]], scale=omlt[:, l, h:h + 1], bias=lbt[:, l, h:h + 1])
                scan(t_b, gseg, t_lf, [Bt[2], B_cst], [Bt[3]])
                act(t_eb, t_b, AF.Exp, [Bt[3]], [Bt[4]])
                act(t_enb, t_b, AF.Exp, [Bt[3]], [Bt[5]], scale=-1.0)
                ts(t_k, t_sig, nomlt[:, l, h:h + 1], omlt[:, l, h:h + 1], ALU.mult, ALU.add, [Bt[1], B_par], [Bt[6]])
                tt(qt[:, h, :], t_qs, t_eb, ALU.mult, [Bt[0], Bt[4]], [Bqt])
                tt(kt[:, h, :], t_k, t_enb, ALU.mult, [Bt[6], Bt[5]], [Bkt])
                vcopy(ebl[:, :, h], t_eb.rearrange("p (c j) -> p c j", j=64)[:, :, 63], [Bt[4]], [Bebl])
        tm_proj(l, O_HGI, vtm, Bv)
        fm_proj_act(l, O_HGG, 8, AF.Silu, sg, Bsg)
        P.barrier()
        A.off = mark
        at = A.alloc([128, 8, 128], BF16)
        ktm = A.alloc([128, 8, 128], BF16)
        tmpS = A.alloc([128, 8, 128])
        sq = A.alloc([128, 8, 128], BF16)
        rs = A.alloc([128, 8, 128])
        yn = A.alloc([128, 8, 128])
        Bat, Bktm, BtS, Bsq, Brs, Byn = Buf(), Buf(), Buf(), Buf(), Buf(), Buf()
        cur = 0
        for pp in range(4):
            c0 = pp * 128
            bo = PS.get(2, hold=True)
            bd = PS.get(2, hold=True)
            for h in range(8):
                ba = PS.get()
                mm(pb(ba)[:, 0:128], kt[:, h, c0:c0 + 128], qt[:, h, c0:c0 + 128], True, True, [Bkt, Bqt], [PB[ba]])
                tt(at[:, h, :], pb(ba)[:, 0:128], mask2, ALU.mult, [PB[ba], B_cst], [Bat])
                bt_ = PS.get()
                tr(pbb(bt_)[:, 0:128], kt[:, h, c0:c0 + 128], identb[:], [Bkt, B_cst], [PB[bt_]])
                acopy(ktm[:, h, :], pbb(bt_)[:, 0:128], [PB[bt_]], [Bktm])
            for c in range(2):
                r0 = c * 64
                ch = pp * 2 + c
                for h in range(8):
                    mm(pd(bd)[:, h * 128:(h + 1) * 128], ktm[r0:r0 + 64, h, :], vtm[r0:r0 + 64, pp, h * 128:(h + 1) * 128],
                       True, True, [Bktm, Bv], [PB[bd], PB[bd + 1]])
                for h in range(8):
                    o = pd(bo)[:, h * 128 + r0:h * 128 + r0 + 64]
                    mm(o, vtm[r0:r0 + 64, pp, h * 128:(h + 1) * 128], at[r0:r0 + 64, h, r0:r0 + 64], True, False,
                       [Bv, Bat], [PB[bo], PB[bo + 1]])
                    mm(o, Sb_hg[cur][:, h, :], qt[:, h, c0 + r0:c0 + r0 + 64], False, True,
                       [B_Sb[cur], Bqt], [PB[bo], PB[bo + 1]])
                tt(tmpS, pd(bd).rearrange("p (h v) -> p h v", h=8), S_hg[:], ALU.add, [PB[bd], PB[bd + 1], B_S], [BtS])
                tt(S_hg[:], tmpS, ebl[:, ch, :].unsqueeze(2).broadcast_to([128, 8, 128]), ALU.mult, [BtS, Bebl], [B_S])
                acopy(Sb_hg[1 - cur][:], S_hg[:], [B_S], [B_Sb[1 - cur]])
                cur = 1 - cur
            PS.release(bd, 2)
            o3 = pd(bo)
            sqf = sq.rearrange("p h t -> p (h t)")
            act(sqf, o3, AF.Square, [PB[bo], PB[bo + 1]], [Bsq])
            bs = PS.get(2)
            for hf in range(2):
                mm(pd(bs)[:, hf * 512:(hf + 1) * 512], ones_bf[:], sqf[:, hf * 512:(hf + 1) * 512], True, True,
                   [Bsq, B_cst], [PB[bs], PB[bs + 1]])
            rstd_from_ss(rs.rearrange("p h t -> p (h t)"), pd(bs), 128, [PB[bs], PB[bs + 1]], [Brs])
            tt(yn.rearrange("p h t -> p (h t)"), o3, rs.rearrange("p h t -> p (h t)"), ALU.mult, [PB[bo], PB[bo + 1], Brs], [Byn])
            stt(ybr[:, :, c0:c0 + 128], yn, prow(l, R_HGG), sg[:, :, c0:c0 + 128], ALU.mult, ALU.mult, [Byn, Bsg, B_par], [By])
            PS.release(bo, 2)
        assert cur == 0
        branch_project(l, 0, ybr, By)
        P.barrier()

    def phase_ml(l, j):
        A.reset()
        qm = A.alloc([128, 4, T], BF16)
        km = A.alloc([128, 4, T], BF16)
        vtm = A.alloc([128, 4, 1024], BF16)
        so = A.alloc([128, 8, T], BF16)
        ybr = A.alloc([128, 8, T], BF16)
        g_lf, g_bh, g_g, g_e, g_ei = [A.alloc([4, T]) for _ in range(5)]
        elbc = A.alloc([128, 4, 8])
        Bqm, Bkm, Bv, Bso, By = Buf(), Buf(), Buf(), Buf(), Buf()
        Bg = [Buf() for _ in range(5)]
        Bel = Buf()
        mark = A.off
        if j == 0:
            memset(C_ml[:], 0.0, [B_C])
            memset(Cb_ml[0][:], 0.0, [B_Cb[0]])
            memset(n_ml[:], 0.0, [B_n])
            memset(nrep[0][:], 0.0, [B_nrep[0]])
        wif, Bwif = loadw(wsrc("w_in", l, O_MLIF, 8))
        bi_, bf_ = PS.get(), PS.get()
        for kc in range(8):
            mm(pb(bi_)[0:4, :], wif[:, kc, 0:4], hT[:, kc, :], kc == 0, kc == 7, [Bwif, B_hT], [PB[bi_]])
        for kc in range(8):
            mm(pb(bf_)[0:4, :], wif[:, kc, 4:8], hT[:, kc, :], kc == 0, kc == 7, [Bwif, B_hT], [PB[bf_]])
        act(g_lf, pb(bf_)[0:4, :], AF.Sigmoid, [PB[bf_], B_par], [Bg[0]], bias=gbt[0:4, l, 1:2])
        act(g_lf, g_lf, AF.Ln, [Bg[0]], [Bg[0]])
        scan(g_bh, gseg[0:4, :], g_lf, [Bg[0], B_cst], [Bg[1]])
        stt(g_g, pb(bi_)[0:4, :], gbt[0:4, l, 0:1], g_bh, ALU.add, ALU.subtract, [PB[bi_], Bg[1], B_par], [Bg[2]])
        act(g_g, g_g, AF.Exp, [Bg[2]], [Bg[2]])
        act(g_e, g_bh, AF.Exp, [Bg[1]], [Bg[3]])
        act(g_ei, g_bh, AF.Exp, [Bg[1]], [Bg[4]], scale=-1.0)
        be = PS.get()
        gel = g_e.rearrange("p (c j) -> p c j", j=64)[:, :, 63]
        for h in range(4):
            mm(pb(be)[:, h * 8:(h + 1) * 8], sel[:, h * 128:(h + 1) * 128], gel, True, True, [Bg[3], B_cst], [PB[be]])
        acopy(elbc.rearrange("p h c -> p (h c)"), pb(be)[:, 0:32], [PB[be]], [Bel])

        def qk_out(c):
            if c < 4:
                return qm[:, c, :], Bqm, None
            return km[:, c - 4, :], Bkm, 128.0 ** -0.5
        conv_fm(l, O_MLQ, 8, 4, tail_ml, B_tml, R_MLW, R_MLB, 8, qk_out, j == 0)
        tm_proj(l, O_MLV, vtm, Bv)
        fm_proj_act(l, O_MLO, 8, AF.Sigmoid, so, Bso)
        P.barrier()
        A.off = mark
        gtm = A.alloc([128, 4])
        einvbc = A.alloc([128, 4, 128])
        at = A.alloc([128, 4, 128], BF16)
        kgtm = A.alloc([128, 4, 128], BF16)
        tmpS = A.alloc([128, 4, 256])
        tmpn = A.alloc([128, 4])
        dnm = A.alloc([128, 4, 128])
        hh = A.alloc([128, 4, 2, 128])
        sq = A.alloc([128, 8, 128], BF16)
        rs = A.alloc([128, 4, 128])
        Bgtm, Bei, Bat, Bkg, BtS, Btn, Bdn, Bhh, Bsq, Brs = [Buf() for _ in range(10)]
        cur = 0
        for pp in range(4):
            c0 = pp * 128
            bgt = PS.get()
            tr(pb(bgt)[:, 0:4], g_g[0:4, c0:c0 + 128], ident[0:4, 0:4], [Bg[2], B_cst], [PB[bgt]])
            acopy(gtm, pb(bgt)[:, 0:4], [PB[bgt]], [Bgtm])
            bei = PS.get()
            for h in range(4):
                mm(pb(bei)[:, h * 128:(h + 1) * 128], sel[:, h * 128:(h + 1) * 128], g_ei[0:4, c0:c0 + 128], True, True,
                   [Bg[4], B_cst], [PB[bei]])
            acopy(einvbc.rearrange("p h t -> p (h t)"), pb(bei), [PB[bei]], [Bei])
            bnum = PS.get(2, hold=True)
            bdC = PS.get(2, hold=True)
            bden = PS.get(1, hold=True)
            bdn = PS.get(1, hold=True)
            for h in range(4):
                ba = PS.get()
                mm(pb(ba)[:, 0:128], km[:, h, c0:c0 + 128], qm[:, h, c0:c0 + 128], True, True, [Bkm, Bqm], [PB[ba]])
                stt(at[:, h, :], pb(ba)[:, 0:128], gtm[:, h:h + 1], mask2, ALU.mult, ALU.mult, [PB[ba], Bgtm, B_cst], [Bat])
                bt_ = PS.get()
                tr(pbb(bt_)[:, 0:128], km[:, h, c0:c0 + 128], identb[:], [Bkm, B_cst], [PB[bt_]])
                ts(kgtm[:, h, :], pbb(bt_)[:, 0:128], gtm[:, h:h + 1], None, ALU.mult, None, [PB[bt_], Bgtm], [Bkg])
            for c in range(2):
                r0 = c * 64
                ch = pp * 2 + c
                tc0 = c0 + r0
                for h in range(4):
                    mm(pd(bdC)[:, h * 256:(h + 1) * 256], kgtm[r0:r0 + 64, h, :], vtm[r0:r0 + 64, pp, h * 256:(h + 1) * 256],
                       True, True, [Bkg, Bv], [PB[bdC], PB[bdC + 1]])
                    mm(pb(bdn)[:, h:h + 1], kgtm[r0:r0 + 64, h, :], ones_bf[r0:r0 + 64, 0:1],
                       True, True, [Bkg, B_cst], [PB[bdn]])
                for h in range(4):
                    for dvc in range(2):
                        o = pd(bnum)[:, (h * 2 + dvc) * 128 + r0:(h * 2 + dvc) * 128 + r0 + 64]
                        mm(o, vtm[r0:r0 + 64, pp, h * 256 + dvc * 128:h * 256 + dvc * 128 + 128], at[r0:r0 + 64, h, r0:r0 + 64],
                           True, False, [Bv, Bat], [PB[bnum], PB[bnum + 1]])
                        mm(o, Cb_ml[cur][:, h, dvc * 128:(dvc + 1) * 128], qm[:, h, tc0:tc0 + 64], False, True,
                           [B_Cb[cur], Bqm], [PB[bnum], PB[bnum + 1]])
                    od = pb(bden)[:, h * 128 + r0:h * 128 + r0 + 64]
                    mm(od, ones_bf[r0:r0 + 64, :], at[r0:r0 + 64, h, r0:r0 + 64], True, False, [Bat, B_cst], [PB[bden]])
                    mm(od, nrep[cur][:, h, :], qm[:, h, tc0:tc0 + 64], False, True, [B_nrep[cur], Bqm], [PB[bden]])
                elb = elbc[:, :, ch]
                tt(tmpS, pd(bdC).rearrange("p (h v) -> p h v", h=4), C_ml[:], ALU.add,
                   [PB[bdC], PB[bdC + 1], B_C], [BtS])
                tt(C_ml[:], tmpS, elb.unsqueeze(2).broadcast_to([128, 4, 256]), ALU.mult, [BtS, Bel], [B_C])
                acopy(Cb_ml[1 - cur][:], C_ml[:], [B_C], [B_Cb[1 - cur]])
                tt(tmpn, pb(bdn)[:, 0:4], n_ml[:], ALU.add, [PB[bdn], B_n], [Btn])
                tt(n_ml[:], tmpn, elb, ALU.mult, [Btn, Bel], [B_n])
                vcopy(nrep[1 - cur][:], n_ml[:].unsqueeze(2).broadcast_to([128, 4, 128]), [B_n], [B_nrep[1 - cur]])
                cur = 1 - cur
            PS.release(bdC, 2)
            PS.release(bdn, 1)
            dflat = dnm.rearrange("p h t -> p (h t)")
            act(dflat, pb(bden), AF.Abs, [PB[bden]], [Bdn])
            tt(dflat, dflat, einvbc.rearrange("p h t -> p (h t)"), ALU.max, [Bdn, Bei], [Bdn])
            act(dflat, dflat, AF.Ln, [Bdn], [Bdn])
            act(dflat, dflat, AF.Exp, [Bdn], [Bdn], scale=-1.0)
            tt(hh, pd(bnum).rearrange("p (h d t) -> p h d t", h=4, d=2), dnm.unsqueeze(2).broadcast_to([128, 4, 2, 128]),
               ALU.mult, [PB[bnum], PB[bnum + 1], Bdn], [Bhh])
            PS.release(bnum, 2)
            PS.release(bden, 1)
            hflat = hh.rearrange("p h d t -> p (h d t)")
            act(sq.rearrange("p c t -> p (c t)"), hflat, AF.Square, [Bhh], [Bsq])
            bss = PS.get()
            for h in range(4):
                for dvc in range(2):
                    mm(pb(bss)[:, h * 128:(h + 1) * 128], ones_bf[:], sq[:, h * 2 + dvc, :], dvc == 0, dvc == 1,
                       [Bsq, B_cst], [PB[bss]])
            rstd_from_ss(rs.rearrange("p h t -> p (h t)"), pb(bss), 256, [PB[bss]], [Brs])
            tt(hh, hh, rs.unsqueeze(2).broadcast_to([128, 4, 2, 128]), ALU.mult, [Bhh, Brs], [Bhh])
            gml = ptab[:, l, R_MLG:R_MLG + 2].unsqueeze(1).unsqueeze(3).broadcast_to([128, 4, 2, 128])
            tt(hh, hh, gml, ALU.mult, [Bhh, B_par], [Bhh])
            tt(ybr[:, :, c0:c0 + 128], hh.rearrange("p h d t -> p (h d) t"), so[:, :, c0:c0 + 128], ALU.mult, [Bhh, Bso], [By])
        assert cur == 0
        branch_project(l, 1, ybr, By)
        P.barrier()

    def phase_mb(l, j):
        A.reset()
        sz = A.alloc([128, 8, T], BF16)
        xbc = A.alloc([128, 16, T], BF16)
        ybr = A.alloc([128, 8, T], BF16)
        yfm = A.alloc([128, 8, T], BF16)
        Bsz, Bx, By, Byf = Buf(), Buf(), Buf(), Buf()
        if j == 0:
            memset(St_mb[:], 0.0, [B_St])
            memset(Stb_mb[:], 0.0, [B_Stb])
        fm_proj_act(l, O_MBZ, 8, AF.Silu, sz, Bsz)
        mark = A.off
        conv_fm(l, O_MBX, 16, 4, tail_mb, B_tmb, R_MBW, R_MBB, 16, lambda c: (xbc[:, c, :], Bx, None), j == 0)
        P.barrier()
        A.off = mark
        wdt, Bwdt = loadw(wsrc("w_in", l, O_MBDT, 16))
        dtp, dte, dt_, dtA, cum, dcd, dec, ecum = [A.alloc([64, 16]) for _ in range(8)]
        elast = A.alloc([128, 16])
        R = A.alloc([64, 16, 64])
        LT = A.alloc([64, 16, 64])
        CBm = A.alloc([64, 4, 64])
        WT = A.alloc([64, 16, 64], BF16)
        xdt = A.alloc([64, 16, 64], BF16)
        xdd = A.alloc([64, 16, 64], BF16)
        Btm = A.alloc([64, 512], BF16)
        tmpy = A.alloc([64, 16, 64])
        ytm = A.alloc([64, 1024], BF16)
        tmpS = A.alloc([128, 16, 64])
        Bd = [Buf() for _ in range(8)]
        Bel, BR, BLT, BCB, BWT, Bxdt, Bxdd, BBt, Bty, Bytm, BtS = [Buf() for _ in range(11)]
        for ch in range(8):
            cc = ch * 64
            b = PS.get()
            for kc in range(8):
                mm(pb(b)[0:64, 0:16], hT[:, kc, cc:cc + 64], wdt[:, kc, 0:16], kc == 0, kc == 7, [Bwdt, B_hT], [PB[b]])
            tt(dtp, pb(b)[0:64, 0:16], bvt[0:64, l, 0:16], ALU.add, [PB[b], B_par], [Bd[0]])
            act(dte, dtp, AF.Exp, [Bd[0]], [Bd[1]])
            ts(dte, dte, 1.0, None, ALU.add, None, [Bd[1]], [Bd[1]])
            act(dt_, dte, AF.Ln, [Bd[1]], [Bd[2]])
            tt(dtA, dt_, aneg[0:64, l, :], ALU.mult, [Bd[2], B_par], [Bd[3]])
            tt(R, dtA.unsqueeze(2).broadcast_to([64, 16, 64]), maskc.unsqueeze(1).broadcast_to([64, 16, 64]), ALU.mult,
               [Bd[3], B_cst], [BR])
            Rf = R.rearrange("p h t -> p (h t)")
            bD = PS.get(2)
            for hf in range(2):
                mm(pd(bD)[0:64, hf * 512:(hf + 1) * 512], SLm, Rf[:, hf * 512:(hf + 1) * 512], True, True,
                   [BR, B_cst], [PB[bD], PB[bD + 1]])
            act(LT.rearrange("p h t -> p (h t)"), pd(bD)[0:64, :], AF.Exp, [PB[bD], PB[bD + 1]], [BLT])
            bc_ = PS.get()
            mm(pb(bc_)[0:64, 0:16], maskc, dtA, True, True, [Bd[3], B_cst], [PB[bc_]])
            mm(pb(bc_)[:, 16:32], ones_f[0:64, :], dtA, True, True, [Bd[3], B_cst], [PB[bc_]])
            acopy(cum, pb(bc_)[0:64, 0:16], [PB[bc_]], [Bd[4]])
            act(ecum, pb(bc_)[0:64, 0:16], AF.Exp, [PB[bc_]], [Bd[7]])
            act(elast, pb(bc_)[:, 16:32], AF.Exp, [PB[bc_]], [Bel])
            tt(dcd, pb(bc_)[0:64, 16:32], cum, ALU.subtract, [PB[bc_], Bd[4]], [Bd[5]])
            act(dec, dcd, AF.Exp, [Bd[5]], [Bd[6]])
            bcb = PS.get()
            for g in range(4):
                mm(pb(bcb)[0:64, g * 64:(g + 1) * 64], xbc[:, 8 + g, cc:cc + 64], xbc[:, 12 + g, cc:cc + 64], True, True,
                   [Bx], [PB[bcb]])
            tt(CBm, pb(bcb)[0:64, 0:256].rearrange("p (g t) -> p g t", g=4), maskc.unsqueeze(1).broadcast_to([64, 4, 64]),
               ALU.mult, [PB[bcb], B_cst], [BCB])
            tt(WT.rearrange("p (g k) t -> p g k t", g=4), LT.rearrange("p (g k) t -> p g k t", g=4),
               CBm.unsqueeze(2).broadcast_to([64, 4, 4, 64]), ALU.mult, [BLT, BCB], [BWT])
            bx = PS.get()
            for hc in range(8):
                tr(pbb(bx)[0:64, hc * 128:(hc + 1) * 128], xbc[:, hc, cc:cc + 64], identb[:], [Bx, B_cst], [PB[bx]])
            bB = PS.get()
            for g in range(4):
                tr(pbb(bB)[0:64, g * 128:(g + 1) * 128], xbc[:, 8 + g, cc:cc + 64], identb[:], [Bx, B_cst], [PB[bB]])
            tt(xdt, pbb(bx)[0:64, 0:1024].rearrange("p (h q) -> p h q", h=16), dt_.unsqueeze(2).broadcast_to([64, 16, 64]),
               ALU.mult, [PB[bx], Bd[2]], [Bxdt])
            tt(xdd, xdt, dec.unsqueeze(2).broadcast_to([64, 16, 64]), ALU.mult, [Bxdt, Bd[6]], [Bxdd], eng=CE)
            acopy(Btm, pbb(bB)[0:64, 0:512], [PB[bB]], [BBt])
            by_ = PS.get(2)
            for h in range(16):
                mm(pd(by_)[0:64, h * 64:(h + 1) * 64], WT[:, h, :], xdt[:, h, :], True, True, [BWT, Bxdt], [PB[by_], PB[by_ + 1]])
            byi = PS.get(2)
            Stbf = Stb_mb[:].rearrange("p h q -> p (h q)")
            for g in range(4):
                mm(pd(byi)[0:64, g * 256:(g + 1) * 256], xbc[:, 12 + g, cc:cc + 64], Stbf[:, g * 256:(g + 1) * 256], True, True,
                   [Bx, B_Stb], [PB[byi], PB[byi + 1]])
            tt(tmpy, pd(byi)[0:64, :].rearrange("p (h q) -> p h q", h=16), ecum.unsqueeze(2).broadcast_to([64, 16, 64]),
               ALU.mult, [PB[byi], PB[byi + 1], Bd[7]], [Bty])
            tt(ytm, pd(by_)[0:64, :], tmpy.rearrange("p h q -> p (h q)"), ALU.add, [PB[by_], PB[by_ + 1], Bty], [Bytm])
            byt = PS.get()
            for hc in range(8):
                tr(pbb(byt)[:, hc * 64:(hc + 1) * 64], ytm[0:64, hc * 128:(hc + 1) * 128], identb[0:64, 0:64],
                   [Bytm, B_cst], [PB[byt]])
            acopy(yfm[:, :, cc:cc + 64], pbb(byt)[:, 0:512].rearrange("p (c t) -> p c t", c=8), [PB[byt]], [Byf])
            bs = PS.get(2)
            xddf = xdd.rearrange("p h q -> p (h q)")
            for g in range(4):
                mm(pd(bs)[:, g * 256:(g + 1) * 256], Btm[0:64, g * 128:(g + 1) * 128], xddf[:, g * 256:(g + 1) * 256], True, True,
                   [BBt, Bxdd], [PB[bs], PB[bs + 1]])
            tt(tmpS, St_mb[:], elast.unsqueeze(2).broadcast_to([128, 16, 64]), ALU.mult, [B_St, Bel], [BtS], eng=CE)
            tt(St_mb[:], tmpS, pd(bs).rearrange("p (h q) -> p h q", h=16), ALU.add, [BtS, PB[bs], PB[bs + 1]], [B_St])
            acopy(Stb_mb[:], St_mb[:], [B_St], [B_Stb])
        P.barrier()
        A.off = mark
        t1 = A.alloc([128, 2, T])
        sq = A.alloc([128, 2, T], BF16)
        rs = A.alloc([128, T])
        Bt1, Bsq, Brs = Buf(), Buf(), Buf()
        for g in range(4):
            dv = ptab[:, l, R_MBD + 2 * g:R_MBD + 2 * g + 2].unsqueeze(2).broadcast_to([128, 2, T])
            tt(t1, xbc[:, 2 * g:2 * g + 2, :], dv, ALU.mult, [Bx, B_par], [Bt1])
            tt(t1, t1, yfm[:, 2 * g:2 * g + 2, :], ALU.add, [Bt1, Byf], [Bt1])
            tt(t1, t1, sz[:, 2 * g:2 * g + 2, :], ALU.mult, [Bt1, Bsz], [Bt1])
            act(sq, t1, AF.Square, [Bt1], [Bsq])
            b = PS.get()
            for k in range(2):
                mm(pb(b), ones_bf[:], sq[:, k, :], k == 0, k == 1, [Bsq, B_cst], [PB[b]])
            rstd_from_ss(rs, pb(b), 256, [PB[b]], [Brs])
            tt(t1, t1, rs.unsqueeze(1).broadcast_to([128, 2, T]), ALU.mult, [Bt1, Brs], [Bt1])
            gm = ptab[:, l, R_MBG + 2 * g:R_MBG + 2 * g + 2].unsqueeze(2).broadcast_to([128, 2, T])
            tt(ybr[:, 2 * g:2 * g + 2, :], t1, gm, ALU.mult, [Bt1, B_par], [By])
        branch_project(l, 2, ybr, By)
        P.barrier()

    def phase_out(l):
        A.reset()
        mixb = A.alloc([128, 8, T], BF16)
        Bm = Buf()
        vcopy(mixb, mixed[:], [B_mixed], [Bm])
        for half in range(2):
            wo, Bw = loadw(wsrc("w_out", l, half * 512, 512))
            for dd in range(4):
                dc = half * 4 + dd
                b = PS.get()
                for kc in range(8):
                    mm(pb(b), wo[:, kc, dd * 128:(dd + 1) * 128], mixb[:, kc, :], kc == 0, kc == 7, [Bw, Bm], [PB[b]])
                tt(xf[:, dc, :], xf[:, dc, :], pb(b), ALU.add, [PB[b], B_xf], [B_xf])
        P.barrier()

    def phase_ffn(l, j):
        A.reset()
        prod = A.alloc([128, 22, T], BF16)
        Bp = Buf()
        ub = [A.alloc([128, T + 2]) for _ in range(4)]
        Bu = [Buf() for _ in range(4)]
        cg = [A.alloc([128, T]) for _ in range(2)]
        cv = [A.alloc([128, T]) for _ in range(2)]
        Bcg, Bcv = [Buf(), Buf()], [Buf(), Buf()]
        wcur = [None, None]

        def front(jc):
            if jc % 4 == 0:
                n = min(4, 22 - jc) * 128
                wcur[0] = loadw(wsrc("w_up", l, jc * 128, n))
                wcur[1] = loadw(wsrc("w_up", l, DFF + jc * 128, n))
            for which in range(2):
                wq, Bw = wcur[which]
                c = jc + 22 * which
                b = PS.get()
                for kc in range(8):
                    mm(pb(b), wq[:, kc, (jc % 4) * 128:(jc % 4 + 1) * 128], hT[:, kc, :], kc == 0, kc == 7, [Bw, B_hT], [PB[b]])
                u, bu = ub[(jc % 2) * 2 + which], Bu[(jc % 2) * 2 + which]
                if j == 0:
                    memset(u[:, 0:2], 0.0, [bu], eng=CE)
                else:
                    vcopy(u[:, 0:2], tail_ff[:, c, :], [B_tff[which]], [bu], eng=CE)
                acopy(u[:, 2:2 + T], pb(b), [PB[b]], [bu])
                vcopy(tail_ff[:, c, :], u[:, T:T + 2], [bu], [B_tff[which]], eng=CE)
                cc, bc = (cg[jc % 2], Bcg[jc % 2]) if which == 0 else (cv[jc % 2], Bcv[jc % 2])
                ts(cc, u[:, 0:T], prow(l, R_FFW + c), prow(l, R_FFB + c), ALU.mult, ALU.add, [bu, B_par], [bc], eng=CE)
                for k in range(1, 3):
                    stt(cc, u[:, k:k + T], prow(l, R_FFW + k * 44 + c), cc, ALU.mult, ALU.add, [bu, bc, B_par], [bc])

        def back(jc):
            act(cg[jc % 2], cg[jc % 2], AF.Silu, [Bcg[jc % 2]], [Bcg[jc % 2]])
            tt(prod[:, jc, :], cg[jc % 2], cv[jc % 2], ALU.mult, [Bcg[jc % 2], Bcv[jc % 2]], [Bp])
        for jc in range(22):
            front(jc)
            if jc >= 1:
                back(jc - 1)
        back(21)
        for half in range(2):
            blks = []
            for k0, nk in ((0, 8), (8, 8), (16, 6)):
                blks.append(loadw(wsrc("w_down", l, half * 512, 512, k0, nk)))
            for dd in range(4):
                dc = half * 4 + dd
                b = PS.get()
                for jc in range(22):
                    wq, Bw = blks[jc // 8]
                    mm(pb(b), wq[:, jc % 8, dd * 128:(dd + 1) * 128], prod[:, jc, :], jc == 0, jc == 21, [Bw, Bp], [PB[b]])
                tt(xf[:, dc, :], xf[:, dc, :], pb(b), ALU.add, [PB[b], B_xf], [B_xf])
        P.barrier()

    def load_x(l, j):
        t0 = j * T
        if l == 0:
            A.reset()
            xtm = A.alloc([128, 4, D])
            Bxt = Buf()
            P.wait_all(XQ)
            P.dma(XQ, xtm, x_in[t0:t0 + T, :].rearrange("(g p) d -> p g d", p=128), writes=[Bxt])
            for g in range(4):
                for kc in range(8):
                    b = PS.get()
                    tr(pb(b)[:, 0:128], xtm[:, g, kc * 128:(kc + 1) * 128], ident, [Bxt, B_cst], [PB[b]])
                    acopy(xf[:, kc, g * 128:(g + 1) * 128], pb(b)[:, 0:128], [PB[b]], [B_xf])
            P.barrier()
        else:
            P.dma(XQ, xf[:], xscr[:, :, t0:t0 + T].rearrange("c p t -> p c t"), reads=[B_xs[j]], writes=[B_xf])

    def store_x(l, j):
        t0 = j * T
        if l < NL - 1:
            P.dma(XQ, xscr[:, :, t0:t0 + T].rearrange("c p t -> p c t"), xf[:], reads=[B_xf], writes=[B_xs[j]])
        else:
            A.reset()
            sq = A.alloc([128, 8, T], BF16)
            rs = A.alloc([128, T])
            yo = A.alloc([128, 8, T])
            otm = A.alloc([128, 4, D])
            Bq, Br, Byo, Bot = Buf(), Buf(), Buf(), Buf()
            act(sq, xf[:], AF.Square, [B_xf], [Bq])
            b = PS.get()
            for kc in range(8):
                mm(pb(b), ones_bf[:], sq[:, kc, :], kc == 0, kc == 7, [Bq, B_cst], [PB[b]])
            rstd_from_ss(rs, pb(b), D, [PB[b]], [Br])
            for kc in range(8):
                stt(yo[:, kc, :], xf[:, kc, :], prow(0, R_FIN + kc), rs, ALU.mult, ALU.mult, [B_xf, Br, B_par], [Byo])
            for g in range(4):
                for kc in range(8):
                    b = PS.get()
                    tr(pb(b)[:, 0:128], yo[:, kc, g * 128:(g + 1) * 128], ident, [Byo, B_cst], [PB[b]])
                    acopy(otm[:, g, kc * 128:(kc + 1) * 128], pb(b)[:, 0:128], [PB[b]], [Bot])
            P.dma(XQ, out_d[t0:t0 + T, :].rearrange("(g p) d -> p g d", p=128), otm, reads=[Bot], writes=[B_out[j]])
            P.barrier()
            P.finish("act", [B_out[j]])
            P.finish("dve", [B_out[j]])
            P.finish("pe", [B_out[j]])

    for l in range(NL):
        for j in range(NTILE):
            P.mark("load")
            load_x(l, j)
            P.mark("norm1")
            norm_to_hT(l, R_N1)
            P.mark("hg")
            phase_hg(l, j)
            P.mark("ml")
            phase_ml(l, j)
            P.mark("mb")
            phase_mb(l, j)
            P.mark("out")
            phase_out(l)
            P.mark("norm2")
            norm_to_hT(l, R_N2)
            P.mark("ffn")
            phase_ffn(l, j)
            P.mark("store")
            store_x(l, j)
    P.mark("end")
    P.finish("sp", B_out)
    P.build(st)
    st.close()
    P.rec_keys = [(k[0], 0) + tuple(k[2:]) for k in rec_keys]
    return nc, P


def make_consts():
    c = np.zeros((128, NCONST), np.float32)
    c[:, C_ID:C_ID + 128] = np.eye(128, dtype=np.float32)
    s = np.arange(128)[:, None]
    t = np.arange(128)[None, :]
    c[:, C_M2:C_M2 + 128] = ((s // 64 == t // 64) & (s <= t)).astype(np.float32)
    c[0:64, C_SL:C_SL + 64] = (np.arange(64)[:, None] > np.arange(64)[None, :]).astype(np.float32)
    gs = np.ones(512, np.float32)
    gs[::64] = 0.0
    c[:, C_GS:C_GS + 512] = gs[None, :]
    for h in range(4):
        c[h, C_SEL + h * 128:C_SEL + (h + 1) * 128] = 1.0
    c[:, C_ONE:C_ONE + 128] = 1.0
    c[:, C_EPS] = EPS
    return c


def pack_params(inp):
    pv = np.zeros((L_ALL, NROW, 128), np.float32)
    for l in range(L_ALL):
        pv[l, R_N1:R_N1 + 8] = inp["norm1_g"][l].reshape(8, 128)
        pv[l, R_N2:R_N2 + 8] = inp["norm2_g"][l].reshape(8, 128)
        pv[l, R_LB:R_LB + 8] = inp["hg_lb_logits"][l].reshape(8, 128)
        pv[l, R_MLW:R_MLW + 32] = inp["ml_conv_w"][l].reshape(4 * 8, 128)
        pv[l, R_MLB:R_MLB + 8] = inp["ml_conv_b"][l].reshape(8, 128)
        pv[l, R_MBW:R_MBW + 64] = inp["mb_conv_w"][l].reshape(4 * 16, 128)
        pv[l, R_MBB:R_MBB + 16] = inp["mb_conv_b"][l].reshape(16, 128)
        pv[l, R_FFW:R_FFW + 132] = inp["ffn_conv_w"][l].reshape(3 * 44, 128)
        pv[l, R_FFB:R_FFB + 44] = inp["ffn_conv_b"][l].reshape(44, 128)
        pv[l, R_MBG:R_MBG + 8] = inp["mb_norm_g"][l].reshape(8, 128)
        pv[l, R_HGG] = inp["hg_norm_g"][l]
        pv[l, R_MLG:R_MLG + 2] = inp["ml_norm_g"][l].reshape(2, 128)
        pv[l, R_MBD:R_MBD + 8] = np.repeat(inp["mb_d"][l], 64).reshape(8, 128)
        pv[l, R_FIN:R_FIN + 8] = inp["final_g"].reshape(8, 128)
    bv = np.concatenate([inp["mb_dt_bias"], inp["mb_a_log"]], axis=1).reshape(1, L_ALL * 32)
    gb = inp["ml_gate_b"].reshape(L_ALL, 2, 4).transpose(2, 0, 1).reshape(4, L_ALL * 2)
    return (np.ascontiguousarray(pv.reshape(L_ALL * 3 * 128, 128)), np.ascontiguousarray(bv.astype(np.float32)),
            np.ascontiguousarray(gb.astype(np.float32)))


_CACHE = {}


def run_cores(inp, xs, NL):
    NTOK = xs[0].shape[0]
    key = (NTOK, NL)
    if key not in _CACHE:
        if "keys0" not in _CACHE:
            _, P0 = build_program(T, 1)
            seen, ks = set(), []
            for k in P0.rec_keys:
                if k not in seen:
                    seen.add(k)
                    ks.append(k)
            _CACHE["keys0"] = ks
        _CACHE[key] = build_program(NTOK, NL, _CACHE["keys0"])[0]
    nc = _CACHE[key]
    pv, bv, gb = pack_params(inp)
    cst = make_consts()
    shared = {"w_in": inp["w_in"], "w_br_hg": inp["w_br_hg"], "w_br_ml": inp["w_br_ml"], "w_br_mb": inp["w_br_mb"],
              "w_out": inp["w_out"], "w_up": inp["w_up"], "w_down": inp["w_down"], "pvec": pv, "bvec": bv, "gbv": gb,
              "consts": cst}
    shared = {k: np.ascontiguousarray(np.asarray(v, dtype=np.float32)) for k, v in shared.items()}
    in_maps = [dict(shared, x=np.ascontiguousarray(x)) for x in xs]
    res = run_bass_kernel_spmd(nc, in_maps, core_ids=list(range(len(xs))))
    return [r["out"] for r in res.results]


def kernel(**inputs):
    inp = {k: np.asarray(v) for k, v in inputs.items()}
    x = inp["x"].astype(np.float32)
    Bn, S, _ = x.shape
    xs = [x[b] for b in range(Bn)]
    outs = run_cores(inp, xs, L_ALL)
    return np.stack(outs[:Bn], axis=0).astype(np.float32)
```

```python
import contextlib
import numpy as np
import concourse.bass as bass
import concourse.mybir as mybir
from concourse.bass_utils import run_bass_kernel_spmd

F32 = mybir.dt.float32
BF16 = mybir.dt.bfloat16
AF = mybir.ActivationFunctionType
ALU = mybir.AluOpType

L_ALL = 4
D = 1024
NIN = 13336
DFF = 2816
T = 512
EPS = 1e-6
O_HGQ, O_HGF, O_HGI, O_HGG = 0, 1024, 2048, 3072
O_MLQ, O_MLK, O_MLV, O_MLIF, O_MLO = 4096, 4608, 5120, 6144, 6152
O_MBZ, O_MBX, O_MBDT, O_GATE = 7176, 8200, 10248, 10264
R_N1, R_N2, R_LB, R_MLW, R_MLB, R_MBW, R_MBB, R_FFW, R_FFB, R_MBG, R_HGG, R_MLG, R_MBD, R_FIN = (
    0, 8, 16, 24, 56, 64, 128, 144, 276, 320, 328, 329, 331, 339)
NROW = 384
C_ID, C_M2, C_SL, C_GS, C_SEL, C_ONE, C_EPS, NCONST = 0, 128, 256, 320, 832, 1344, 1472, 1473


class Buf:
    __slots__ = ("name", "w", "r")

    def __init__(self, name=""):
        self.name = name
        self.w = None
        self.r = []


class Prog:
    ENGS = ("pe", "act", "dve", "pool", "sp")

    def __init__(self, nc, n_dma_sems=8, same_engine_sync=True):
        self.nc = nc
        self.ops = {e: [] for e in self.ENGS}
        self.cnt = {e: 0 for e in self.ENGS}
        self.waited = {e: {} for e in self.ENGS}
        self.same_engine_sync = same_engine_sync
        self.n_dma_sems = n_dma_sems
        self.dma_rr = {e: 0 for e in self.ENGS}
        self.dma_cnt = {}
        self.semh = {}
        self.sem_names = [("c", e) for e in self.ENGS]
        for e in ("sp", "act", "pool"):
            for j in range(n_dma_sems):
                self.sem_names.append(("d", e, j))
                self.dma_cnt[("d", e, j)] = 0
        self.ninst = 0
        self.marks = []
        self.sym = {e: [] for e in self.ENGS}

    def _need(self, eng, tok):
        if tok is None:
            return
        semkey, val, weng = tok
        if weng == eng and semkey[0] == "c":
            if eng == "pe" or not self.same_engine_sync:
                return
        if self.waited[eng].get(semkey, 0) >= val:
            return
        self.waited[eng][semkey] = val
        self.sym[eng].append(("w", semkey, val))
        self.ops[eng].append(lambda h: h.wait_ge(self.semh[semkey], val))

    def _deps(self, eng, reads, writes):
        best = {}
        for b in reads:
            t = b.w
            if t is not None and best.get(t[0], (None, 0))[1] < t[1]:
                best[t[0]] = t
        for b in writes:
            for t in [b.w] + b.r:
                if t is not None and best.get(t[0], (None, 0))[1] < t[1]:
                    best[t[0]] = t
        for t in best.values():
            self._need(eng, t)

    def _mark(self, tok, reads, writes):
        for b in reads:
            for i, t in enumerate(b.r):
                if t[0] == tok[0]:
                    if t[1] < tok[1]:
                        b.r[i] = tok
                    break
            else:
                b.r.append(tok)
        for b in writes:
            b.w = tok
            b.r = []

    def op(self, eng, fn, reads=(), writes=()):
        self._deps(eng, reads, writes)
        self.cnt[eng] += 1
        self.ninst += 1
        semkey = ("c", eng)
        self.sym[eng].append(("i", semkey, 1))
        self.ops[eng].append(lambda h: fn(h).then_inc(self.semh[semkey], 1))
        self._mark((semkey, self.cnt[eng], eng), reads, writes)

    def dma(self, eng, out, in_, reads=(), writes=(), **kw):
        j = self.dma_rr[eng]
        self.dma_rr[eng] = (j + 1) % self.n_dma_sems
        semkey = ("d", eng, j)
        m = self.dma_cnt[semkey]
        if m > 0:
            self._need(eng, (semkey, 16 * m, "dma"))
        self._deps(eng, reads, writes)
        self.dma_cnt[semkey] = m + 1
        self.ninst += 1
        self.sym[eng].append(("i", semkey, 16))
        self.ops[eng].append(lambda h: h.dma_start(out=out, in_=in_, **kw).then_inc(self.semh[semkey], 16))
        tok = (semkey, 16 * (m + 1), "dma")
        self._mark(tok, reads, writes)
        return tok

    def mark(self, name):
        self.marks.append((name, dict(self.cnt)))

    def barrier(self):
        for e in ("pe", "act", "dve", "pool"):
            for o in ("pe", "act", "dve", "pool"):
                if o != e and self.cnt[o] > 0:
                    semkey = ("c", o)
                    val = self.cnt[o]
                    if self.waited[e].get(semkey, 0) < val:
                        self.waited[e][semkey] = val
                        self.sym[e].append(("w", semkey, val))
                        self.ops[e].append(lambda h, semkey=semkey, val=val: h.wait_ge(self.semh[semkey], val))

    def wait_all(self, eng):
        for o in ("pe", "act", "dve", "pool"):
            if o != eng and self.cnt[o] > 0:
                semkey = ("c", o)
                val = self.cnt[o]
                if self.waited[eng].get(semkey, 0) < val:
                    self.waited[eng][semkey] = val
                    self.sym[eng].append(("w", semkey, val))
                    self.ops[eng].append(lambda h, semkey=semkey, val=val: h.wait_ge(self.semh[semkey], val))

    def check_deadlock(self):
        sem = {}
        pc = {e: 0 for e in self.ENGS}
        prog = True
        while prog:
            prog = False
            for e in self.ENGS:
                ops = self.sym[e]
                while pc[e] < len(ops):
                    k, key, v = ops[pc[e]]
                    if k == "w":
                        if sem.get(key, 0) < v:
                            break
                    else:
                        sem[key] = sem.get(key, 0) + v
                    pc[e] += 1
                    prog = True
        stuck = {e: (pc[e], len(self.sym[e]), self.sym[e][pc[e]] if pc[e] < len(self.sym[e]) else None) for e in self.ENGS}
        return all(pc[e] == len(self.sym[e]) for e in self.ENGS), stuck, sem

    def finish(self, eng, bufs):
        for b in bufs:
            self._need(eng, b.w)

    def build(self, st):
        nc = self.nc
        for k in self.sem_names:
            self.semh[k] = st.enter_context(nc.semaphore("s_" + "_".join(str(x) for x in k)))
        blk = st.enter_context(nc.Block())

        def mk(e):
            def f(h):
                for o in self.ops[e]:
                    o(h)
            return f
        blk.tensor(mk("pe"))
        blk.scalar(mk("act"))
        blk.vector(mk("dve"))
        blk.gpsimd(mk("pool"))
        blk.sync(mk("sp"))


SES = True
CE = "dve"
WQ = "pool"
XQ = "sp"


def build_program(NTOK, NL, keys0=None):
    assert NTOK % T == 0
    NTILE = NTOK // T
    nc = bass.Bass("TRN2", target_bir_lowering=False)
    dr = {}

    def dram(name, shape, kind="ExternalInput", dt=F32):
        dr[name] = nc.dram_tensor(name, list(shape), dt, kind=kind).ap()
        return dr[name]
    x_in = dram("x", [NTOK, D])
    w_in = dram("w_in", [L_ALL, D, NIN])
    w_br = [dram("w_br_hg", [L_ALL, D, D]), dram("w_br_ml", [L_ALL, D, D]), dram("w_br_mb", [L_ALL, D, D])]
    w_out = dram("w_out", [L_ALL, D, D])
    w_up = dram("w_up", [L_ALL, D, 2 * DFF])
    w_down = dram("w_down", [L_ALL, DFF, D])
    pvec = dram("pvec", [L_ALL * 3 * 128, 128])
    bvec = dram("bvec", [1, L_ALL * 32])
    gbv = dram("gbv", [4, L_ALL * 2])
    consts = dram("consts", [128, NCONST])
    out_d = dram("out", [NTOK, D], kind="ExternalOutput")
    WSRC = {"w_in": w_in, "w_br0": w_br[0], "w_br1": w_br[1], "w_br2": w_br[2], "w_out": w_out, "w_up": w_up,
            "w_down": w_down}
    WDST = {k: dram("bf_" + k, list(v.shape), kind="Internal", dt=BF16) for k, v in WSRC.items()}
    WBUF = {}
    rec_keys = []
    xscr = dram("xscr", [8, 128, NTOK], kind="Internal")

    P = Prog(nc, same_engine_sync=SES)
    st = contextlib.ExitStack()

    def sb(name, shape, dt=F32):
        return st.enter_context(nc.sbuf_tensor(name, list(shape), dt))

    cst = sb("cst", [128, NCONST])
    identb = sb("identb", [128, 128], BF16)
    ones_bf = sb("ones_bf", [128, 128], BF16)
    ptab = sb("ptab", [128, L_ALL, NROW])
    bvt = sb("bvt", [128, L_ALL, 32])
    aneg = sb("aneg", [128, L_ALL, 16])
    gbt = sb("gbt", [128, L_ALL, 2])
    lbt = sb("lbt", [128, L_ALL, 8])
    omlt = sb("omlt", [128, L_ALL, 8])
    nomlt = sb("nomlt", [128, L_ALL, 8])
    xf = sb("xf", [128, 8, T])
    mixed = sb("mixed", [128, 8, T])
    hT = sb("hT", [128, 8, T], BF16)
    NWB = 5
    wring = sb("wring", [128, NWB, 8, 512], BF16)
    S_hg = sb("S_hg", [128, 8, 128])
    Sb_hg = [sb(f"Sb_hg{i}", [128, 8, 128], BF16) for i in range(2)]
    C_ml = sb("C_ml", [128, 4, 256])
    Cb_ml = [sb(f"Cb_ml{i}", [128, 4, 256], BF16) for i in range(2)]
    n_ml = sb("n_ml", [128, 4])
    nrep = [sb(f"nrep{i}", [128, 4, 128], BF16) for i in range(2)]
    St_mb = sb("St_mb", [128, 16, 64])
    Stb_mb = sb("Stb_mb", [128, 16, 64], BF16)
    tail_ml = sb("tail_ml", [128, 8, 3])
    tail_mb = sb("tail_mb", [128, 16, 3])
    tail_ff = sb("tail_ff", [128, 44, 2])
    AW = 17408
    arena = sb("arena", [128, AW])
    psum_t = [st.enter_context(nc.psum_tensor(f"ps{i}", [128, 1024], F32)) for i in range(4)]

    ident = cst[:, C_ID:C_ID + 128]
    mask2 = cst[:, C_M2:C_M2 + 128]
    maskc = cst[0:64, C_M2:C_M2 + 64]
    SLm = cst[0:64, C_SL:C_SL + 64]
    gseg = cst[:, C_GS:C_GS + 512]
    sel = cst[0:4, C_SEL:C_SEL + 512]
    ones_f = cst[:, C_ONE:C_ONE + 128]

    B_cst, B_par = Buf("cst"), Buf("par")
    B_xf, B_mixed, B_hT = Buf("xf"), Buf("mixed"), Buf("hT")
    WB = [Buf(f"wb{i}") for i in range(NWB)]
    PB = [Buf(f"pb{i}") for i in range(8)]
    B_S, B_Sb = Buf("S"), [Buf("Sb0"), Buf("Sb1")]
    B_C, B_Cb, B_n, B_nrep = Buf("C"), [Buf("Cb0"), Buf("Cb1")], Buf("n"), [Buf("nr0"), Buf("nr1")]
    B_St, B_Stb = Buf("St"), Buf("Stb")
    B_tml, B_tmb, B_tff = [Buf("tml0"), Buf("tml1")], [Buf("tmb0"), Buf("tmb1")], [Buf("tff0"), Buf("tff1")]
    B_xs = [Buf(f"xs{j}") for j in range(NTILE)]
    B_out = [Buf(f"out{j}") for j in range(NTILE)]

    class Arena:
        def __init__(self):
            self.off = 0

        def reset(self):
            self.off = 0

        def alloc(self, shape, dt=F32, parts=128):
            n = int(np.prod(shape[1:]))
            esz = 4 if dt == F32 else 2
            nbytes = (n * esz + 63) // 64 * 64
            o = self.off
            self.off += nbytes
            assert self.off <= AW * 4, ("arena overflow", self.off)
            if dt == F32:
                v = arena[0:shape[0], o // 4:o // 4 + n]
            else:
                v = arena[0:shape[0], o // 4:o // 4 + (n * 2 + 3) // 4].bitcast(BF16)[:, 0:n]
            if len(shape) == 3:
                v = v.rearrange("p (a b) -> p a b", a=shape[1])
            elif len(shape) == 4:
                v = v.rearrange("p (a b c) -> p a b c", a=shape[1], b=shape[2])
            return v
    A = Arena()

    class PSA:
        def __init__(self):
            self.i = 0
            self.excl = set()

        def get(self, n=1, hold=False):
            for _ in range(32):
                if n == 2 and self.i % 2:
                    self.i += 1
                b = self.i % 8
                if all(((b + k) % 8) not in self.excl for k in range(n)) and b + n <= 8:
                    self.i += n
                    if hold:
                        for k in range(n):
                            self.excl.add(b + k)
                    return b
                self.i += 1
            raise RuntimeError("psum exhausted")

        def release(self, b, n=1):
            for k in range(n):
                self.excl.discard(b + k)
    PS = PSA()

    def pb(b):
        return psum_t[b // 2][:, (b % 2) * 512:(b % 2) * 512 + 512]

    def pbb(b):
        return psum_t[b // 2].bitcast(BF16)[:, (b % 2) * 1024:(b % 2) * 1024 + 1024]

    def pd(b):
        assert b % 2 == 0
        return psum_t[b // 2][:, 0:1024]

    def mm(out, lhsT, rhs, start, stop, R, W):
        P.op("pe", lambda h: h.matmul(out=out, lhsT=lhsT, rhs=rhs, start=start, stop=stop), R, W)

    def tr(out, in_, idn, R, W):
        P.op("pe", lambda h: h.transpose(out=out, in_=in_, identity=idn), R, W)

    def act(out, in_, func, R, W, scale=None, bias=None):
        kw = {}
        if scale is not None:
            kw["scale"] = scale
        if bias is not None:
            kw["bias"] = bias
        P.op("act", lambda h: h.activation(out=out, in_=in_, func=func, **kw), R, W)

    def acopy(out, in_, R, W):
        P.op("act", lambda h: h.copy(out=out, in_=in_), R, W)

    def tt(out, a, b, op, R, W, eng="dve"):
        P.op(eng, lambda h: h.tensor_tensor(out=out, in0=a, in1=b, op=op), R, W)

    def ts(out, a, s1, s2, op0, op1, R, W, eng="dve"):
        if s2 is None:
            P.op(eng, lambda h: h.tensor_scalar(out=out, in0=a, scalar1=s1, scalar2=None, op0=op0), R, W)
        else:
            P.op(eng, lambda h: h.tensor_scalar(out=out, in0=a, scalar1=s1, scalar2=s2, op0=op0, op1=op1), R, W)

    def stt(out, a, s, b, op0, op1, R, W, eng="dve"):
        P.op(eng, lambda h: h.scalar_tensor_tensor(out=out, in0=a, scalar=s, in1=b, op0=op0, op1=op1), R, W)

    def vcopy(out, in_, R, W, eng="dve"):
        P.op(eng, lambda h: h.tensor_copy(out=out, in_=in_), R, W)

    def recip(out, in_, R, W):
        P.op("dve", lambda h: h.reciprocal(out=out, in_=in_), R, W)

    def scan(out, d0, d1, R, W):
        P.op("dve", lambda h: h.tensor_tensor_scan(out=out, data0=d0, data1=d1, initial=0.0,
                                                    op0=ALU.mult, op1=ALU.add), R, W)

    def memset(ap, val, W, eng="dve"):
        P.op(eng, lambda h: h.memset(ap, val), (), W)

    def rstd_from_ss(rs, ss_psum, n, R, W):
        act(rs, ss_psum, AF.Ln, list(R) + [B_cst], W, scale=1.0 / n, bias=cst[:, C_EPS:C_EPS + 1])
        act(rs, rs, AF.Exp, W, W, scale=-0.5)

    wslot = [0]

    def wview(wt, l, c0, n, k0, nk):
        v = wt[l].rearrange("(k p) n -> p k n", p=128)
        if nk is None:
            nk = v.shape[1] - k0
        return v[:, k0:k0 + nk, c0:c0 + n]

    def convert_w(key):
        name, l, c0, n, k0, nk = key
        b = Buf("w" + str(key))
        WBUF[key] = b
        P.dma("pool", wview(WDST[name], l, c0, n, k0, nk), wview(WSRC[name], l, c0, n, k0, nk), writes=[b])

    def loadw(key):
        name, l, c0, n, k0, nk = key
        if key not in WBUF:
            convert_w(key)
        if l == 0:
            rec_keys.append(key)
        s = wslot[0]
        wslot[0] = (s + 1) % NWB
        src = wview(WDST[name], l, c0, n, k0, nk)
        dst = wring[:, s, 0:src.shape[1], 0:n]
        P.dma(WQ, dst, src, reads=[WBUF[key]], writes=[WB[s]])
        nxt = (name, l + 1, c0, n, k0, nk)
        if l + 1 < NL and nxt not in WBUF:
            convert_w(nxt)
        return dst, WB[s]

    def wsrc(name, l, c0, n, k0=0, nk=None):
        return (name, l, c0, n, k0, nk)

    P.dma("sp", cst[:], consts, writes=[B_cst])
    vcopy(identb[:], ident, [B_cst], [B_cst])
    vcopy(ones_bf[:], ones_f, [B_cst], [B_cst])
    A.reset()
    praw = A.alloc([128, L_ALL * 3, 128])
    B_praw = Buf("praw")
    P.dma("sp", praw, pvec.rearrange("(n p) c -> p n c", p=128), writes=[B_praw])
    for l in range(L_ALL):
        for i in range(3):
            b = PS.get()
            tr(pb(b)[:, 0:128], praw[:, l * 3 + i, :], ident, [B_praw, B_cst], [PB[b]])
            acopy(ptab[:, l, i * 128:(i + 1) * 128], pb(b)[:, 0:128], [PB[b]], [B_par])
    P.dma("sp", bvt[:].rearrange("p l c -> p (l c)"), bvec.partition_broadcast(128).rearrange("p o c -> p (o c)"),
          writes=[B_par])
    P.dma("sp", gbt[0:4].rearrange("p l c -> p (l c)"), gbv, writes=[B_par])
    act(aneg[:], bvt[:, :, 16:32], AF.Exp, [B_par], [B_par])
    ts(aneg[:], aneg[:], -1.0, None, ALU.mult, None, [B_par], [B_par])
    lbe = A.alloc([128, L_ALL, 8])
    lsum = A.alloc([128, 8])
    B_lb = Buf("lb")
    for l in range(L_ALL):
        act(lbe[:, l, :], ptab[:, l, R_LB:R_LB + 8], AF.Exp, [B_par], [B_lb])
    tt(lsum, lbe[:, 0, :], lbe[:, 1, :], ALU.add, [B_lb], [B_lb])
    tt(lsum, lsum, lbe[:, 2, :], ALU.add, [B_lb], [B_lb])
    tt(lsum, lsum, lbe[:, 3, :], ALU.add, [B_lb], [B_lb])
    recip(lsum, lsum, [B_lb], [B_lb])
    for l in range(L_ALL):
        tt(lbe[:, l, :], lbe[:, l, :], lsum, ALU.mult, [B_lb], [B_lb])
    memset(lbt[:, 0, :], 0.0, [B_par])
    for l in range(1, L_ALL):
        tt(lbt[:, l, :], lbt[:, l - 1, :], lbe[:, l, :], ALU.add, [B_lb, B_par], [B_par])
    ts(omlt[:], lbt[:], -1.0, 1.0, ALU.mult, ALU.add, [B_par], [B_par])
    ts(nomlt[:], lbt[:], 1.0, -1.0, ALU.mult, ALU.add, [B_par], [B_par])
    P.barrier()

    if keys0 is not None:
        for (name, _, c0, n, k0, nk) in keys0:
            convert_w((name, 0, c0, n, k0, nk))

    def prow(l, r):
        return ptab[:, l, r:r + 1]

    def norm_to_hT(l, row0):
        A.reset()
        sq = A.alloc([128, 8, T], BF16)
        rs = A.alloc([128, T])
        Bq, Br = Buf(), Buf()
        act(sq, xf[:], AF.Square, [B_xf], [Bq])
        b = PS.get()
        for kc in range(8):
            mm(pb(b), ones_bf[:], sq[:, kc, :], kc == 0, kc == 7, [Bq, B_cst], [PB[b]])
        rstd_from_ss(rs, pb(b), D, [PB[b]], [Br])
        for kc in range(8):
            stt(hT[:, kc, :], xf[:, kc, :], prow(l, row0 + kc), rs, ALU.mult, ALU.mult, [B_xf, Br, B_par], [B_hT])
        P.barrier()

    def branch_project(l, br, ybr, B_y):
        gt = A.alloc([128, T])
        tm = A.alloc([128, T])
        Bg, Bt = Buf(), Buf()
        for half in range(2):
            wb_, Bw = loadw(wsrc("w_br%d" % br, l, half * 512, 512))
            wg_, Bwg = loadw(wsrc("w_in", l, O_GATE + br * 1024 + half * 512, 512))
            for dd in range(4):
                dc = half * 4 + dd
                bg = PS.get()
                for kc in range(8):
                    mm(pb(bg), wg_[:, kc, dd * 128:(dd + 1) * 128], hT[:, kc, :], kc == 0, kc == 7, [Bwg, B_hT], [PB[bg]])
                act(gt, pb(bg), AF.Sigmoid, [PB[bg]], [Bg])
                bp = PS.get()
                for kc in range(8):
                    mm(pb(bp), wb_[:, kc, dd * 128:(dd + 1) * 128], ybr[:, kc, :], kc == 0, kc == 7, [Bw, B_y], [PB[bp]])
                if br == 0:
                    tt(mixed[:, dc, :], pb(bp), gt, ALU.mult, [PB[bp], Bg], [B_mixed])
                else:
                    tt(tm, pb(bp), gt, ALU.mult, [PB[bp], Bg], [Bt])
                    tt(mixed[:, dc, :], mixed[:, dc, :], tm, ALU.add, [Bt, B_mixed], [B_mixed])

    def tm_proj(l, c0, dst, B_dst):
        for cb in range(2):
            wv, Bw = loadw(wsrc("w_in", l, c0 + cb * 512, 512))
            for pp in range(4):
                b = PS.get()
                for kc in range(8):
                    mm(pb(b), hT[:, kc, pp * 128:(pp + 1) * 128], wv[:, kc, :], kc == 0, kc == 7, [Bw, B_hT], [PB[b]])
                acopy(dst[:, pp, cb * 512:(cb + 1) * 512], pb(b), [PB[b]], [B_dst])

    def fm_proj_act(l, c0, nchunk, func, dst, B_dst):
        for c in range(nchunk):
            if c % 4 == 0:
                n = min(4, nchunk - c) * 128
                wq, Bw = loadw(wsrc("w_in", l, c0 + c * 128, n))
            b = PS.get()
            for kc in range(8):
                mm(pb(b), wq[:, kc, (c % 4) * 128:(c % 4 + 1) * 128], hT[:, kc, :], kc == 0, kc == 7, [Bw, B_hT], [PB[b]])
            act(dst[:, c, :], pb(b), func, [PB[b]], [B_dst])

    def conv_fm(l, c0, nchunk, K, tail, B_tail, rw, rb, wstride, outs, first_tile):
        NBF = 4
        ub = [A.alloc([128, T + K - 1]) for _ in range(NBF)]
        Bu = [Buf() for _ in range(NBF)]
        But = [Buf() for _ in range(NBF)]
        ca = [A.alloc([128, T]) for _ in range(NBF)]
        Bc = [Buf() for _ in range(NBF)]
        wcur = [None]

        def front(c):
            if c % 4 == 0:
                n = min(4, nchunk - c) * 128
                wcur[0] = loadw(wsrc("w_in", l, c0 + c * 128, n))
            wq, Bw = wcur[0]
            b = PS.get()
            for kc in range(8):
                mm(pb(b), wq[:, kc, (c % 4) * 128:(c % 4 + 1) * 128], hT[:, kc, :], kc == 0, kc == 7, [Bw, B_hT], [PB[b]])
            u, bu, cc, bc = ub[c % NBF], Bu[c % NBF], ca[c % NBF], Bc[c % NBF]
            bt = B_tail[c % 2]
            but = But[c % NBF]
            if first_tile:
                memset(u[:, 0:K - 1], 0.0, [but])
            else:
                acopy(u[:, 0:K - 1], tail[:, c, :], [bt], [but])
            acopy(u[:, K - 1:K - 1 + T], pb(b), [PB[b]], [bu])
            act(cc, pb(b), AF.Identity, [PB[b], B_par], [bc], scale=prow(l, rw + (K - 1) * wstride + c), bias=prow(l, rb + c))
            acopy(tail[:, c, :], u[:, T:T + K - 1], [bu], [bt])
            for k in range(0, K - 1):
                stt(cc, u[:, k:k + T], prow(l, rw + k * wstride + c), cc, ALU.mult, ALU.add, [bu, but, bc, B_par], [bc])

        def back(c):
            cc, bc = ca[c % NBF], Bc[c % NBF]
            dst, bd, post = outs(c)
            if post is None:
                act(dst, cc, AF.Silu, [bc], [bd])
            else:
                act(cc, cc, AF.Silu, [bc], [bc])
                ts(dst, cc, post, None, ALU.mult, None, [bc], [bd])
        for c in range(nchunk):
            front(c)
            if c >= 2:
                back(c - 2)
        back(nchunk - 2)
        back(nchunk - 1)

    def phase_hg(l, j):
        A.reset()
        qt = A.alloc([128, 8, T], BF16)
        kt = A.alloc([128, 8, T], BF16)
        vtm = A.alloc([128, 4, 1024], BF16)
        sg = A.alloc([128, 8, T], BF16)
        ybr = A.alloc([128, 8, T], BF16)
        ebl = A.alloc([128, 8, 8])
        mark = A.off
        t_qs, t_sig, t_lf, t_b, t_eb, t_enb, t_k = [A.alloc([128, T]) for _ in range(7)]
        Bqt, Bkt, Bv, Bsg, By, Bebl = Buf(), Buf(), Buf(), Buf(), Buf(), Buf()
        Bt = [Buf() for _ in range(7)]
        if j == 0:
            memset(S_hg[:], 0.0, [B_S])
            memset(Sb_hg[0][:], 0.0, [B_Sb[0]])
        for hb in range(2):
            wq, Bwq = loadw(wsrc("w_in", l, O_HGQ + hb * 512, 512))
            wf, Bwf = loadw(wsrc("w_in", l, O_HGF + hb * 512, 512))
            for hh in range(4):
                h = hb * 4 + hh
                b = PS.get()
                for kc in range(8):
                    mm(pb(b), wq[:, kc, hh * 128:(hh + 1) * 128], hT[:, kc, :], kc == 0, kc == 7, [Bwq, B_hT], [PB[b]])
                act(t_qs, pb(b), AF.Silu, [PB[b]], [Bt[0]])
                b2 = PS.get()
                for kc in range(8):
                    mm(pb(b2), wf[:, kc, hh * 128:(hh + 1) * 128], hT[:, kc, :], kc == 0, kc == 7, [Bwf, B_hT], [PB[b2]])
                act(t_sig, pb(b2), AF.Sigmoid, [PB[b2]], [Bt[1]])
                act(t_lf, t_sig, AF.Ln, [Bt[1], B_par], [Bt[2]], scale=omlt[:, l, h:h + 1], bias=lbt[:, l, h:h + 1])
                scan(t_b, gseg, t_lf, [Bt[2], B_cst], [Bt[3]])
                act(t_eb, t_b, AF.Exp, [Bt[3]], [Bt[4]])
                act(t_enb, t_b, AF.Exp, [Bt[3]], [Bt[5]], scale=-1.0)
                ts(t_k, t_sig, nomlt[:, l, h:h + 1], omlt[:, l, h:h + 1], ALU.mult, ALU.add, [Bt[1], B_par], [Bt[6]])
                tt(qt[:, h, :], t_qs, t_eb, ALU.mult, [Bt[0], Bt[4]], [Bqt])
                tt(kt[:, h, :], t_k, t_enb, ALU.mult, [Bt[6], Bt[5]], [Bkt])
                vcopy(ebl[:, :, h], t_eb.rearrange("p (c j) -> p c j", j=64)[:, :, 63], [Bt[4]], [Bebl])
        tm_proj(l, O_HGI, vtm, Bv)
        fm_proj_act(l, O_HGG, 8, AF.Silu, sg, Bsg)
        P.barrier()
        A.off = mark
        at = A.alloc([128, 8, 128], BF16)
        ktm = A.alloc([128, 8, 128], BF16)
        tmpS = A.alloc([128, 8, 128])
        sq = A.alloc([128, 8, 128], BF16)
        rs = A.alloc([128, 8, 128])
        yn = A.alloc([128, 8, 128])
        Bat, Bktm, BtS, Bsq, Brs, Byn = Buf(), Buf(), Buf(), Buf(), Buf(), Buf()
        cur = 0
        for pp in range(4):
            c0 = pp * 128
            bo = PS.get(2, hold=True)
            bd = PS.get(2, hold=True)
            for h in range(8):
                ba = PS.get()
                mm(pb(ba)[:, 0:128], kt[:, h, c0:c0 + 128], qt[:, h, c0:c0 + 128], True, True, [Bkt, Bqt], [PB[ba]])
                tt(at[:, h, :], pb(ba)[:, 0:128], mask2, ALU.mult, [PB[ba], B_cst], [Bat])
                bt_ = PS.get()
                tr(pbb(bt_)[:, 0:128], kt[:, h, c0:c0 + 128], identb[:], [Bkt, B_cst], [PB[bt_]])
                acopy(ktm[:, h, :], pbb(bt_)[:, 0:128], [PB[bt_]], [Bktm])
            for c in range(2):
                r0 = c * 64
                ch = pp * 2 + c
                for h in range(8):
                    mm(pd(bd)[:, h * 128:(h + 1) * 128], ktm[r0:r0 + 64, h, :], vtm[r0:r0 + 64, pp, h * 128:(h + 1) * 128],
                       True, True, [Bktm, Bv], [PB[bd], PB[bd + 1]])
                for h in range(8):
                    o = pd(bo)[:, h * 128 + r0:h * 128 + r0 + 64]
                    mm(o, vtm[r0:r0 + 64, pp, h * 128:(h + 1) * 128], at[r0:r0 + 64, h, r0:r0 + 64], True, False,
                       [Bv, Bat], [PB[bo], PB[bo + 1]])
                    mm(o, Sb_hg[cur][:, h, :], qt[:, h, c0 + r0:c0 + r0 + 64], False, True,
                       [B_Sb[cur], Bqt], [PB[bo], PB[bo + 1]])
                tt(tmpS, pd(bd).rearrange("p (h v) -> p h v", h=8), S_hg[:], ALU.add, [PB[bd], PB[bd + 1], B_S], [BtS])
                tt(S_hg[:], tmpS, ebl[:, ch, :].unsqueeze(2).broadcast_to([128, 8, 128]), ALU.mult, [BtS, Bebl], [B_S])
                acopy(Sb_hg[1 - cur][:], S_hg[:], [B_S], [B_Sb[1 - cur]])
                cur = 1 - cur
            PS.release(bd, 2)
            o3 = pd(bo)
            sqf = sq.rearrange("p h t -> p (h t)")
            act(sqf, o3, AF.Square, [PB[bo], PB[bo + 1]], [Bsq])
            bs = PS.get(2)
            for hf in range(2):
                mm(pd(bs)[:, hf * 512:(hf + 1) * 512], ones_bf[:], sqf[:, hf * 512:(hf + 1) * 512], True, True,
                   [Bsq, B_cst], [PB[bs], PB[bs + 1]])
            rstd_from_ss(rs.rearrange("p h t -> p (h t)"), pd(bs), 128, [PB[bs], PB[bs + 1]], [Brs])
            tt(yn.rearrange("p h t -> p (h t)"), o3, rs.rearrange("p h t -> p (h t)"), ALU.mult, [PB[bo], PB[bo + 1], Brs], [Byn])
            stt(ybr[:, :, c0:c0 + 128], yn, prow(l, R_HGG), sg[:, :, c0:c0 + 128], ALU.mult, ALU.mult, [Byn, Bsg, B_par], [By])
            PS.release(bo, 2)
        assert cur == 0
        branch_project(l, 0, ybr, By)
        P.barrier()

    def phase_ml(l, j):
        A.reset()
        qm = A.alloc([128, 4, T], BF16)
        km = A.alloc([128, 4, T], BF16)
        vtm = A.alloc([128, 4, 1024], BF16)
        so = A.alloc([128, 8, T], BF16)
        ybr = A.alloc([128, 8, T], BF16)
        g_lf, g_bh, g_g, g_e, g_ei = [A.alloc([4, T]) for _ in range(5)]
        elbc = A.alloc([128, 4, 8])
        Bqm, Bkm, Bv, Bso, By = Buf(), Buf(), Buf(), Buf(), Buf()
        Bg = [Buf() for _ in range(5)]
        Bel = Buf()
        mark = A.off
        if j == 0:
            memset(C_ml[:], 0.0, [B_C])
            memset(Cb_ml[0][:], 0.0, [B_Cb[0]])
            memset(n_ml[:], 0.0, [B_n])
            memset(nrep[0][:], 0.0, [B_nrep[0]])
        wif, Bwif = loadw(wsrc("w_in", l, O_MLIF, 8))
        bi_, bf_ = PS.get(), PS.get()
        for kc in range(8):
            mm(pb(bi_)[0:4, :], wif[:, kc, 0:4], hT[:, kc, :], kc == 0, kc == 7, [Bwif, B_hT], [PB[bi_]])
        for kc in range(8):
            mm(pb(bf_)[0:4, :], wif[:, kc, 4:8], hT[:, kc, :], kc == 0, kc == 7, [Bwif, B_hT], [PB[bf_]])
        act(g_lf, pb(bf_)[0:4, :], AF.Sigmoid, [PB[bf_], B_par], [Bg[0]], bias=gbt[0:4, l, 1:2])
        act(g_lf, g_lf, AF.Ln, [Bg[0]], [Bg[0]])
        scan(g_bh, gseg[0:4, :], g_lf, [Bg[0], B_cst], [Bg[1]])
        stt(g_g, pb(bi_)[0:4, :], gbt[0:4, l, 0:1], g_bh, ALU.add, ALU.subtract, [PB[bi_], Bg[1], B_par], [Bg[2]])
        act(g_g, g_g, AF.Exp, [Bg[2]], [Bg[2]])
        act(g_e, g_bh, AF.Exp, [Bg[1]], [Bg[3]])
        act(g_ei, g_bh, AF.Exp, [Bg[1]], [Bg[4]], scale=-1.0)
        be = PS.get()
        gel = g_e.rearrange("p (c j) -> p c j", j=64)[:, :, 63]
        for h in range(4):
            mm(pb(be)[:, h * 8:(h + 1) * 8], sel[:, h * 128:(h + 1) * 128], gel, True, True, [Bg[3], B_cst], [PB[be]])
        acopy(elbc.rearrange("p h c -> p (h c)"), pb(be)[:, 0:32], [PB[be]], [Bel])

        def qk_out(c):
            if c < 4:
                return qm[:, c, :], Bqm, None
            return km[:, c - 4, :], Bkm, 128.0 ** -0.5
        conv_fm(l, O_MLQ, 8, 4, tail_ml, B_tml, R_MLW, R_MLB, 8, qk_out, j == 0)
        tm_proj(l, O_MLV, vtm, Bv)
        fm_proj_act(l, O_MLO, 8, AF.Sigmoid, so, Bso)
        P.barrier()
        A.off = mark
        gtm = A.alloc([128, 4])
        einvbc = A.alloc([128, 4, 128])
        at = A.alloc([128, 4, 128], BF16)
        kgtm = A.alloc([128, 4, 128], BF16)
        tmpS = A.alloc([128, 4, 256])
        tmpn = A.alloc([128, 4])
        dnm = A.alloc([128, 4, 128])
        hh = A.alloc([128, 4, 2, 128])
        sq = A.alloc([128, 8, 128], BF16)
        rs = A.alloc([128, 4, 128])
        Bgtm, Bei, Bat, Bkg, BtS, Btn, Bdn, Bhh, Bsq, Brs = [Buf() for _ in range(10)]
        cur = 0
        for pp in range(4):
            c0 = pp * 128
            bgt = PS.get()
            tr(pb(bgt)[:, 0:4], g_g[0:4, c0:c0 + 128], ident[0:4, 0:4], [Bg[2], B_cst], [PB[bgt]])
            acopy(gtm, pb(bgt)[:, 0:4], [PB[bgt]], [Bgtm])
            bei = PS.get()
            for h in range(4):
                mm(pb(bei)[:, h * 128:(h + 1) * 128], sel[:, h * 128:(h + 1) * 128], g_ei[0:4, c0:c0 + 128], True, True,
                   [Bg[4], B_cst], [PB[bei]])
            acopy(einvbc.rearrange("p h t -> p (h t)"), pb(bei), [PB[bei]], [Bei])
            bnum = PS.get(2, hold=True)
            bdC = PS.get(2, hold=True)
            bden = PS.get(1, hold=True)
            bdn = PS.get(1, hold=True)
            for h in range(4):
                ba = PS.get()
                mm(pb(ba)[:, 0:128], km[:, h, c0:c0 + 128], qm[:, h, c0:c0 + 128], True, True, [Bkm, Bqm], [PB[ba]])
                stt(at[:, h, :], pb(ba)[:, 0:128], gtm[:, h:h + 1], mask2, ALU.mult, ALU.mult, [PB[ba], Bgtm, B_cst], [Bat])
                bt_ = PS.get()
                tr(pbb(bt_)[:, 0:128], km[:, h, c0:c0 + 128], identb[:], [Bkm, B_cst], [PB[bt_]])
                ts(kgtm[:, h, :], pbb(bt_)[:, 0:128], gtm[:, h:h + 1], None, ALU.mult, None, [PB[bt_], Bgtm], [Bkg])
            for c in range(2):
                r0 = c * 64
                ch = pp * 2 + c
                tc0 = c0 + r0
                for h in range(4):
                    mm(pd(bdC)[:, h * 256:(h + 1) * 256], kgtm[r0:r0 + 64, h, :], vtm[r0:r0 + 64, pp, h * 256:(h + 1) * 256],
                       True, True, [Bkg, Bv], [PB[bdC], PB[bdC + 1]])
                    mm(pb(bdn)[:, h:h + 1], kgtm[r0:r0 + 64, h, :], ones_bf[r0:r0 + 64, 0:1],
                       True, True, [Bkg, B_cst], [PB[bdn]])
                for h in range(4):
                    for dvc in range(2):
                        o = pd(bnum)[:, (h * 2 + dvc) * 128 + r0:(h * 2 + dvc) * 128 + r0 + 64]
                        mm(o, vtm[r0:r0 + 64, pp, h * 256 + dvc * 128:h * 256 + dvc * 128 + 128], at[r0:r0 + 64, h, r0:r0 + 64],
                           True, False, [Bv, Bat], [PB[bnum], PB[bnum + 1]])
                        mm(o, Cb_ml[cur][:, h, dvc * 128:(dvc + 1) * 128], qm[:, h, tc0:tc0 + 64], False, True,
                           [B_Cb[cur], Bqm], [PB[bnum], PB[bnum + 1]])
                    od = pb(bden)[:, h * 128 + r0:h * 128 + r0 + 64]
                    mm(od, ones_bf[r0:r0 + 64, :], at[r0:r0 + 64, h, r0:r0 + 64], True, False, [Bat, B_cst], [PB[bden]])
                    mm(od, nrep[cur][:, h, :], qm[:, h, tc0:tc0 + 64], False, True, [B_nrep[cur], Bqm], [PB[bden]])
                elb = elbc[:, :, ch]
                tt(tmpS, pd(bdC).rearrange("p (h v) -> p h v", h=4), C_ml[:], ALU.add,
                   [PB[bdC], PB[bdC + 1], B_C], [BtS])
                tt(C_ml[:], tmpS, elb.unsqueeze(2).broadcast_to([128, 4, 256]), ALU.mult, [BtS, Bel], [B_C])
                acopy(Cb_ml[1 - cur][:], C_ml[:], [B_C], [B_Cb[1 - cur]])
                tt(tmpn, pb(bdn)[:, 0:4], n_ml[:], ALU.add, [PB[bdn], B_n], [Btn])
                tt(n_ml[:], tmpn, elb, ALU.mult, [Btn, Bel], [B_n])
                vcopy(nrep[1 - cur][:], n_ml[:].unsqueeze(2).broadcast_to([128, 4, 128]), [B_n], [B_nrep[1 - cur]])
                cur = 1 - cur
            PS.release(bdC, 2)
            PS.release(bdn, 1)
            dflat = dnm.rearrange("p h t -> p (h t)")
            act(dflat, pb(bden), AF.Abs, [PB[bden]], [Bdn])
            tt(dflat, dflat, einvbc.rearrange("p h t -> p (h t)"), ALU.max, [Bdn, Bei], [Bdn])
            act(dflat, dflat, AF.Ln, [Bdn], [Bdn])
            act(dflat, dflat, AF.Exp, [Bdn], [Bdn], scale=-1.0)
            tt(hh, pd(bnum).rearrange("p (h d t) -> p h d t", h=4, d=2), dnm.unsqueeze(2).broadcast_to([128, 4, 2, 128]),
               ALU.mult, [PB[bnum], PB[bnum + 1], Bdn], [Bhh])
            PS.release(bnum, 2)
            PS.release(bden, 1)
            hflat = hh.rearrange("p h d t -> p (h d t)")
            act(sq.rearrange("p c t -> p (c t)"), hflat, AF.Square, [Bhh], [Bsq])
            bss = PS.get()
            for h in range(4):
                for dvc in range(2):
                    mm(pb(bss)[:, h * 128:(h + 1) * 128], ones_bf[:], sq[:, h * 2 + dvc, :], dvc == 0, dvc == 1,
                       [Bsq, B_cst], [PB[bss]])
            rstd_from_ss(rs.rearrange("p h t -> p (h t)"), pb(bss), 256, [PB[bss]], [Brs])
            tt(hh, hh, rs.unsqueeze(2).broadcast_to([128, 4, 2, 128]), ALU.mult, [Bhh, Brs], [Bhh])
            gml = ptab[:, l, R_MLG:R_MLG + 2].unsqueeze(1).unsqueeze(3).broadcast_to([128, 4, 2, 128])
            tt(hh, hh, gml, ALU.mult, [Bhh, B_par], [Bhh])
            tt(ybr[:, :, c0:c0 + 128], hh.rearrange("p h d t -> p (h d) t"), so[:, :, c0:c0 + 128], ALU.mult, [Bhh, Bso], [By])
        assert cur == 0
        branch_project(l, 1, ybr, By)
        P.barrier()

    def phase_mb(l, j):
        A.reset()
        sz = A.alloc([128, 8, T], BF16)
        xbc = A.alloc([128, 16, T], BF16)
        ybr = A.alloc([128, 8, T], BF16)
        yfm = A.alloc([128, 8, T], BF16)
        Bsz, Bx, By, Byf = Buf(), Buf(), Buf(), Buf()
        if j == 0:
            memset(St_mb[:], 0.0, [B_St])
            memset(Stb_mb[:], 0.0, [B_Stb])
        fm_proj_act(l, O_MBZ, 8, AF.Silu, sz, Bsz)
        mark = A.off
        conv_fm(l, O_MBX, 16, 4, tail_mb, B_tmb, R_MBW, R_MBB, 16, lambda c: (xbc[:, c, :], Bx, None), j == 0)
        P.barrier()
        A.off = mark
        wdt, Bwdt = loadw(wsrc("w_in", l, O_MBDT, 16))
        dtp, dte, dt_, dtA, cum, dcd, dec, ecum = [A.alloc([64, 16]) for _ in range(8)]
        elast = A.alloc([128, 16])
        R = A.alloc([64, 16, 64])
        LT = A.alloc([64, 16, 64])
        CBm = A.alloc([64, 4, 64])
        WT = A.alloc([64, 16, 64], BF16)
        xdt = A.alloc([64, 16, 64], BF16)
        xdd = A.alloc([64, 16, 64], BF16)
        Btm = A.alloc([64, 512], BF16)
        tmpy = A.alloc([64, 16, 64])
        ytm = A.alloc([64, 1024], BF16)
        tmpS = A.alloc([128, 16, 64])
        Bd = [Buf() for _ in range(8)]
        Bel, BR, BLT, BCB, BWT, Bxdt, Bxdd, BBt, Bty, Bytm, BtS = [Buf() for _ in range(11)]
        dt_all = A.alloc([64, 8, 16])
        dtA_all = A.alloc([64, 8, 16])
        dtw = dt_all
        bdt = PS.get()
        for ch in range(8):
            for kc in range(8):
                mm(pb(bdt)[0:64, ch * 16:(ch + 1) * 16], hT[:, kc, ch * 64:ch * 64 + 64], wdt[:, kc, 0:16], kc == 0, kc == 7,
                   [Bwdt, B_hT], [PB[bdt]])
        tt(dtw, pb(bdt)[0:64, 0:128].rearrange("p (c h) -> p c h", c=8), bvt[0:64, l, 0:16].unsqueeze(1).broadcast_to([64, 8, 16]),
           ALU.add, [PB[bdt], B_par], [Bd[0]])
        act(dtw, dtw, AF.Exp, [Bd[0]], [Bd[0]])
        ts(dtw, dtw, 1.0, None, ALU.add, None, [Bd[0]], [Bd[0]])
        act(dt_all, dtw, AF.Ln, [Bd[0]], [Bd[0], Bd[2]])
        tt(dtA_all, dt_all, aneg[0:64, l, :].unsqueeze(1).broadcast_to([64, 8, 16]), ALU.mult, [Bd[2], B_par], [Bd[3]])
        for ch in range(8):
            cc = ch * 64
            dt_ = dt_all[:, ch, :]
            dtA = dtA_all[:, ch, :]
            tt(R, dtA.unsqueeze(2).broadcast_to([64, 16, 64]), maskc.unsqueeze(1).broadcast_to([64, 16, 64]), ALU.mult,
               [Bd[3], B_cst], [BR])
            Rf = R.rearrange("p h t -> p (h t)")
            bD = PS.get(2)
            for hf in range(2):
                mm(pd(bD)[0:64, hf * 512:(hf + 1) * 512], SLm, Rf[:, hf * 512:(hf + 1) * 512], True, True,
                   [BR, B_cst], [PB[bD], PB[bD + 1]])
            act(LT.rearrange("p h t -> p (h t)"), pd(bD)[0:64, :], AF.Exp, [PB[bD], PB[bD + 1]], [BLT])
            bc_ = PS.get()
            mm(pb(bc_)[0:64, 0:16], maskc, dtA, True, True, [Bd[3], B_cst], [PB[bc_]])
            mm(pb(bc_)[:, 16:32], ones_f[0:64, :], dtA, True, True, [Bd[3], B_cst], [PB[bc_]])
            acopy(cum, pb(bc_)[0:64, 0:16], [PB[bc_]], [Bd[4]])
            act(ecum, pb(bc_)[0:64, 0:16], AF.Exp, [PB[bc_]], [Bd[7]])
            act(elast, pb(bc_)[:, 16:32], AF.Exp, [PB[bc_]], [Bel])
            tt(dcd, pb(bc_)[0:64, 16:32], cum, ALU.subtract, [PB[bc_], Bd[4]], [Bd[5]])
            act(dec, dcd, AF.Exp, [Bd[5]], [Bd[6]])
            bcb = PS.get()
            for g in range(4):
                mm(pb(bcb)[0:64, g * 64:(g + 1) * 64], xbc[:, 8 + g, cc:cc + 64], xbc[:, 12 + g, cc:cc + 64], True, True,
                   [Bx], [PB[bcb]])
            tt(CBm, pb(bcb)[0:64, 0:256].rearrange("p (g t) -> p g t", g=4), maskc.unsqueeze(1).broadcast_to([64, 4, 64]),
               ALU.mult, [PB[bcb], B_cst], [BCB])
            tt(WT.rearrange("p (g k) t -> p g k t", g=4), LT.rearrange("p (g k) t -> p g k t", g=4),
               CBm.unsqueeze(2).broadcast_to([64, 4, 4, 64]), ALU.mult, [BLT, BCB], [BWT])
            bx = PS.get()
            for hc in range(8):
                tr(pbb(bx)[0:64, hc * 128:(hc + 1) * 128], xbc[:, hc, cc:cc + 64], identb[:], [Bx, B_cst], [PB[bx]])
            bB = PS.get()
            for g in range(4):
                tr(pbb(bB)[0:64, g * 128:(g + 1) * 128], xbc[:, 8 + g, cc:cc + 64], identb[:], [Bx, B_cst], [PB[bB]])
            tt(xdt, pbb(bx)[0:64, 0:1024].rearrange("p (h q) -> p h q", h=16), dt_.unsqueeze(2).broadcast_to([64, 16, 64]),
               ALU.mult, [PB[bx], Bd[2]], [Bxdt])
            tt(xdd, xdt, dec.unsqueeze(2).broadcast_to([64, 16, 64]), ALU.mult, [Bxdt, Bd[6]], [Bxdd], eng=CE)
            acopy(Btm, pbb(bB)[0:64, 0:512], [PB[bB]], [BBt])
            by_ = PS.get(2)
            for h in range(16):
                mm(pd(by_)[0:64, h * 64:(h + 1) * 64], WT[:, h, :], xdt[:, h, :], True, True, [BWT, Bxdt], [PB[by_], PB[by_ + 1]])
            byi = PS.get(2)
            Stbf = Stb_mb[:].rearrange("p h q -> p (h q)")
            for g in range(4):
                mm(pd(byi)[0:64, g * 256:(g + 1) * 256], xbc[:, 12 + g, cc:cc + 64], Stbf[:, g * 256:(g + 1) * 256], True, True,
                   [Bx, B_Stb], [PB[byi], PB[byi + 1]])
            tt(tmpy, pd(byi)[0:64, :].rearrange("p (h q) -> p h q", h=16), ecum.unsqueeze(2).broadcast_to([64, 16, 64]),
               ALU.mult, [PB[byi], PB[byi + 1], Bd[7]], [Bty])
            tt(ytm, pd(by_)[0:64, :], tmpy.rearrange("p h q -> p (h q)"), ALU.add, [PB[by_], PB[by_ + 1], Bty], [Bytm])
            byt = PS.get()
            for hc in range(8):
                tr(pbb(byt)[:, hc * 64:(hc + 1) * 64], ytm[0:64, hc * 128:(hc + 1) * 128], identb[0:64, 0:64],
                   [Bytm, B_cst], [PB[byt]])
            acopy(yfm[:, :, cc:cc + 64], pbb(byt)[:, 0:512].rearrange("p (c t) -> p c t", c=8), [PB[byt]], [Byf])
            bs = PS.get(2)
            xddf = xdd.rearrange("p h q -> p (h q)")
            for g in range(4):
                mm(pd(bs)[:, g * 256:(g + 1) * 256], Btm[0:64, g * 128:(g + 1) * 128], xddf[:, g * 256:(g + 1) * 256], True, True,
                   [BBt, Bxdd], [PB[bs], PB[bs + 1]])
            tt(tmpS, St_mb[:], elast.unsqueeze(2).broadcast_to([128, 16, 64]), ALU.mult, [B_St, Bel], [BtS], eng=CE)
            tt(St_mb[:], tmpS, pd(bs).rearrange("p (h q) -> p h q", h=16), ALU.add, [BtS, PB[bs], PB[bs + 1]], [B_St])
            acopy(Stb_mb[:], St_mb[:], [B_St], [B_Stb])
        P.barrier()
        A.off = mark
        t1 = A.alloc([128, 2, T])
        sq = A.alloc([128, 2, T], BF16)
        rs = A.alloc([128, T])
        Bt1, Bsq, Brs = Buf(), Buf(), Buf()
        for g in range(4):
            dv = ptab[:, l, R_MBD + 2 * g:R_MBD + 2 * g + 2].unsqueeze(2).broadcast_to([128, 2, T])
            tt(t1, xbc[:, 2 * g:2 * g + 2, :], dv, ALU.mult, [Bx, B_par], [Bt1])
            tt(t1, t1, yfm[:, 2 * g:2 * g + 2, :], ALU.add, [Bt1, Byf], [Bt1])
            tt(t1, t1, sz[:, 2 * g:2 * g + 2, :], ALU.mult, [Bt1, Bsz], [Bt1])
            act(sq, t1, AF.Square, [Bt1], [Bsq])
            b = PS.get()
            for k in range(2):
                mm(pb(b), ones_bf[:], sq[:, k, :], k == 0, k == 1, [Bsq, B_cst], [PB[b]])
            rstd_from_ss(rs, pb(b), 256, [PB[b]], [Brs])
            tt(t1, t1, rs.unsqueeze(1).broadcast_to([128, 2, T]), ALU.mult, [Bt1, Brs], [Bt1])
            gm = ptab[:, l, R_MBG + 2 * g:R_MBG + 2 * g + 2].unsqueeze(2).broadcast_to([128, 2, T])
            tt(ybr[:, 2 * g:2 * g + 2, :], t1, gm, ALU.mult, [Bt1, B_par], [By])
        branch_project(l, 2, ybr, By)
        P.barrier()

    def phase_out(l):
        A.reset()
        mixb = A.alloc([128, 8, T], BF16)
        Bm = Buf()
        vcopy(mixb, mixed[:], [B_mixed], [Bm])
        for half in range(2):
            wo, Bw = loadw(wsrc("w_out", l, half * 512, 512))
            for dd in range(4):
                dc = half * 4 + dd
                b = PS.get()
                for kc in range(8):
                    mm(pb(b), wo[:, kc, dd * 128:(dd + 1) * 128], mixb[:, kc, :], kc == 0, kc == 7, [Bw, Bm], [PB[b]])
                tt(xf[:, dc, :], xf[:, dc, :], pb(b), ALU.add, [PB[b], B_xf], [B_xf])
        P.barrier()

    def phase_ffn(l, j):
        A.reset()
        prod = A.alloc([128, 22, T], BF16)
        Bp = Buf()
        ub = [A.alloc([128, T + 2]) for _ in range(8)]
        Bu = [Buf() for _ in range(8)]
        But = [Buf() for _ in range(8)]
        cg = [A.alloc([128, T]) for _ in range(4)]
        cv = [A.alloc([128, T]) for _ in range(4)]
        Bcg, Bcv = [Buf() for _ in range(4)], [Buf() for _ in range(4)]
        wcur = [None, None]

        def front(jc):
            if jc % 4 == 0:
                n = min(4, 22 - jc) * 128
                wcur[0] = loadw(wsrc("w_up", l, jc * 128, n))
                wcur[1] = loadw(wsrc("w_up", l, DFF + jc * 128, n))
            for which in range(2):
                wq, Bw = wcur[which]
                c = jc + 22 * which
                b = PS.get()
                for kc in range(8):
                    mm(pb(b), wq[:, kc, (jc % 4) * 128:(jc % 4 + 1) * 128], hT[:, kc, :], kc == 0, kc == 7, [Bw, B_hT], [PB[b]])
                u, bu, but = ub[(jc % 4) * 2 + which], Bu[(jc % 4) * 2 + which], But[(jc % 4) * 2 + which]
                if j == 0:
                    memset(u[:, 0:2], 0.0, [but])
                else:
                    acopy(u[:, 0:2], tail_ff[:, c, :], [B_tff[which]], [but])
                acopy(u[:, 2:2 + T], pb(b), [PB[b]], [bu])
                cc, bc = (cg[jc % 4], Bcg[jc % 4]) if which == 0 else (cv[jc % 4], Bcv[jc % 4])
                act(cc, pb(b), AF.Identity, [PB[b], B_par], [bc], scale=prow(l, R_FFW + 2 * 44 + c), bias=prow(l, R_FFB + c))
                acopy(tail_ff[:, c, :], u[:, T:T + 2], [bu], [B_tff[which]])
                for k in range(0, 2):
                    stt(cc, u[:, k:k + T], prow(l, R_FFW + k * 44 + c), cc, ALU.mult, ALU.add, [bu, but, bc, B_par], [bc])

        def back(jc):
            act(cg[jc % 4], cg[jc % 4], AF.Silu, [Bcg[jc % 4]], [Bcg[jc % 4]])
            tt(prod[:, jc, :], cg[jc % 4], cv[jc % 4], ALU.mult, [Bcg[jc % 4], Bcv[jc % 4]], [Bp])
        for jc in range(22):
            front(jc)
            if jc >= 2:
                back(jc - 2)
        back(20)
        back(21)
        for half in range(2):
            blks = []
            for k0, nk in ((0, 8), (8, 8), (16, 6)):
                blks.append(loadw(wsrc("w_down", l, half * 512, 512, k0, nk)))
            for dd in range(4):
                dc = half * 4 + dd
                b = PS.get()
                for jc in range(22):
                    wq, Bw = blks[jc // 8]
                    mm(pb(b), wq[:, jc % 8, dd * 128:(dd + 1) * 128], prod[:, jc, :], jc == 0, jc == 21, [Bw, Bp], [PB[b]])
                tt(xf[:, dc, :], xf[:, dc, :], pb(b), ALU.add, [PB[b], B_xf], [B_xf])
        P.barrier()

    def load_x(l, j):
        t0 = j * T
        if l == 0:
            A.reset()
            xtm = A.alloc([128, 4, D])
            Bxt = Buf()
            P.wait_all(XQ)
            P.dma(XQ, xtm, x_in[t0:t0 + T, :].rearrange("(g p) d -> p g d", p=128), writes=[Bxt])
            for g in range(4):
                for kc in range(8):
                    b = PS.get()
                    tr(pb(b)[:, 0:128], xtm[:, g, kc * 128:(kc + 1) * 128], ident, [Bxt, B_cst], [PB[b]])
                    acopy(xf[:, kc, g * 128:(g + 1) * 128], pb(b)[:, 0:128], [PB[b]], [B_xf])
            P.barrier()
        else:
            P.dma(XQ, xf[:], xscr[:, :, t0:t0 + T].rearrange("c p t -> p c t"), reads=[B_xs[j]], writes=[B_xf])

    def store_x(l, j):
        t0 = j * T
        if l < NL - 1:
            P.dma(XQ, xscr[:, :, t0:t0 + T].rearrange("c p t -> p c t"), xf[:], reads=[B_xf], writes=[B_xs[j]])
        else:
            A.reset()
            sq = A.alloc([128, 8, T], BF16)
            rs = A.alloc([128, T])
            yo = A.alloc([128, 8, T])
            otm = A.alloc([128, 4, D])
            Bq, Br, Byo, Bot = Buf(), Buf(), Buf(), Buf()
            act(sq, xf[:], AF.Square, [B_xf], [Bq])
            b = PS.get()
            for kc in range(8):
                mm(pb(b), ones_bf[:], sq[:, kc, :], kc == 0, kc == 7, [Bq, B_cst], [PB[b]])
            rstd_from_ss(rs, pb(b), D, [PB[b]], [Br])
            for kc in range(8):
                stt(yo[:, kc, :], xf[:, kc, :], prow(0, R_FIN + kc), rs, ALU.mult, ALU.mult, [B_xf, Br, B_par], [Byo])
            for g in range(4):
                for kc in range(8):
                    b = PS.get()
                    tr(pb(b)[:, 0:128], yo[:, kc, g * 128:(g + 1) * 128], ident, [Byo, B_cst], [PB[b]])
                    acopy(otm[:, g, kc * 128:(kc + 1) * 128], pb(b)[:, 0:128], [PB[b]], [Bot])
            P.dma(XQ, out_d[t0:t0 + T, :].rearrange("(g p) d -> p g d", p=128), otm, reads=[Bot], writes=[B_out[j]])
            P.barrier()
            P.finish("act", [B_out[j]])
            P.finish("dve", [B_out[j]])
            P.finish("pe", [B_out[j]])

    for l in range(NL):
        for j in range(NTILE):
            P.mark("load")
            load_x(l, j)
            P.mark("norm1")
            norm_to_hT(l, R_N1)
            P.mark("hg")
            phase_hg(l, j)
            P.mark("ml")
            phase_ml(l, j)
            P.mark("mb")
            phase_mb(l, j)
            P.mark("out")
            phase_out(l)
            P.mark("norm2")
            norm_to_hT(l, R_N2)
            P.mark("ffn")
            phase_ffn(l, j)
            P.mark("store")
            store_x(l, j)
    P.mark("end")
    P.finish("sp", B_out)
    P.build(st)
    st.close()
    P.rec_keys = [(k[0], 0) + tuple(k[2:]) for k in rec_keys]
    return nc, P


def make_consts():
    c = np.zeros((128, NCONST), np.float32)
    c[:, C_ID:C_ID + 128] = np.eye(128, dtype=np.float32)
    s = np.arange(128)[:, None]
    t = np.arange(128)[None, :]
    c[:, C_M2:C_M2 + 128] = ((s // 64 == t // 64) & (s <= t)).astype(np.float32)
    c[0:64, C_SL:C_SL + 64] = (np.arange(64)[:, None] > np.arange(64)[None, :]).astype(np.float32)
    gs = np.ones(512, np.float32)
    gs[::64] = 0.0
    c[:, C_GS:C_GS + 512] = gs[None, :]
    for h in range(4):
        c[h, C_SEL + h * 128:C_SEL + (h + 1) * 128] = 1.0
    c[:, C_ONE:C_ONE + 128] = 1.0
    c[:, C_EPS] = EPS
    return c


def pack_params(inp):
    pv = np.zeros((L_ALL, NROW, 128), np.float32)
    for l in range(L_ALL):
        pv[l, R_N1:R_N1 + 8] = inp["norm1_g"][l].reshape(8, 128)
        pv[l, R_N2:R_N2 + 8] = inp["norm2_g"][l].reshape(8, 128)
        pv[l, R_LB:R_LB + 8] = inp["hg_lb_logits"][l].reshape(8, 128)
        pv[l, R_MLW:R_MLW + 32] = inp["ml_conv_w"][l].reshape(4 * 8, 128)
        pv[l, R_MLB:R_MLB + 8] = inp["ml_conv_b"][l].reshape(8, 128)
        pv[l, R_MBW:R_MBW + 64] = inp["mb_conv_w"][l].reshape(4 * 16, 128)
        pv[l, R_MBB:R_MBB + 16] = inp["mb_conv_b"][l].reshape(16, 128)
        pv[l, R_FFW:R_FFW + 132] = inp["ffn_conv_w"][l].reshape(3 * 44, 128)
        pv[l, R_FFB:R_FFB + 44] = inp["ffn_conv_b"][l].reshape(44, 128)
        pv[l, R_MBG:R_MBG + 8] = inp["mb_norm_g"][l].reshape(8, 128)
        pv[l, R_HGG] = inp["hg_norm_g"][l]
        pv[l, R_MLG:R_MLG + 2] = inp["ml_norm_g"][l].reshape(2, 128)
        pv[l, R_MBD:R_MBD + 8] = np.repeat(inp["mb_d"][l], 64).reshape(8, 128)
        pv[l, R_FIN:R_FIN + 8] = inp["final_g"].reshape(8, 128)
    bv = np.concatenate([inp["mb_dt_bias"], inp["mb_a_log"]], axis=1).reshape(1, L_ALL * 32)
    gb = inp["ml_gate_b"].reshape(L_ALL, 2, 4).transpose(2, 0, 1).reshape(4, L_ALL * 2)
    return (np.ascontiguousarray(pv.reshape(L_ALL * 3 * 128, 128)), np.ascontiguousarray(bv.astype(np.float32)),
            np.ascontiguousarray(gb.astype(np.float32)))


_CACHE = {}


def run_cores(inp, xs, NL):
    NTOK = xs[0].shape[0]
    key = (NTOK, NL)
    if key not in _CACHE:
        if "keys0" not in _CACHE:
            _, P0 = build_program(T, 1)
            seen, ks = set(), []
            for k in P0.rec_keys:
                if k not in seen:
                    seen.add(k)
                    ks.append(k)
            _CACHE["keys0"] = ks
        _CACHE[key] = build_program(NTOK, NL, _CACHE["keys0"])[0]
    nc = _CACHE[key]
    pv, bv, gb = pack_params(inp)
    cst = make_consts()
    shared = {"w_in": inp["w_in"], "w_br_hg": inp["w_br_hg"], "w_br_ml": inp["w_br_ml"], "w_br_mb": inp["w_br_mb"],
              "w_out": inp["w_out"], "w_up": inp["w_up"], "w_down": inp["w_down"], "pvec": pv, "bvec": bv, "gbv": gb,
              "consts": cst}
    shared = {k: np.ascontiguousarray(np.asarray(v, dtype=np.float32)) for k, v in shared.items()}
    in_maps = [dict(shared, x=np.ascontiguousarray(x)) for x in xs]
    res = run_bass_kernel_spmd(nc, in_maps, core_ids=list(range(len(xs))))
    return [r["out"] for r in res.results]


def kernel(**inputs):
    inp = {k: np.asarray(v) for k, v in inputs.items()}
    x = inp["x"].astype(np.float32)
    Bn, S, _ = x.shape
    xs = [x[b] for b in range(Bn)]
    outs = run_cores(inp, xs, L_ALL)
    return np.stack(outs[:Bn], axis=0).astype(np.float32)
```
